# Optimizing a Trainium2 kernel written in Bass

```python
import math
import jax, jax.numpy as jnp
from jax import lax
import numpy as np

D_MODEL = 1024
BATCH = 2
SEQ = 8192
DEPTH = 4
DEC_BATCH = 128
DEC_SEQ = 4
PAST_LEN = 8192
PAGE_SIZE = 128

H_A = 4
DK_A = 32
DV_A = 64
RET_CHUNK = 128
RET_THETA = 10000.0
H_B = 8
KV_B = 2
G_B = H_B // KV_B
HD_B = 64
WINDOW = 128
ROPE_THETA = 500000.0
ROT_B = HD_B // 4
W_C = 256
POOL_WINDOWS = (2, 4, 8, 16)
N_POOL = 4
GC = W_C // N_POOL
POOL_HIST = 15
H_D = 4
DK_D = 64
DV_D = 64
HGRN_CHUNK = 16

W_A = H_A * DV_A
W_B = H_B * HD_B
W_D = H_D * DV_D
N_BRANCH = 4
BRANCH_WIDTHS = (W_A, W_B, W_C, W_D)
W_MIX = W_A + W_B + W_C + W_D
IN_WIDTHS = (
    H_A * DK_A, H_A * DK_A, W_A, W_A,
    H_B * HD_B, KV_B * HD_B, KV_B * HD_B, W_B,
    W_C, W_C,
    H_D * DK_D, H_D * DK_D, W_D, W_D,
    N_BRANCH * D_MODEL,
)
D_IN = sum(IN_WIDTHS)
EPS = 1e-6

kernel_name = "hybrid_gated_parallel_decoder_step"


def rms_norm(x, g=None):
    xf = x.astype(jnp.float32)
    y = xf * lax.rsqrt(jnp.mean(xf * xf, axis=-1, keepdims=True) + EPS)
    if g is not None:
        y = y * g.astype(jnp.float32)
    return y.astype(x.dtype)


def rope(x, pos, rot_dim, theta):
    half = rot_dim // 2
    inv = jnp.power(theta, -jnp.arange(half, dtype=jnp.float32) / half)
    ang = pos.astype(jnp.float32)[:, None] * inv[None, :]
    cos = jnp.cos(ang)[None, :, None, :]
    sin = jnp.sin(ang)[None, :, None, :]
    xf = x.astype(jnp.float32)
    x1 = xf[..., :half]
    x2 = xf[..., half:rot_dim]
    out = jnp.concatenate([x1 * cos - x2 * sin, x2 * cos + x1 * sin, xf[..., rot_dim:]], axis=-1)
    return out.astype(x.dtype)


def retention(q, k, v, s0):
    B, T, H, DK = q.shape
    DV = v.shape[-1]
    C = math.gcd(T, RET_CHUNK)
    n = T // C
    f32 = jnp.float32
    q = q.astype(f32).reshape(B, n, C, H, DK)
    k = k.astype(f32).reshape(B, n, C, H, DK)
    v = v.astype(f32).reshape(B, n, C, H, DV)
    lg = jnp.log1p(-jnp.power(2.0, -5.0 - jnp.arange(H, dtype=f32)))
    idx = jnp.arange(C, dtype=f32)
    rel = idx[:, None] - idx[None, :]
    dmask = jnp.where(rel[None] >= 0, jnp.exp(jnp.maximum(rel, 0.0)[None] * lg[:, None, None]), 0.0)
    scores = jnp.einsum('bnthd,bnshd->bnhts', q, k) * dmask[None, None]
    o = jnp.einsum('bnhts,bnshe->bnthe', scores, v)
    q_dec = q * jnp.exp((idx[:, None] + 1.0) * lg[None, :])[:, :, None]
    k_dec = k * jnp.exp((C - 1.0 - idx)[:, None] * lg[None, :])[:, :, None]
    kv = jnp.einsum('bnshd,bnshe->nbhde', k_dec, v)
    cdec = jnp.exp(C * lg)[None, :, None, None]

    def step(s, kv_n):
        return cdec * s + kv_n, s

    s_fin, s_starts = lax.scan(step, s0.astype(f32), kv)
    o = o + jnp.einsum('bnthd,nbhde->bnthe', q_dec, s_starts)
    return o.reshape(B, T, H, DV), s_fin


def hgrn2_scan(q, k, v, log_f, s0):
    B, T, H, DK = q.shape
    DV = v.shape[-1]
    C = math.gcd(T, HGRN_CHUNK)
    n = T // C
    f32 = jnp.float32
    q = q.astype(f32).reshape(B, n, C, H, DK)
    k = k.astype(f32).reshape(B, n, C, H, DK)
    v = v.astype(f32).reshape(B, n, C, H, DV)
    b = jnp.cumsum(log_f.astype(f32).reshape(B, n, C, H, DK), axis=2)
    causal = jnp.tril(jnp.ones((C, C), dtype=bool))
    diff = b[:, :, :, None] - b[:, :, None, :]
    w = jnp.exp(jnp.where(causal[None, None, :, :, None, None], diff, -jnp.inf))
    scores = jnp.einsum('bnthd,bntshd,bnshd->bnhts', q, w, k)
    o = jnp.einsum('bnhts,bnshe->bnthe', scores, v)
    b_last = b[:, :, -1]
    q_dec = q * jnp.exp(b)
    k_dec = k * jnp.exp(b_last[:, :, None] - b)
    kv = jnp.einsum('bnshd,bnshe->nbhde', k_dec, v)
    cdec = jnp.moveaxis(jnp.exp(b_last), 1, 0)[..., None]

    def step(s, xs):
        dec, kv_n = xs
        return dec * s + kv_n, s

    s_fin, s_starts = lax.scan(step, s0.astype(f32), (cdec, kv))
    o = o + jnp.einsum('bnthd,nbhde->bnthe', q_dec, s_starts)
    return o.reshape(B, T, H, DV), s_fin


def swa_sink_attention(q, k, v, k_buf, v_buf, sink, start):
    B, T = q.shape[:2]
    Bq = WINDOW if T % WINDOW == 0 else T
    nb = T // Bq
    L = WINDOW + Bq
    k_ext = jnp.concatenate([k_buf.astype(k.dtype), k], axis=1)
    v_ext = jnp.concatenate([v_buf.astype(v.dtype), v], axis=1)
    ctx = jnp.arange(nb)[:, None] * Bq + jnp.arange(L)[None, :]
    kb = k_ext[:, ctx].astype(jnp.float32)
    vb = v_ext[:, ctx].astype(jnp.float32)
    qb = q.astype(jnp.float32).reshape(B, nb, Bq, KV_B, G_B, HD_B)
    q_pos = start + jnp.arange(T).reshape(nb, Bq)
    k_pos = start - WINDOW + ctx
    dist = q_pos[:, :, None] - k_pos[:, None, :]
    mask = (dist >= 0) & (dist <= WINDOW) & (k_pos[:, None, :] >= 0)
    s = jnp.einsum('bnqkgd,bnlkd->bnkgql', qb, kb) * (HD_B ** -0.5)
    s = jnp.where(mask[None, :, None, None], s, -jnp.inf)
    sink_col = jnp.broadcast_to(sink.astype(jnp.float32).reshape(KV_B, G_B)[None, None, :, :, None, None],
                                s.shape[:-1] + (1,))
    p = jax.nn.softmax(jnp.concatenate([s, sink_col], axis=-1), axis=-1)[..., :-1]
    o = jnp.einsum('bnkgql,bnlkd->bnqkgd', p, vb)
    return o.reshape(B, T, H_B * HD_B), k_ext[:, -WINDOW:], v_ext[:, -WINDOW:]


def pool_mix(u, hist, w_pool, pool_scale, start):
    B, T, _ = u.shape
    ext_raw = jnp.concatenate([hist.astype(u.dtype), u], axis=1)
    ext = ext_raw.astype(jnp.float32)
    cs = jnp.concatenate([jnp.zeros((B, 1, W_C), jnp.float32), jnp.cumsum(ext, axis=1)], axis=1)
    end = cs[:, POOL_HIST + 1:]
    pos = start + jnp.arange(T)
    outs = []
    for g, w in enumerate(POOL_WINDOWS):
        sl = slice(g * GC, (g + 1) * GC)
        win = end[..., sl] - cs[:, POOL_HIST + 1 - w: POOL_HIST + 1 - w + T, sl]
        cnt = jnp.minimum(pos + 1, w).astype(jnp.float32)[None, :, None]
        outs.append(win / cnt)
    pooled = jnp.concatenate(outs, axis=-1) - u.astype(jnp.float32)
    mixed = jnp.einsum('btgc,gcd->btgd', pooled.reshape(B, T, N_POOL, GC),
                       w_pool.astype(jnp.float32)).reshape(B, T, W_C)
    return mixed * pool_scale.astype(jnp.float32), ext_raw[:, -POOL_HIST:]


def mixer_layer(x, start, s_ret, k_buf, v_buf, p_hist, s_hgrn, lb,
                w_in, w_branch, w_out, g_pre, g_post, sink, w_pool, pool_scale, g_hgrn):
    B, T, _ = x.shape
    dt = x.dtype
    h = rms_norm(x, g_pre)
    proj = h @ w_in
    split_idx = [int(i) for i in np.cumsum(IN_WIDTHS)[:-1]]
    (q_a, k_a, v_a, z_a, q_b, k_b, v_b, z_b, u_c, z_c,
     q_d, f_d, i_d, z_d, gl) = jnp.split(proj, split_idx, axis=-1)
    pos = start + jnp.arange(T)

    qa = rope(q_a.reshape(B, T, H_A, DK_A), pos, DK_A, RET_THETA)
    ka = rope(k_a.reshape(B, T, H_A, DK_A), pos, DK_A, RET_THETA) * (DK_A ** -0.5)
    o_a, s_ret_new = retention(qa, ka, v_a.reshape(B, T, H_A, DV_A), s_ret)
    y_a = (rms_norm(o_a).reshape(B, T, W_A) * jax.nn.silu(z_a.astype(jnp.float32))).astype(dt)

    qb = rope(q_b.reshape(B, T, H_B, HD_B), pos, ROT_B, ROPE_THETA)
    kb = rope(k_b.reshape(B, T, KV_B, HD_B), pos, ROT_B, ROPE_THETA)
    o_b, k_new, v_new = swa_sink_attention(qb, kb, v_b.reshape(B, T, KV_B, HD_B), k_buf, v_buf, sink, start)
    y_b = (o_b * jax.nn.silu(z_b.astype(jnp.float32))).astype(dt)

    o_c, p_new = pool_mix(u_c, p_hist, w_pool, pool_scale, start)
    y_c = (o_c * jax.nn.silu(z_c.astype(jnp.float32))).astype(dt)

    f = lb + (1.0 - lb) * jax.nn.sigmoid(f_d.astype(jnp.float32))
    o_d, s_hgrn_new = hgrn2_scan(jax.nn.silu(q_d.astype(jnp.float32)).reshape(B, T, H_D, DK_D),
                                 (1.0 - f).reshape(B, T, H_D, DK_D),
                                 i_d.reshape(B, T, H_D, DV_D),
                                 jnp.log(f).reshape(B, T, H_D, DK_D), s_hgrn)
    y_d = (rms_norm(o_d, g_hgrn.reshape(H_D, DV_D)).reshape(B, T, W_D)
           * jax.nn.silu(z_d.astype(jnp.float32))).astype(dt)

    gates = jax.nn.sigmoid(gl.reshape(B, T, N_BRANCH, D_MODEL))
    ys = (y_a, y_b, y_c, y_d)
    off = 0
    merged = None
    for i in range(N_BRANCH):
        wd = BRANCH_WIDTHS[i]
        term = gates[:, :, i] * (ys[i] @ w_branch[off:off + wd])
        merged = term if merged is None else merged + term
        off += wd
    out = merged @ w_out
    x_new = x + rms_norm(out, g_post)
    return (x_new, s_ret_new.astype(dt), k_new.astype(dt), v_new.astype(dt),
            p_new.astype(dt), s_hgrn_new.astype(dt))


def trunk(x, start, s_ret, k_buf, v_buf, p_hist, s_hgrn, lb,
          w_in, w_branch, w_out, g_pre, g_post, attn_sink, w_pool, pool_scale, g_hgrn):
    acc = ([], [], [], [], [])
    for l in range(DEPTH):
        x, *st = mixer_layer(x, start, s_ret[l], k_buf[l], v_buf[l], p_hist[l], s_hgrn[l], lb[l],
                             w_in[l], w_branch[l], w_out[l], g_pre[l], g_post[l], attn_sink[l],
                             w_pool[l], pool_scale[l], g_hgrn[l])
        for a, s in zip(acc, st):
            a.append(s)
    return (x, jnp.stack(acc[0]), jnp.stack(acc[1]), jnp.stack(acc[2]),
            jnp.stack(acc[3]), jnp.stack(acc[4]))


def setup_inputs(seed: int = 0) -> dict:
    key = jax.random.key(seed)
    ks = jax.random.split(key, 17)

    def nrm(k, shape, s):
        return jax.random.normal(k, shape, jnp.float32) * s

    return {
        "x_prompt": nrm(ks[0], (BATCH, SEQ, D_MODEL), 1.0),
        "x_sample": nrm(ks[1], (DEC_BATCH, DEC_SEQ, D_MODEL), 1.0),
        "state_ret": nrm(ks[2], (DEPTH, DEC_BATCH, H_A, DK_A, DV_A), 0.5),
        "cache_swa_k": nrm(ks[3], (DEPTH, DEC_BATCH, WINDOW, KV_B, HD_B), 1.0),
        "cache_swa_v": nrm(ks[4], (DEPTH, DEC_BATCH, WINDOW, KV_B, HD_B), 1.0),
        "state_pool": nrm(ks[5], (DEPTH, DEC_BATCH, POOL_HIST, W_C), 1.0),
        "state_hgrn": nrm(ks[6], (DEPTH, DEC_BATCH, H_D, DK_D, DV_D), 0.5),
        "w_in": nrm(ks[7], (DEPTH, D_MODEL, D_IN), D_MODEL ** -0.5),
        "w_branch": nrm(ks[8], (DEPTH, W_MIX, D_MODEL), (W_MIX // N_BRANCH) ** -0.5),
        "w_out": nrm(ks[9], (DEPTH, D_MODEL, D_MODEL), D_MODEL ** -0.5),
        "g_pre": 1.0 + nrm(ks[10], (DEPTH, D_MODEL), 0.1),
        "g_post": 1.0 + nrm(ks[11], (DEPTH, D_MODEL), 0.1),
        "attn_sink": nrm(ks[12], (DEPTH, H_B), 0.5),
        "w_pool": nrm(ks[13], (DEPTH, N_POOL, GC, GC), GC ** -0.5),
        "pool_scale": 1.0 + nrm(ks[14], (DEPTH, W_C), 0.1),
        "g_hgrn": 1.0 + nrm(ks[15], (DEPTH, W_D), 0.1),
        "lower_bounds": nrm(ks[16], (DEPTH, H_D * DK_D), 0.1),
    }


def reference(x_prompt, x_sample, state_ret, cache_swa_k, cache_swa_v, state_pool, state_hgrn,
              w_in, w_branch, w_out, g_pre, g_post, attn_sink, w_pool, pool_scale, g_hgrn,
              lower_bounds):
    lb = jnp.cumsum(jax.nn.softmax(lower_bounds.astype(jnp.float32), axis=0), axis=0)
    lb = lb - lb[0]
    dt = x_prompt.dtype
    y_p, r_p, k_p, v_p, pool_p, h_p = trunk(
        x_prompt, 0,
        jnp.zeros((DEPTH, BATCH, H_A, DK_A, DV_A), dt),
        jnp.zeros((DEPTH, BATCH, WINDOW, KV_B, HD_B), dt),
        jnp.zeros((DEPTH, BATCH, WINDOW, KV_B, HD_B), dt),
        jnp.zeros((DEPTH, BATCH, POOL_HIST, W_C), dt),
        jnp.zeros((DEPTH, BATCH, H_D, DK_D, DV_D), dt),
        lb, w_in, w_branch, w_out, g_pre, g_post, attn_sink, w_pool, pool_scale, g_hgrn)
    y_s, r_s, k_s, v_s, pool_s, h_s = trunk(
        x_sample, PAST_LEN, state_ret, cache_swa_k, cache_swa_v, state_pool, state_hgrn,
        lb, w_in, w_branch, w_out, g_pre, g_post, attn_sink, w_pool, pool_scale, g_hgrn)
    return (y_p, y_s, r_p, k_p, v_p, pool_p, h_p, r_s, k_s, v_s, pool_s, h_s)
```

```python
import numpy as np
from contextlib import ExitStack
import ml_dtypes
import concourse.bass as bass
import concourse.mybir as mybir
from concourse.bass_utils import run_bass_kernel_spmd

F32 = mybir.dt.float32
BF16 = mybir.dt.bfloat16
AF = mybir.ActivationFunctionType
ALU = mybir.AluOpType


def bc(ap, shape):
    lst = [list(x) for x in ap.ap]
    for n in shape[len(lst):]:
        lst.append([0, n])
    return bass.AP(ap.tensor, ap.offset, lst)


def bcast(ap, axis, n):
    lst = [list(x) for x in ap.ap]
    lst.insert(axis, [0, n])
    return bass.AP(ap.tensor, ap.offset, lst)


class T:
    __slots__ = ("t", "name", "w", "rd", "psum")

    def __init__(self, t, name, psum=False):
        self.t = t
        self.name = name
        self.w = None
        self.rd = {}
        self.psum = psum


class Prog:
    ENG = ("pe", "act", "dve", "pool", "sp")

    def __init__(self, nc, es):
        self.nc = nc
        self.es = es
        self.sem = {k: es.enter_context(nc.semaphore("s_" + k)) for k in self.ENG}
        self.cnt = {k: 0 for k in self.ENG}
        self.seen = {k: {} for k in self.ENG}
        self.items = {k: [] for k in self.ENG}
        self.chan = {}
        self.finals = {}
        self.uid = 0

    def sb(self, name, shape, dt):
        return T(self.es.enter_context(self.nc.sbuf_tensor("sb_" + name, shape, dt)), name)

    def ps(self, name, shape, dt):
        return T(self.es.enter_context(self.nc.psum_tensor("ps_" + name, shape, dt)), name, psum=True)

    def res(self, name):
        return T(None, name)

    def _deps(self, e, rd, wr):
        deps = {}

        def add(k, c):
            if deps.get(k, 0) < c:
                deps[k] = c

        for r in rd:
            if r.w is not None:
                add(*r.w)
            if r.psum:
                for k, c in r.rd.items():
                    if k != e:
                        add(k, c)
        for r in wr:
            if r.w is not None:
                add(*r.w)
            for k, c in r.rd.items():
                add(k, c)
        waits = []
        for k, c in deps.items():
            if k == e and e == "pe":
                continue
            if k in self.chan:
                c = self.chan[k][1]
            if self.seen[e].get(k, 0) < c:
                self.seen[e][k] = c
                waits.append((k, c))
        return waits

    def _semobj(self, k):
        return self.sem[k] if k in self.sem else self.chan[k][0]

    def op(self, e, fn, rd=(), wr=()):
        waits = self._deps(e, rd, wr)
        self.cnt[e] += 1
        c = self.cnt[e]
        self.items[e].append((waits, fn, self.sem[e], 1))
        for r in rd:
            r.rd[e] = c
        for r in wr:
            r.w = (e, c)
            r.rd = {}

    def dma(self, q, out, in_, rd=(), wr=(), ch=None, slow=False, out_final=False):
        if ch is None:
            ch = "d_" + (wr[0].name if len(wr) and wr[0].t is not None else rd[0].name)
        if ch not in self.chan:
            self.chan[ch] = [self.es.enter_context(self.nc.semaphore(ch)), 0]
        if not isinstance(out, bass.AP):
            out = out[tuple(slice(None) for _ in out.shape)]
        if not isinstance(in_, bass.AP):
            in_ = in_[tuple(slice(None) for _ in in_.shape)]
        waits = self._deps(q, rd, wr)
        self.chan[ch][1] += 16
        c = self.chan[ch][1]
        if slow:
            fn = lambda e: e.dma_start(out=out, in_=in_, allow_slow_non_contiguous=True)
        else:
            fn = lambda e: e.dma_start(out=out, in_=in_)
        self.items[q].append((waits, fn, self.chan[ch][0], 16))
        for r in rd:
            r.rd[ch] = c
        for r in wr:
            r.w = (ch, c)
            r.rd = {}
        if out_final:
            self.finals[ch] = c

    def coll(self, fn, rd=(), wr=(), ch="c_coll"):
        if ch not in self.chan:
            self.chan[ch] = [self.es.enter_context(self.nc.semaphore(ch)), 0]
        waits = self._deps("pool", rd, wr)
        self.chan[ch][1] += 1
        c = self.chan[ch][1]
        self.items["pool"].append((waits, fn, self.chan[ch][0], 1))
        for r in rd:
            r.rd[ch] = c
        for r in wr:
            r.w = (ch, c)
            r.rd = {}

    def make_ident(self, ident_bf, ident_f):
        self.op("pool", lambda e: e.memset(ident_f.t[:, :], 1.0), wr=[ident_f])
        self.op("pool", lambda e: e.affine_select(out=ident_f.t[:, :], in_=ident_f.t[:, :], pattern=[[-1, 128]],
                                                  compare_op=ALU.is_equal, fill=0.0, base=0, channel_multiplier=1), rd=[ident_f], wr=[ident_f])
        self.op("dve", lambda e: e.tensor_copy(out=ident_bf.t[:, :], in_=ident_f.t[:, :]), rd=[ident_f], wr=[ident_bf])

    def finish(self):
        fw = [(ch, c) for ch, c in self.finals.items()]
        engs = {"pe": "tensor", "act": "scalar", "dve": "vector", "pool": "gpsimd", "sp": "sync"}
        with self.nc.Block() as block:
            for k in self.ENG:
                items = self.items[k]
                extra = fw if k == "sp" else []

                def body(eng, items=items, extra=extra):
                    for waits, fn, sem, amt in items:
                        for (sk, sc) in waits:
                            eng.wait_ge(self._semobj(sk), sc)
                        ins = fn(eng)
                        ins.then_inc(sem, amt)
                    for (sk, sc) in extra:
                        eng.wait_ge(self._semobj(sk), sc)
                getattr(block, engs[k])(body)


D = 1024
DEPTH = 4
NCORE = 8
SEQ = 8192
PAST = 8192
NS = 16
TS = NS * 4
G = 2
NFM = 68
CPB = 2
NBLK = NFM // CPB
EPS = 1e-6
GAM = [1.0 - 2.0 ** (-5.0 - h) for h in range(4)]
DEBUG_STOP = 9
DEBUG_SUB = 9
DEBUG_DUMP = False

FM_NAMES = [
    ("ka", 0), ("ka", 1), ("kas", 0), ("kas", 1),
    ("fd", 0), ("fd", 1), ("uc", 0), ("uc", 1),
    ("kb", 0), ("kb", 1), ("kbs", 0), ("kbs", 1),
    ("qa", 0), ("qa", 1), ("qas", 0), ("qas", 1),
    ("qb", 0), ("qb", 1), ("qb", 2), ("qb", 3),
    ("qbs", 0), ("qbs", 1), ("qbs", 2), ("qbs", 3),
    ("za", 0), ("za", 1), ("zc", 0), ("zc", 1),
    ("zb", 0), ("zb", 1), ("zb", 2), ("zb", 3),
    ("qd", 0), ("qd", 1), ("zd", 0), ("zd", 1),
]
RES_SPEC = {"ka": (2, BF16), "kas": (2, BF16), "qa": (2, BF16), "qas": (2, BF16),
            "qb": (4, BF16), "qbs": (4, BF16), "kb": (2, BF16), "kbs": (2, BF16),
            "fd": (2, F32), "uc": (2, F32), "za": (2, BF16), "zb": (4, BF16),
            "zc": (2, BF16), "zd": (2, BF16), "qd": (2, BF16)}
SILU = {"za", "zb", "zc", "zd", "qd"}
ROPE_TAB = {"ka": 0, "qa": 0, "kas": 1, "qas": 1, "kb": 2, "qb": 2, "kbs": 3, "qbs": 3}


def _fm_cols():
    off = {}
    o = 0
    for nm, w in [("q_a", 128), ("k_a", 128), ("v_a", 256), ("z_a", 256), ("q_b", 512), ("k_b", 128),
                  ("v_b", 128), ("z_b", 512), ("u_c", 256), ("z_c", 256), ("q_d", 256), ("f_d", 256),
                  ("i_d", 256), ("z_d", 256), ("gl", 4096)]:
        off[nm] = o
        o += w
    cols = np.full(NFM * 128, -1, np.int64)

    def ret_chunk(base, j, swap):
        c = np.full(128, -1, np.int64)
        for hh in range(2):
            h = 2 * j + hh
            for d in range(32):
                sd = (d + 16) % 32 if swap else d
                c[hh * 64 + d] = base + h * 32 + sd
        return c

    def swa_q_chunk(base, j, swap):
        c = np.zeros(128, np.int64)
        for hh in range(2):
            h = 2 * j + hh
            for d in range(64):
                sd = d
                if swap and d < 16:
                    sd = d + 8 if d < 8 else d - 8
                c[hh * 64 + d] = base + h * 64 + sd
        return c

    def swa_k_chunk(base, kv, swap):
        c = np.zeros(128, np.int64)
        for hh in range(2):
            for d in range(64):
                sd = d
                if swap and d < 16:
                    sd = d + 8 if d < 8 else d - 8
                c[hh * 64 + d] = base + kv * 64 + sd
        return c

    for ci, (nm, j) in enumerate(FM_NAMES):
        if nm == "ka":
            c = ret_chunk(off["k_a"], j, False)
        elif nm == "kas":
            c = ret_chunk(off["k_a"], j, True)
        elif nm == "qa":
            c = ret_chunk(off["q_a"], j, False)
        elif nm == "qas":
            c = ret_chunk(off["q_a"], j, True)
        elif nm == "qb":
            c = swa_q_chunk(off["q_b"], j, False)
        elif nm == "qbs":
            c = swa_q_chunk(off["q_b"], j, True)
        elif nm == "kb":
            c = swa_k_chunk(off["k_b"], j, False)
        elif nm == "kbs":
            c = swa_k_chunk(off["k_b"], j, True)
        else:
            src = {"fd": "f_d", "uc": "u_c", "za": "z_a", "zb": "z_b", "zc": "z_c", "zd": "z_d", "qd": "q_d"}[nm]
            c = off[src] + j * 128 + np.arange(128)
        cols[ci * 128:(ci + 1) * 128] = c
    for m in range(8):
        for i in range(4):
            ci = 36 + m * 4 + i
            cols[ci * 128:(ci + 1) * 128] = off["gl"] + i * 1024 + m * 128 + np.arange(128)
    tm = np.concatenate([off["v_a"] + np.arange(256), off["i_d"] + np.arange(256), off["v_b"] + np.arange(128)])
    return cols, tm


def host_tables(NT, seg_start, first_seg, rank=0):
    tb = {}
    def rope_tabs(pos):
        T_ = len(pos)
        pos = pos.astype(np.float32)
        invA = np.power(np.float32(10000.0), -np.arange(16, dtype=np.float32) / np.float32(16))
        invB = np.power(np.float32(500000.0), -np.arange(8, dtype=np.float32) / np.float32(8))
        angA = (pos[None, :] * invA[:, None]).astype(np.float32)
        angB = (pos[None, :] * invB[:, None]).astype(np.float32)
        cA = np.zeros((128, T_), np.float32); sA = np.zeros((128, T_), np.float32)
        cB = np.ones((128, T_), np.float32); sB = np.zeros((128, T_), np.float32)
        for hh in range(2):
            for d in range(32):
                cA[hh * 64 + d] = np.cos(angA[d % 16])
                sA[hh * 64 + d] = (-1.0 if d < 16 else 1.0) * np.sin(angA[d % 16])
            for d in range(16):
                cB[hh * 64 + d] = np.cos(angB[d % 8])
                sB[hh * 64 + d] = (-1.0 if d < 8 else 1.0) * np.sin(angB[d % 8])
        return np.stack([cA, sA, cB, sB])
    tb["ropeP"] = np.stack([rope_tabs(seg_start + i * 128 + np.arange(128)) for i in range(NT)])
    tS = np.arange(TS) // NS
    tb["ropeS"] = rope_tabs(PAST + tS)
    scale = 32.0 ** -0.5
    def ret_tabs(tpos, cid, C):
        T_ = len(tpos)
        gq = np.zeros((128, 2, T_), np.float32); gk = np.zeros((128, 2, T_), np.float32)
        dm = np.zeros((T_, 4, T_), np.float32)
        for h in range(4):
            j, hh = h // 2, h % 2
            gq[hh * 64:(hh + 1) * 64, j, :] = GAM[h] ** (tpos + 1.0)
            gk[hh * 64:(hh + 1) * 64, j, :] = GAM[h] ** (C - 1.0 - tpos) * scale
            rel = tpos[None, :] - tpos[:, None]
            ok = (cid[None, :] == cid[:, None]) & (rel >= 0)
            dm[:, h, :] = np.where(ok, GAM[h] ** np.maximum(rel, 0) * scale, 0.0)
        return gq, gk, dm
    tP = np.arange(128).astype(np.float64)
    tb["gqP"], tb["gkP"], tb["dmP"] = ret_tabs(tP, np.zeros(128), 128)
    tb["gqS"], tb["gkS"], tb["dmS"] = ret_tabs(tS.astype(np.float64), np.arange(TS) % NS, 4)
    decA = np.zeros((128, 4), np.float32)
    for h in range(4):
        j, hh = h // 2, h % 2
        decA[hh * 64:(hh + 1) * 64, j] = GAM[h] ** 128
        decA[hh * 64:(hh + 1) * 64, 2 + j] = GAM[h] ** 4
    tb["decA"] = decA
    cP = np.arange(128) // 16
    tb["mdP"] = ((cP[:, None] == cP[None, :]) & (np.arange(128)[:, None] <= np.arange(128)[None, :])).astype(np.float32)
    sq = np.arange(TS) % NS
    tb["mdS"] = ((sq[:, None] == sq[None, :]) & (tS[:, None] <= tS[None, :])).astype(np.float32)
    tb["cmP"] = (cP[:, None] == np.arange(8)[None, :]).astype(np.float32)
    tb["cmS"] = (sq[:, None] == np.arange(NS)[None, :]).astype(np.float32)
    rs = np.ones((128, 128), np.float32); rs[:, ::16] = 0.0
    tb["resetP"] = rs
    a = np.arange(128)
    tb["mcurP"] = (a[:, None] <= a[None, :]).astype(np.float32)
    tb["mprevP"] = (a[:, None] >= a[None, :]).astype(np.float32)
    tb["mprev0"] = np.zeros((128, 128), np.float32) if first_seg else tb["mprevP"].copy()
    tb["mcurS"] = tb["mdS"].copy()
    tb["mcacheS"] = (a[:, None] >= tS[None, :]).astype(np.float32)
    wins = [2, 4, 8, 16]
    pr = np.zeros((128, 2, 128), np.float32); p0 = np.zeros((128, 2, 128), np.float32)
    for g_ in range(4):
        c, r0 = g_ // 2, (g_ % 2) * 64
        pr[r0:r0 + 64, c, :] = 1.0 / wins[g_]
        cnt = np.minimum(seg_start + np.arange(128) + 1, wins[g_]).astype(np.float32)
        p0[r0:r0 + 64, c, :] = 1.0 / cnt
    tb["pinvR"] = pr
    tb["pinv0"] = p0
    bo = np.zeros((128, 128), np.float32)
    bo[:64, :64] = 1.0 / 64; bo[64:, 64:] = 1.0 / 64
    tb["bones"] = bo
    op = np.zeros((128, 2, 128), np.float32)
    op[:, 0, :64] = 1.0; op[:, 1, 64:] = 1.0
    tb["onespad"] = op
    selp = np.zeros((128, 4), np.float32); sels = np.zeros((128, 4), np.float32)
    if rank > 0:
        selp[:, rank - 1] = 1.0
        sels[:, rank] = 1.0
    cret = np.zeros((128, 2, 4), np.float32)
    L = NT * 128
    for h in range(4):
        j, hh = h // 2, h % 2
        for q in range(rank):
            cret[hh * 64:(hh + 1) * 64, j, q] = GAM[h] ** (L * (rank - 1 - q))
    tb["selp"] = selp; tb["sels"] = sels; tb["cret"] = cret
    return {k: np.ascontiguousarray(v, dtype=np.float32) for k, v in tb.items()}


def build_program(NT, depth=DEPTH, with_sample=True, bmode=True):
    NG = NT // G
    nc = bass.Bass("TRN2", target_bir_lowering=False)

    def din(name, shape, dt=F32):
        return nc.dram_tensor(name, list(shape), dt, kind="ExternalInput").ap()

    def dout(name, shape):
        return nc.dram_tensor(name, list(shape), F32, kind="ExternalOutput").ap()

    xp_d = din("xp", [NT * 128, D]); xs_d = din("xs", [TS, D])
    wfm_d = din("wfm", [depth, D, NFM * 128]); wtm_d = din("wtm", [depth, D, 640])
    wbr_d = din("wbr", [depth, 1280, D]); wout_d = din("wout", [depth, D, D])
    gpre_d = din("gpre", [depth, D]); gpost_d = din("gpost", [depth, D])
    sink_d = din("sink", [depth, 8]); wpool_d = din("wpool", [depth, 4, 64, 64])
    pscale_d = din("pscale", [depth, 256]); ghg_d = din("ghg", [depth, 256]); lbnd_d = din("lbnd", [depth, 256])
    sret_d = din("s_ret", [depth, NS, 4, 32, 64]); sk_d = din("s_k", [depth, NS, 128, 128])
    sv_d = din("s_v", [depth, NS, 128, 128]); spool_d = din("s_pool", [depth, NS, 15, 256])
    shg_d = din("s_hg", [depth, NS, 4, 64, 64])
    tb_shapes = {k: v.shape for k, v in host_tables(NT, 0, True).items()}
    tb_d = {k: din("tb_" + k, list(s)) for k, s in tb_shapes.items()}
    yp_d = dout("yp", [NT * 128, D]); ys_d = dout("ys", [TS, D])
    if DEBUG_DUMP:
        dbg_y = nc.dram_tensor("dbg_y", [128, 10, G * 128], BF16, kind="ExternalOutput").ap()
        dbg_m = nc.dram_tensor("dbg_m", [128, 8, G * 128], BF16, kind="ExternalOutput").ap()
        dbg_h = nc.dram_tensor("dbg_h", [128, 8, G * 128], BF16, kind="ExternalOutput").ap()
        dbg_ys = nc.dram_tensor("dbg_ys", [128, 10, G * 128], BF16, kind="ExternalOutput").ap()
    oret_d = dout("o_ret", [depth, 4, 32, 64]); ok_d = dout("o_k", [depth, 128, 128]); ov_d = dout("o_v", [depth, 128, 128])
    opool_d = dout("o_pool", [depth, 15, 256]); ohg_d = dout("o_hg", [depth, 4, 64, 64])
    osret_d = dout("os_ret", [depth, NS, 4, 32, 64]); osk_d = dout("os_k", [depth, NS, 128, 128])
    osv_d = dout("os_v", [depth, NS, 128, 128]); ospool_d = dout("os_pool", [depth, NS, 15, 256])
    oshg_d = dout("os_hg", [depth, NS, 4, 64, 64])

    wfm_b = nc.dram_tensor("wfm_b", [depth, NBLK, 128, 8, CPB * 128], BF16, kind="Internal").ap()
    wbr_b = nc.dram_tensor("wbr_b", [depth, 8, 128, 10, 128], BF16, kind="Internal").ap()
    XR = 1664
    xin_d = nc.dram_tensor("xchg_in", [XR, 128], F32, kind="Internal").ap()
    xg_d = nc.dram_tensor("xchg_all", [4 * XR, 128], F32, kind="Internal").ap()
    es = ExitStack()
    with es:
        P = Prog(nc, es)
        NTOK = G * 128
        ident = P.sb("ident", [128, 128], BF16); identf = P.sb("identf", [128, 128], F32)
        P.make_ident(ident, identf)
        tbs = {}
        for k in ("gqP", "gkP", "gqS", "gkS", "decA", "resetP", "pinvR", "pinv0", "selp", "sels", "cret"):
            tbs[k] = P.sb("t_" + k, list(tb_shapes[k]), F32)
            P.dma("sp", tbs[k].t, tb_d[k], wr=[tbs[k]])
        for k in ("dmP", "dmS", "mdP", "mdS", "cmP", "cmS", "mcurP", "mprevP", "mprev0", "mcurS", "mcacheS", "bones", "onespad"):
            tbs[k] = P.sb("t_" + k, list(tb_shapes[k]), BF16)
            P.dma("pool", tbs[k].t, tb_d[k], wr=[tbs[k]])
        gpreT = P.sb("gpreT", [128, depth * 8], F32)
        P.dma("sp", gpreT.t, gpre_d.rearrange("l (kc p) -> p (l kc)", p=128), wr=[gpreT], slow=True)
        pscT = P.sb("pscT", [128, depth * 2], F32)
        P.dma("sp", pscT.t, pscale_d.rearrange("l (c p) -> p (l c)", p=128), wr=[pscT], slow=True)
        ghgT = P.sb("ghgT", [128, depth * 2], F32)
        P.dma("sp", ghgT.t, ghg_d.rearrange("l (c p) -> p (l c)", p=128), wr=[ghgT], slow=True)
        lbT = P.sb("lbT", [128, 2, depth], F32)
        for l_ in range(depth):
            P.dma("sp", lbT.t[:, :, l_], lbnd_d[l_].rearrange("(c p) -> p c", p=128), wr=[lbT], slow=True)
        esink = P.sb("esink", [128, depth * 4], F32)
        for hh in range(2):
            src = bass.AP(sink_d.tensor, sink_d.offset + hh, [[0, 64], [2, depth * 4]])
            P.dma("sp", esink.t[hh * 64:(hh + 1) * 64, :], src, wr=[esink], slow=True)
        P.op("act", lambda e: e.activation(out=esink.t[:, :], in_=esink.t[:, :], func=AF.Exp), rd=[esink], wr=[esink])
        lbm = P.sb("lbm", [128, 2], F32); lbe = P.sb("lbe", [128, 2, depth], F32); lbs = P.sb("lbs", [128, 2], F32)
        lbc = P.sb("lbc", [128, 2, depth], F32); omlb = P.sb("omlb", [128, 2, depth], F32)
        P.op("dve", lambda e: e.tensor_reduce(out=lbm.t[:, :], in_=lbT.t[:, :, :], axis=mybir.AxisListType.X, op=ALU.max), rd=[lbT], wr=[lbm])
        P.op("dve", lambda e: e.tensor_tensor(out=lbe.t[:, :, :], in0=lbT.t[:, :, :], in1=bc(lbm.t[:, :], [128, 2, depth]), op=ALU.subtract), rd=[lbT, lbm], wr=[lbe])
        P.op("act", lambda e: e.activation(out=lbe.t[:, :, :], in_=lbe.t[:, :, :], func=AF.Exp), rd=[lbe], wr=[lbe])
        P.op("dve", lambda e: e.tensor_reduce(out=lbs.t[:, :], in_=lbe.t[:, :, :], axis=mybir.AxisListType.X, op=ALU.add), rd=[lbe], wr=[lbs])
        P.op("dve", lambda e: e.reciprocal(out=lbs.t[:, :], in_=lbs.t[:, :]), rd=[lbs], wr=[lbs])
        P.op("dve", lambda e: e.tensor_tensor(out=lbe.t[:, :, :], in0=lbe.t[:, :, :], in1=bc(lbs.t[:, :], [128, 2, depth]), op=ALU.mult), rd=[lbe, lbs], wr=[lbe])
        P.op("dve", lambda e: e.memset(lbc.t[:, :, :], 0.0), wr=[lbc])
        for l in range(1, depth):
            P.op("dve", lambda e, l=l: e.tensor_tensor(out=lbc.t[:, :, l], in0=lbc.t[:, :, l - 1], in1=lbe.t[:, :, l], op=ALU.add), rd=[lbc, lbe], wr=[lbc])
        P.op("dve", lambda e: e.tensor_scalar(out=omlb.t[:, :, :], in0=lbc.t[:, :, :], scalar1=-1.0, scalar2=1.0, op0=ALU.mult, op1=ALU.add), rd=[lbc], wr=[omlb])
        wpbd = P.sb("wpbd", [128, depth * 2, 128], BF16)
        P.op("pool", lambda e: e.memset(wpbd.t[:, :, :], 0.0), wr=[wpbd])
        for l in range(depth):
            for g_ in range(4):
                c, r0 = g_ // 2, (g_ % 2) * 64
                P.dma("pool", wpbd.t[r0:r0 + 64, l * 2 + c, r0:r0 + 64], wpool_d[l, g_], wr=[wpbd])

        wtm = P.sb("wtm", [128, 8, 640], BF16)
        wbrk = [P.sb("wbrk%d" % i, [128, 10, 128], BF16) for i in range(2)]
        wbst = {"n": 0}
        wbres = [P.res("wbrb%d" % l_) for l_ in range(depth)]
        wout = P.sb("wout", [128, 8, D], BF16)
        gpost = P.sb("gpost", [128, D], F32)
        NWB = 2
        wblk = [P.sb("wblk%d" % i, [128, 8, CPB * 128], BF16) for i in range(NWB)]
        wstate = {"n": 0}
        NB1 = 12 // CPB
        wres = [[P.res("wfmb%d_%d" % (l_, k_)) for k_ in range(2)] for l_ in range(depth)]

        def convert_layer(l):
            for blk in range(NBLK):
                P.dma("pool", wfm_b[l, blk], wfm_d[l, :, blk * CPB * 128:(blk + 1) * CPB * 128].rearrange("(kc p) c -> p kc c", p=128),
                      rd=[], wr=[wres[l][int(blk >= NB1)]], ch="d_cvt%d_%d" % (l, int(blk >= NB1)))

        def convert_wbr(l):
            for m in range(8):
                P.dma("pool", wbr_b[l, m], wbr_d[l, :, m * 128:(m + 1) * 128].rearrange("(kc p) c -> p kc c", p=128), rd=[], wr=[wbres[l]], ch="d_cvb%d" % l)

        def load_block(l, blk):
            wb = wblk[wstate["n"] % NWB]
            wstate["n"] += 1
            P.dma("sp", wb.t[:, :, :], wfm_b[l, blk], rd=[wres[l][int(blk >= NB1)]], wr=[wb])
            return wb

        xts = [[P.sb("xt%d_%d" % (p_, i), [128, D], F32) for i in range(G)] for p_ in range(2)]
        xt = xts[0]
        xsns = [P.sb("xsn%d" % i, [128, D], BF16) for i in range(2)]
        xsn = xsns[0]
        ss = P.sb("ss", [128, 4], F32); rstd = P.sb("rstd", [128, 1], F32)
        hTs = [P.sb("hT%d" % i, [128, 8, NTOK], BF16) for i in range(2)]
        cur = {"hT": hTs[0]}
        res = {k: P.sb("r_" + k, [128, n, NTOK], dt) for k, (n, dt) in RES_SPEC.items()}
        yT = P.sb("yT", [128, 10, NTOK], BF16)
        mT = P.sb("mT", [128, 8, NTOK], BF16)
        gsb = [P.sb("gsb%d" % i, [128, NTOK], BF16) for i in range(2)]
        macc = P.sb("macc", [128, NTOK], F32); mtmp = P.sb("mtmp", [128, NTOK], F32)
        xo = [P.sb("xo%d" % i, [128, D], F32) for i in range(1)]
        pj = [P.ps("pj%d" % i, [128, 512], F32) for i in range(3)]
        pjs = {"n": 0}

        def getpj():
            p = pj[pjs["n"] % 3]
            pjs["n"] += 1
            return p
        scp = P.ps("scp", [128, 512], F32)
        otp = P.ps("otp", [128, 512], F32)
        kvp = [P.ps("kvp%d" % i, [128, 512], F32) for i in range(2)]
        tpp = P.ps("tpp", [128, 8, 128], BF16)
        rtabG = P.sb("rtabG", [128, 4, G * 128], F32)
        qrA = P.sb("qrA", [128, 2, 128], BF16); qdA = P.sb("qdA", [128, 2, 128], BF16)
        krA = P.sb("krA", [128, 2, 128], BF16); kdA = P.sb("kdA", [128, 2, 128], BF16)
        kdTok = P.sb("kdTok", [128, 2, 128], BF16)
        vtok = P.sb("vtok", [128, 640], BF16)
        vpad = P.sb("vpad", [128, 8, 128], BF16)
        P.op("pool", lambda e: e.memset(vpad.t[:, :, :], 0.0), wr=[vpad])
        scm = P.sb("scm", [128, 4, 128], BF16)
        osq = P.sb("osq", [128, 256], BF16); rsn = P.sb("rsn", [128, 256], F32); ytmp = P.sb("ytmp", [128, 256], F32)
        SA = P.sb("SA", [128, 2, 128], F32); SAb = P.sb("SAb", [128, 2, 128], BF16)
        hsg = P.sb("hsg", [128, 2, 128], F32); hf = P.sb("hf", [128, 2, 128], F32); hb = P.sb("hb", [128, 2, 128], F32)
        heb = P.sb("heb", [128, 2, 128], F32); henb = P.sb("henb", [128, 2, 128], F32)
        qdd = P.sb("qdd", [128, 2, 128], BF16); ktf = P.sb("ktf", [128, 2, 128], F32); ktb = P.sb("ktb", [128, 2, 128], BF16)
        decD = P.sb("decD", [128, 2, 16], F32); kdd = P.sb("kdd", [128, 2, 128], BF16)
        vexp = P.sb("vexp", [128, 16, 128], BF16)
        SD = P.sb("SD", [128, 2, 9, 128], F32); SDb = P.sb("SDb", [128, 2, 8, 128], BF16)
        SDc = [[T(SD.t, "SDc%d%d" % (j_, h_)) for h_ in range(2)] for j_ in range(2)]
        SDall = [SD, SDc[0][0], SDc[0][1], SDc[1][0], SDc[1][1]]
        qrB = P.sb("qrB", [128, 4, 128], BF16)
        krB = [P.sb("krB%d" % i, [128, 2, 128], BF16) for i in range(2)]
        vpB = [P.sb("vpB%d" % i, [128, 2, 2, 128], BF16) for i in range(2)]
        for v_ in vpB:
            P.op("pool", lambda e, v_=v_: e.memset(v_.t[:, :, :, :], 0.0), wr=[v_])
        esbs = [P.sb("esb%d" % i, [128, 4, 128], BF16) for i in range(2)]
        rden = P.sb("rden", [128, 128], F32); obt = P.sb("obt", [128, 128], F32)
        Eb = [P.sb("Eb%d" % i, [128, 2, 144], F32) for i in range(2)]
        s2 = P.sb("s2", [128, 2, 143], F32); s4 = P.sb("s4", [128, 2, 141], F32)
        s8 = P.sb("s8", [128, 137], F32); s16 = P.sb("s16", [128, 129], F32)
        plf = P.sb("plf", [128, 2, 128], F32); plb = P.sb("plb", [128, 2, 128], BF16)
        stg = P.sb("stg", [128, 256], F32)
        for t_ in (SA, SAb, SD, SDb, Eb[0], Eb[1], krB[0], krB[1]):
            nd = len(t_.t.shape)
            P.op("pool", lambda e, t_=t_: e.memset(t_.t[tuple([slice(None)] * len(t_.t.shape))], 0.0), wr=[t_])

        def act_copy(out, in_, rd, wr, func=AF.Copy, **kw):
            P.op("act", lambda e: e.activation(out=out, in_=in_, func=func, **kw), rd=rd, wr=wr)

        def dve_tt(out, in0, in1, op, rd, wr):
            P.op("dve", lambda e: e.tensor_tensor(out=out, in0=in0, in1=in1, op=op), rd=rd, wr=wr)

        def pool_tt(out, in0, in1, op, rd, wr):
            P.op("pool", lambda e: e.tensor_tensor(out=out, in0=in0, in1=in1, op=op), rd=rd, wr=wr)

        def rsqrt_ln_exp(out, in_, rd_t, wr_t):
            P.op("act", lambda e: e.activation(out=out, in_=in_, func=AF.Ln, bias=epsc.t[:in_.shape[0], 0:1]), rd=[rd_t, epsc], wr=[wr_t])
            P.op("act", lambda e: e.activation(out=out, in_=out, func=AF.Exp, scale=-0.5), rd=[wr_t], wr=[wr_t])
        epsc = P.sb("epsc", [128, 1], F32)
        P.op("dve", lambda e: e.memset(epsc.t[:, :], EPS), wr=[epsc])

        def prenorm_a(l, xtile, T_, xsn_):
            P.op("act", lambda e: e.activation(out=xsn_.t[:T_, :], in_=xtile.t[:T_, :], func=AF.Square, accum_out=ss.t[:T_, 0:1]), rd=[xtile], wr=[xsn_, ss])
            P.op("dve", lambda e: e.tensor_scalar(out=rstd.t[:T_, :], in0=ss.t[:T_, 0:1], scalar1=1.0 / D, scalar2=None, op0=ALU.mult), rd=[ss], wr=[rstd])
            rsqrt_ln_exp(rstd.t[:T_, :], rstd.t[:T_, :], rstd, rstd)
            P.op("act", lambda e: e.activation(out=xsn_.t[:T_, :], in_=xtile.t[:T_, :], func=AF.Copy, scale=rstd.t[:T_, 0:1]), rd=[xtile, rstd], wr=[xsn_])

        def prenorm_b(l, xsn_, T_, col0, hT_):
            def tr(pe):
                for kc in range(8):
                    ins = pe.transpose(tpp.t[:, kc, :T_], xsn_.t[:T_, kc * 128:(kc + 1) * 128], ident.t[:T_, :T_])
                return ins
            P.op("pe", tr, rd=[xsn_, ident], wr=[tpp])
            dve_tt(hT_.t[:, :, col0:col0 + T_], tpp.t[:, :, :T_], bc(gpreT.t[:, l * 8:(l + 1) * 8], [128, 8, T_]), ALU.mult, [tpp, gpreT], [hT_])

        def prenorm(l, xtile, T_, col0):
            prenorm_a(l, xtile, T_, xsns[0])
            prenorm_b(l, xsns[0], T_, col0, cur["hT"])

        def fm_blocks(l, blks, NTK):
            hT = cur["hT"]
            evn = {"n": 0}
            for blk in blks:
                wb = load_block(l, blk)
                for c in range(CPB):
                    ci = blk * CPB + c
                    nm, j = FM_NAMES[ci]
                    ps_ = getpj()

                    def grp(pe, wb=wb, c=c, ps_=ps_):
                        for kc in range(8):
                            ins = pe.matmul(ps_.t[:, :NTK], lhsT=wb.t[:, kc, c * 128:(c + 1) * 128], rhs=hT.t[:, kc, :NTK], start=(kc == 0), stop=(kc == 7))
                        return ins
                    P.op("pe", grp, rd=[wb, hT], wr=[ps_])
                    dst = res[nm]
                    if nm in ROPE_TAB:
                        tab = rtabG.t[:, ROPE_TAB[nm], :NTK]
                        P.op("dve", lambda e, dst=dst, j=j, ps_=ps_, tab=tab: e.tensor_tensor(out=dst.t[:, j, :NTK], in0=ps_.t[:, :NTK], in1=tab, op=ALU.mult), rd=[ps_, rtabG], wr=[dst])
                    elif nm in SILU:
                        act_copy(dst.t[:, j, :NTK], ps_.t[:, :NTK], [ps_], [dst], func=AF.Silu)
                    else:
                        act_copy(dst.t[:, j, :NTK], ps_.t[:, :NTK], [ps_], [dst])

        def load_layer_weights(l):
            P.dma("pool", wtm.t[:, :, :], wtm_d[l].rearrange("(kc p) c -> p kc c", p=128), wr=[wtm])
            P.dma("pool", wout.t[:, :, :], wout_d[l].rearrange("(kc p) c -> p kc c", p=128), wr=[wout])
            src = bass.AP(gpost_d.tensor, gpost_d.offset + l * D, [[0, 128], [1, D]])
            P.dma("sp", gpost.t[:, :], src, wr=[gpost])

        st = {"par": 0}

        def mixers(l, mode, cs0, T_, tile_idx, ropesrc, last):
            cs = slice(cs0, cs0 + T_)
            Pm = mode == "P"
            hT = cur["hT"]
            par = st["par"]
            st["par"] ^= 1
            pa = getpj(); pb = getpj()

            def tmg(pe):
                for kc in range(8):
                    pe.matmul(pa.t[:T_, :512], lhsT=hT.t[:, kc, cs], rhs=wtm.t[:, kc, 0:512], start=(kc == 0), stop=(kc == 7))
                for kc in range(8):
                    ins = pe.matmul(pb.t[:T_, :128], lhsT=hT.t[:, kc, cs], rhs=wtm.t[:, kc, 512:640], start=(kc == 0), stop=(kc == 7))
                return ins
            P.op("pe", tmg, rd=[hT, wtm], wr=[pa, pb])
            if DEBUG_SUB <= -4:
                return
            act_copy(vtok.t[:T_, 0:512], pa.t[:T_, :512], [pa], [vtok])
            act_copy(vtok.t[:T_, 512:640], pb.t[:T_, :128], [pb], [vtok])
            if DEBUG_SUB <= -3:
                return
            for hh in range(2):
                src = pa.t[:T_, :512].rearrange("p (a b d) -> p a b d", a=4, b=2, d=64)[:, :, hh, :]
                dstv = vpad.t[:T_, :, :].rearrange("p (a b) c -> p a b c", b=2)[:, :, hh, hh * 64:(hh + 1) * 64]
                P.op("dve", lambda e, src=src, dstv=dstv: e.tensor_copy(out=dstv, in_=src), rd=[pa], wr=[vpad])
            if DEBUG_SUB <= -2:
                return
            vb = vpB[par]
            for hh in range(2):
                srcb = pb.t[:T_, :128].rearrange("p (k d) -> p k d", k=2)
                P.op("dve", lambda e, hh=hh, srcb=srcb: e.tensor_copy(out=vb.t[:T_, :, hh, hh * 64:(hh + 1) * 64], in_=srcb), rd=[pb], wr=[vb])
            if DEBUG_SUB <= -1:
                return
            if DEBUG_SUB <= 0:
                return
            gq, gk, dm = (tbs["gqP"], tbs["gkP"], tbs["dmP"]) if Pm else (tbs["gqS"], tbs["gkS"], tbs["dmS"])
            for (xa, xsw, outr, gtab, outd) in ((res["qa"], res["qas"], qrA, gq, qdA), (res["ka"], res["kas"], krA, gk, kdA)):
                dve_tt(outr.t[:, :, :T_], xa.t[:, :, cs], xsw.t[:, :, cs], ALU.add, [xa, xsw], [outr])
                dve_tt(outd.t[:, :, :T_], outr.t[:, :, :T_], gtab.t[:, :, :T_], ALU.mult, [outr, gtab], [outd])

            if DEBUG_SUB <= 0.1:
                return

            def trk(pe, src=kdA):
                for j in range(2):
                    ins = pe.transpose(tpp.t[:T_, j, :], src.t[:, j, :T_], ident.t[:, :])
                return ins
            P.op("pe", trk, rd=[kdA, ident], wr=[tpp])
            act_copy(kdTok.t[:T_, :, :], tpp.t[:T_, 0:2, :], [tpp], [kdTok])

            if DEBUG_SUB <= 0.2:
                return

            scqA = getpj()

            def scA(pe):
                for h in range(4):
                    j, hh = h // 2, h % 2
                    bank = scp if hh == 0 else scqA
                    ins = pe.matmul(bank.t[:T_, j * 128:j * 128 + T_], lhsT=krA.t[hh * 64:(hh + 1) * 64, j, :T_], rhs=qrA.t[hh * 64:(hh + 1) * 64, j, :T_], start=True, stop=True)
                return ins
            P.op("pe", scA, rd=[krA, qrA], wr=[scp, scqA])
            for hh, bank in enumerate((scp, scqA)):
                dve_tt(scm.t[:T_, hh::2, :T_], bank.t[:T_, 0:256].rearrange("p (j t) -> p j t", j=2)[:, :, :T_], dm.t[:T_, hh::2, :T_], ALU.mult, [bank, dm], [scm])

            if DEBUG_SUB <= 0.3:
                return
            for j in range(2):
                if not Pm:
                    load_sample_state(l, "A", j)

                def oA(pe, j=j):
                    o_ = otp.t[:, j * 128:j * 128 + T_]
                    for hh in range(2):
                        pe.matmul(o_, lhsT=vpad.t[:T_, 2 * j + hh, :], rhs=scm.t[:T_, 2 * j + hh, :T_], start=(hh == 0), stop=False)
                    if Pm:
                        ins = pe.matmul(o_, lhsT=SAb.t[:, j, :], rhs=qdA.t[:, j, :T_], start=False, stop=True)
                    else:
                        for s_ in range(NS):
                            ins = pe.matmul(otp.t[:, j * 128 + s_:j * 128 + T_:NS], lhsT=S0b.t[:, s_, :], rhs=qdA.t[:, j, s_:T_:NS], start=False, stop=(s_ == NS - 1))
                    return ins
                P.op("pe", oA, rd=[SAb if Pm else S0b, qdA, vpad, scm], wr=[otp])
                if not Pm:
                    sample_state_update(l, "A", kdTok, 0, T_, j)
            if DEBUG_SUB <= 0.4:
                return
            gnorm(l, cs, T_, 0, res["za"], None)
            if DEBUG_SUB <= 0.5:
                return
            if Pm:
                def kvA(pe):
                    for j in range(2):
                        ins = pe.matmul(kvp[0].t[:, j * 128:(j + 1) * 128], lhsT=kdTok.t[:T_, j, :], rhs=vtok.t[:T_, j * 128:(j + 1) * 128], start=True, stop=True)
                    return ins
                P.op("pe", kvA, rd=[kdTok, vtok], wr=[kvp[0]])
                for j in range(2):
                    for hh in range(2):
                        r_ = slice(hh * 64, (hh + 1) * 64)
                        P.op("dve", lambda e, j=j, r_=r_: e.scalar_tensor_tensor(out=SA.t[r_, j, r_], in0=SA.t[r_, j, r_], scalar=tbs["decA"].t[r_, j:j + 1], in1=kvp[0].t[r_, j * 128 + r_.start:j * 128 + r_.stop], op0=ALU.mult, op1=ALU.add), rd=[SA, tbs["decA"], kvp[0]], wr=[SA])
                P.op("dve", lambda e: e.tensor_copy(out=SAb.t[:, :, :], in_=SA.t[:, :, :]), rd=[SA], wr=[SAb])
                if last:
                    for h in range(4):
                        j, hh = h // 2, h % 2
                        P.dma("sp", oret_d[l, h], SA.t[hh * 64:hh * 64 + 32, j, hh * 64:(hh + 1) * 64], rd=[SA], out_final=True)
            if DEBUG_SUB <= 1:
                return
            fd = res["fd"]
            act_copy(hsg.t[:, :, :T_], fd.t[:, :, cs], [fd], [hsg], func=AF.Sigmoid)
            for j in range(2):
                P.op("dve", lambda e, j=j: e.tensor_scalar(out=hf.t[:, j, :T_], in0=hsg.t[:, j, :T_], scalar1=omlb.t[:, j, l:l + 1], scalar2=lbc.t[:, j, l:l + 1], op0=ALU.mult, op1=ALU.add), rd=[hsg, omlb, lbc], wr=[hf])
            act_copy(hsg.t[:, :, :T_], hf.t[:, :, :T_], [hf], [hsg], func=AF.Ln)
            if Pm:
                for j in range(2):
                    P.op("dve", lambda e, j=j: e.tensor_tensor_scan(out=hb.t[:, j, :], data0=tbs["resetP"].t[:, :], data1=hsg.t[:, j, :], initial=0.0, op0=ALU.mult, op1=ALU.add), rd=[hsg, tbs["resetP"]], wr=[hb])
                nch, C = 8, 16
            else:
                P.op("dve", lambda e: e.tensor_copy(out=hb.t[:, :, 0:NS], in_=hsg.t[:, :, 0:NS]), rd=[hsg], wr=[hb])
                for t_ in range(1, 4):
                    dve_tt(hb.t[:, :, t_ * NS:(t_ + 1) * NS], hb.t[:, :, (t_ - 1) * NS:t_ * NS], hsg.t[:, :, t_ * NS:(t_ + 1) * NS], ALU.add, [hb, hsg], [hb])
                nch, C = NS, 4
            act_copy(heb.t[:, :, :T_], hb.t[:, :, :T_], [hb], [heb], func=AF.Exp)
            act_copy(henb.t[:, :, :T_], hb.t[:, :, :T_], [hb], [henb], func=AF.Exp, scale=-1.0)
            dve_tt(qdd.t[:, :, :T_], res["qd"].t[:, :, cs], heb.t[:, :, :T_], ALU.mult, [res["qd"], heb], [qdd])
            P.op("dve", lambda e: e.tensor_scalar(out=hf.t[:, :, :T_], in0=hf.t[:, :, :T_], scalar1=-1.0, scalar2=1.0, op0=ALU.mult, op1=ALU.add), rd=[hf], wr=[hf])
            dve_tt(ktf.t[:, :, :T_], hf.t[:, :, :T_], henb.t[:, :, :T_], ALU.mult, [hf, henb], [ktf])
            act_copy(ktb.t[:, :, :T_], ktf.t[:, :, :T_], [ktf], [ktb])
            if Pm:
                lastv = heb.t[:, :, :].rearrange("p j (c k) -> p j c k", k=16)[:, :, :, 15]
                P.op("dve", lambda e: e.tensor_copy(out=decD.t[:, :, 0:8], in_=lastv), rd=[heb], wr=[decD])
                dve_tt(kdd.t[:, :, :].rearrange("p j (c k) -> p j c k", k=16), ktf.t[:, :, :].rearrange("p j (c k) -> p j c k", k=16), bc(decD.t[:, :, 0:8], [128, 2, 8, 16]), ALU.mult, [ktf, decD], [kdd])
            else:
                P.op("dve", lambda e: e.tensor_copy(out=decD.t[:, :, 0:NS], in_=heb.t[:, :, 3 * NS:4 * NS]), rd=[heb], wr=[decD])
                dve_tt(kdd.t[:, :, :T_].rearrange("p j (k c) -> p j k c", c=NS), ktf.t[:, :, :T_].rearrange("p j (k c) -> p j k c", c=NS), bcast(decD.t[:, :, 0:NS], 2, 4), ALU.mult, [ktf, decD], [kdd])
            P.op("pe", lambda pe: trk(pe, kdd), rd=[kdd, ident], wr=[tpp])
            act_copy(kdTok.t[:T_, :, :], tpp.t[:T_, 0:2, :], [tpp], [kdTok])

            scqD = getpj()

            def scD(pe):
                for h in range(4):
                    j, hh = h // 2, h % 2
                    bank = scp if hh == 0 else scqD
                    ins = pe.matmul(bank.t[:T_, j * 128:j * 128 + T_], lhsT=ktb.t[hh * 64:(hh + 1) * 64, j, :T_], rhs=qdd.t[hh * 64:(hh + 1) * 64, j, :T_], start=True, stop=True)
                return ins
            P.op("pe", scD, rd=[ktb, qdd], wr=[scp, scqD])
            md = tbs["mdP"] if Pm else tbs["mdS"]
            for hh, bank in enumerate((scp, scqD)):
                dve_tt(scm.t[:T_, hh::2, :T_], bank.t[:T_, 0:256].rearrange("p (j t) -> p j t", j=2)[:, :, :T_], bcast(md.t[:T_, :T_], 1, 2), ALU.mult, [bank, md], [scm])
            if Pm:
                cm = tbs["cmP"]
                for j in range(2):
                    P.op("dve", lambda e, j=j: e.tensor_tensor(out=vexp.t[:, 0:8, :], in0=bcast(vtok.t[:, 256 + j * 128:256 + (j + 1) * 128], 1, 8), in1=bc(cm.t[:, :], [128, 8, 128]), op=ALU.mult), rd=[vtok, cm], wr=[vexp])

                    def kvD(pe, j=j):
                        for q in range(2):
                            ins = pe.matmul(kvp[q].t[:, :], lhsT=kdTok.t[:, j, :], rhs=vexp.t[:, q * 4:(q + 1) * 4, :], start=True, stop=True)
                        return ins
                    P.op("pe", kvD, rd=[kdTok, vexp], wr=[kvp[0], kvp[1]])
                    for c in range(8):
                        for hh in range(2):
                            r_ = slice(hh * 64, (hh + 1) * 64)
                            P.op("dve", lambda e, j=j, c=c, r_=r_: e.scalar_tensor_tensor(out=SD.t[r_, j, c + 1, r_], in0=SD.t[r_, j, c, r_], scalar=decD.t[r_, j, c:c + 1], in1=kvp[c // 4].t[r_, (c % 4) * 128 + r_.start:(c % 4) * 128 + r_.stop], op0=ALU.mult, op1=ALU.add), rd=[SDc[j][hh], decD, kvp[c // 4]], wr=[SDc[j][hh]])
                act_copy(SDb.t[:, :, :, :], SD.t[:, :, 0:8, :], SDall, [SDb])

            for j in range(2):
                if not Pm:
                    load_sample_state(l, "D", j)

                def oD(pe, j=j):
                    o_ = otp.t[:, j * 128:j * 128 + T_]
                    for hh in range(2):
                        pe.matmul(o_, lhsT=vpad.t[:T_, 4 + 2 * j + hh, :], rhs=scm.t[:T_, 2 * j + hh, :T_], start=(hh == 0), stop=False)
                    for c in range(nch):
                        if Pm:
                            ins = pe.matmul(otp.t[:, j * 128 + c * 16:j * 128 + (c + 1) * 16], lhsT=SDb.t[:, j, c, :], rhs=qdd.t[:, j, c * 16:(c + 1) * 16], start=False, stop=(c == nch - 1))
                        else:
                            ins = pe.matmul(otp.t[:, j * 128 + c:j * 128 + T_:NS], lhsT=S0b.t[:, c, :], rhs=qdd.t[:, j, c:T_:NS], start=False, stop=(c == nch - 1))
                    return ins
                P.op("pe", oD, rd=[SDb if Pm else S0b, qdd, vpad, scm], wr=[otp])
                if not Pm:
                    sample_state_update(l, "D", kdTok, 256, T_, j)
            gnorm(l, cs, T_, 8, res["zd"], ghgT)
            if Pm:
                P.op("dve", lambda e: e.tensor_copy(out=SD.t[:, :, 0, :], in_=SD.t[:, :, 8, :]), rd=SDall, wr=SDall)
                if last:
                    for h in range(4):
                        j, hh = h // 2, h % 2
                        P.dma("sp", ohg_d[l, h], SD.t[hh * 64:(hh + 1) * 64, j, 0, hh * 64:(hh + 1) * 64], rd=SDall, out_final=True, ch="d_SD")
            if DEBUG_SUB <= 2:
                return
            kcur = krB[par]; kprev = krB[par ^ 1]; vprev = vpB[par ^ 1]
            dve_tt(qrB.t[:, :, :T_], res["qb"].t[:, :, cs], res["qbs"].t[:, :, cs], ALU.add, [res["qb"], res["qbs"]], [qrB])
            dve_tt(kcur.t[:, :, :T_], res["kb"].t[:, :, cs], res["kbs"].t[:, :, cs], ALU.add, [res["kb"], res["kbs"]], [kcur])
            if Pm:
                mprev = tbs["mprev0"] if tile_idx == 0 else tbs["mprevP"]
                mcur = tbs["mcurP"]
                for jq in range(4):
                    kv = jq // 2
                    sp_ = getpj(); sq_ = getpj(); op_ = kvp[jq % 2]
                    esb = esbs[jq % 2]

                    def scB(pe, jq=jq, kv=kv, sp_=sp_, sq_=sq_):
                        for hh, bank in enumerate((sp_, sq_)):
                            r_ = slice(hh * 64, (hh + 1) * 64)
                            for b_, kt_ in enumerate((kprev, kcur)):
                                ins = pe.matmul(bank.t[:, b_ * 128:(b_ + 1) * 128], lhsT=kt_.t[r_, kv, :], rhs=qrB.t[r_, jq, :], start=True, stop=True)
                        return ins
                    P.op("pe", scB, rd=[kprev, kcur, qrB], wr=[sp_, sq_])
                    for hh, bank in enumerate((sp_, sq_)):
                        act_copy(esb.t[:, 2 * hh:2 * hh + 2, :].rearrange("p a t -> p (a t)"), bank.t[:, 0:256], [bank], [esb], func=AF.Exp, scale=0.125)
                    for b_, mk in enumerate((mprev, mcur)):
                        P.op("pool", lambda e, b_=b_, mk=mk, esb=esb: e.tensor_tensor(out=esb.t[:, b_::2, :], in0=esb.t[:, b_::2, :], in1=bcast(mk.t[:, :], 1, 2), op=ALU.mult), rd=[esb, mk], wr=[esb])

                    def pvB(pe, jq=jq, kv=kv, op_=op_, esb=esb):
                        n = 0
                        for hh in range(2):
                            for b_, vv in enumerate((vprev, vb)):
                                pe.matmul(op_.t[:, 0:128], lhsT=vv.t[:, kv, hh, :], rhs=esb.t[:, hh * 2 + b_, :], start=(n == 0), stop=(n == 3))
                                n += 1
                        n = 0
                        for hh in range(2):
                            for b_ in range(2):
                                ins = pe.matmul(op_.t[:, 128:256], lhsT=tbs["onespad"].t[:, hh, :], rhs=esb.t[:, hh * 2 + b_, :], start=(n == 0), stop=(n == 3))
                                n += 1
                        return ins
                    P.op("pe", pvB, rd=[vprev, vb, esb, tbs["onespad"]], wr=[op_])
                    swa_finish(l, jq, op_, cs, T_)
                if last:
                    def trkb(pe):
                        for kv in range(2):
                            ins = pe.transpose(tpp.t[:, kv, :], kcur.t[:, kv, :], ident.t[:, :])
                        return ins
                    P.op("pe", trkb, rd=[kcur, ident], wr=[tpp])
                    P.op("dve", lambda e: e.tensor_copy(out=stg.t[:, 0:128].rearrange("p (k d) -> p k d", k=2), in_=tpp.t[:, 0:2, 0:64]), rd=[tpp], wr=[stg])
                    P.dma("sp", ok_d[l], stg.t[:, 0:128], rd=[stg], out_final=True)
                    P.op("dve", lambda e: e.tensor_copy(out=stg.t[:, 128:256], in_=vtok.t[:, 512:640]), rd=[vtok], wr=[stg])
                    P.dma("sp", ov_d[l], stg.t[:, 128:256], rd=[stg], out_final=True)
            else:
                sample_swa(l, kcur, vb, cs, T_)
            if DEBUG_SUB <= 3:
                return
            uc = res["uc"]
            Ec = Eb[par]; Ep = Eb[par ^ 1]
            P.op("pool", lambda e: e.tensor_copy(out=Ec.t[:, :, 16:16 + T_], in_=uc.t[:, :, cs]), rd=[uc], wr=[Ec])
            if Pm:
                P.op("pool", lambda e: e.tensor_copy(out=Ec.t[:, :, 0:16], in_=Ep.t[:, :, 128:144]), rd=[Ep], wr=[Ec])
                W = 16 + 128
                pooling(l, Ec, W, T_, cs, tbs["pinv0"] if tile_idx == 0 else tbs["pinvR"])
                if last:
                    for c in range(2):
                        P.op("pe", lambda pe, c=c: pe.transpose(otp.t[:, c * 128:(c + 1) * 128], Ec.t[:, c, 16:144], identf.t[:, :]), rd=[Ec, identf], wr=[otp])
                    P.op("dve", lambda e: e.tensor_copy(out=stg.t[:, :], in_=otp.t[:, 0:256]), rd=[otp], wr=[stg])
                    P.dma("sp", opool_d[l], stg.t[113:128, :], rd=[stg], out_final=True)
            else:
                sample_pool(l, Ec, cs, T_)

        def gnorm(l, cs, T_, y0, ztile, gaincol):
            o3 = otp.t[:, 0:256].rearrange("p (j t) -> p j t", j=2)[:, :, :T_]
            P.op("act", lambda e: e.activation(out=osq.t[:, :].rearrange("p (j t) -> p j t", j=2)[:, :, :T_], in_=o3, func=AF.Square), rd=[otp], wr=[osq])
            P.op("pe", lambda pe: pe.matmul(otp.t[:, 256:512], lhsT=tbs["bones"].t[:, :], rhs=osq.t[:, :], start=True, stop=True), rd=[osq, tbs["bones"]], wr=[otp])
            rsqrt_ln_exp(rsn.t[:, :], otp.t[:, 256:512], otp, rsn)
            r3 = rsn.t[:, :].rearrange("p (j t) -> p j t", j=2)[:, :, :T_]
            y3 = ytmp.t[:, :].rearrange("p (j t) -> p j t", j=2)[:, :, :T_]
            dve_tt(y3, o3, r3, ALU.mult, [otp, rsn], [ytmp])
            if gaincol is None:
                dve_tt(yT.t[:, y0:y0 + 2, cs], y3, ztile.t[:, :, cs], ALU.mult, [ytmp, ztile], [yT])
            else:
                for j in range(2):
                    P.op("dve", lambda e, j=j: e.scalar_tensor_tensor(out=yT.t[:, y0 + j, cs], in0=ytmp.t[:, j * 128:j * 128 + T_], scalar=gaincol.t[:, l * 2 + j:l * 2 + j + 1], in1=ztile.t[:, j, cs], op0=ALU.mult, op1=ALU.mult), rd=[ytmp, gaincol, ztile], wr=[yT])

        def swa_finish(l, jq, op_, cs, T_):
            P.op("dve", lambda e: e.tensor_scalar(out=rden.t[:, :T_], in0=op_.t[:, 128:128 + T_], scalar1=esink.t[:, l * 4 + jq:l * 4 + jq + 1], scalar2=None, op0=ALU.add), rd=[op_, esink], wr=[rden])
            P.op("dve", lambda e: e.reciprocal(out=rden.t[:, :T_], in_=rden.t[:, :T_]), rd=[rden], wr=[rden])
            dve_tt(obt.t[:, :T_], op_.t[:, 0:T_], rden.t[:, :T_], ALU.mult, [op_, rden], [obt])
            dve_tt(yT.t[:, 2 + jq, cs], obt.t[:, :T_], res["zb"].t[:, jq, cs], ALU.mult, [obt, res["zb"]], [yT])

        def pooling(l, Ec, W, T_, cs, pinv):
            pool_tt(s2.t[:, :, 0:W - 1], Ec.t[:, :, 1:W], Ec.t[:, :, 0:W - 1], ALU.add, [Ec], [s2])
            pool_tt(s4.t[:, :, 0:W - 3], s2.t[:, :, 2:W - 1], s2.t[:, :, 0:W - 3], ALU.add, [s2], [s4])
            pool_tt(s8.t[:, 0:W - 7], s4.t[:, 1, 4:W - 3], s4.t[:, 1, 0:W - 7], ALU.add, [s4], [s8])
            pool_tt(s16.t[64:128, 0:W - 15], s8.t[64:128, 8:W - 7], s8.t[64:128, 0:W - 15], ALU.add, [s8], [s16])
            wsrc = [(s2.t[0:64, 0, 15:15 + T_], 0, 0), (s4.t[64:128, 0, 13:13 + T_], 0, 64), (s8.t[0:64, 9:9 + T_], 1, 0), (s16.t[64:128, 1:1 + T_], 1, 64)]
            for src, c, r0 in wsrc:
                pool_tt(plf.t[r0:r0 + 64, c, :T_], src, pinv.t[r0:r0 + 64, c, :T_], ALU.mult, [s2, s4, s8, s16, pinv], [plf])
            pool_tt(plb.t[:, :, :T_], plf.t[:, :, :T_], Ec.t[:, :, 16:16 + T_], ALU.subtract, [plf, Ec], [plb])
            pp = getpj()

            def mmc(pe):
                for c in range(2):
                    ins = pe.matmul(pp.t[:, c * 128:c * 128 + T_], lhsT=wpbd.t[:, l * 2 + c, :], rhs=plb.t[:, c, :T_], start=True, stop=True)
                return ins
            P.op("pe", mmc, rd=[wpbd, plb], wr=[pp])
            for c in range(2):
                P.op("dve", lambda e, c=c: e.scalar_tensor_tensor(out=yT.t[:, 6 + c, cs], in0=pp.t[:, c * 128:c * 128 + T_], scalar=pscT.t[:, l * 2 + c:l * 2 + c + 1], in1=res["zc"].t[:, c, cs], op0=ALU.mult, op1=ALU.mult), rd=[pp, pscT, res["zc"]], wr=[yT])

        def merge_out(l, NTK, tiles, nxt=None):
            hT = cur["hT"]
            for m in range(8):
                if nxt is not None and m == 2:
                    for i in range(G):
                        r0 = (nxt["g"] * G + i) * 128
                        P.dma("sp", nxt["xt"][i].t[:, :], nxt["src"][r0:r0 + 128, :], rd=nxt["rd"], wr=[nxt["xt"][i]])
                        prenorm_a(l, nxt["xt"][i], 128, xsns[i])
                koff = [0, 2, 6, 8, 10]
                wbk = wbrk[wbst["n"] % 2]
                wbst["n"] += 1
                P.dma("sp", wbk.t[:, :, :], wbr_b[l, m], rd=[wbres[l]], wr=[wbk])
                for i in range(4):
                    if i % CPB == 0:
                        wb = load_block(l, (36 + m * 4 + i) // CPB)
                    pg = getpj()

                    def gg(pe, i=i, pg=pg, wb=wb):
                        for kc in range(8):
                            ins = pe.matmul(pg.t[:, :NTK], lhsT=wb.t[:, kc, (i % CPB) * 128:(i % CPB + 1) * 128], rhs=hT.t[:, kc, :NTK], start=(kc == 0), stop=(kc == 7))
                        return ins
                    P.op("pe", gg, rd=[wb, hT], wr=[pg])
                    gs = gsb[i % 2]
                    act_copy(gs.t[:, :NTK], pg.t[:, :NTK], [pg], [gs], func=AF.Sigmoid)
                    pb_ = getpj()

                    def bb(pe, i=i, pb_=pb_, wbk=wbk):
                        ks = list(range(koff[i], koff[i + 1]))
                        for n, kc in enumerate(ks):
                            ins = pe.matmul(pb_.t[:, :NTK], lhsT=wbk.t[:, kc, :], rhs=yT.t[:, kc, :NTK], start=(n == 0), stop=(n == len(ks) - 1))
                        return ins
                    P.op("pe", bb, rd=[wbk, yT], wr=[pb_])
                    if i == 0:
                        dve_tt(macc.t[:, :NTK], pb_.t[:, :NTK], gs.t[:, :NTK], ALU.mult, [pb_, gs], [macc])
                    else:
                        dve_tt(mtmp.t[:, :NTK], pb_.t[:, :NTK], gs.t[:, :NTK], ALU.mult, [pb_, gs], [mtmp])
                        if i < 3:
                            P.op("pool", lambda e: e.tensor_tensor(out=macc.t[:, :NTK], in0=macc.t[:, :NTK], in1=mtmp.t[:, :NTK], op=ALU.add), rd=[macc, mtmp], wr=[macc])
                        else:
                            P.op("pool", lambda e, m=m: e.tensor_tensor(out=mT.t[:, m, :NTK], in0=macc.t[:, :NTK], in1=mtmp.t[:, :NTK], op=ALU.add), rd=[macc, mtmp], wr=[mT])
            if nxt is not None:
                for i in range(G):
                    prenorm_b(l, xsns[i], 128, i * 128, nxt["hT"])
            for ti, (xtile, T_, col0, dst_ap, dres) in enumerate(tiles):
                pa = getpj(); pb = getpj()

                def og(pe, pa=pa, pb=pb, col0=col0, T_=T_):
                    for n_, pp in enumerate((pa, pb)):
                        for kc in range(8):
                            ins = pe.matmul(pp.t[:T_, :], lhsT=mT.t[:, kc, col0:col0 + T_], rhs=wout.t[:, kc, n_ * 512:(n_ + 1) * 512], start=(kc == 0), stop=(kc == 7))
                    return ins
                P.op("pe", og, rd=[mT, wout], wr=[pa, pb])
                for n_, pp in enumerate((pa, pb)):
                    P.op("act", lambda e, n_=n_, pp=pp, T_=T_: e.activation(out=xsn.t[:T_, n_ * 512:(n_ + 1) * 512], in_=pp.t[:T_, :], func=AF.Square, accum_out=ss.t[:T_, 2 + n_:3 + n_]), rd=[pp], wr=[xsn, ss])
                P.op("dve", lambda e, T_=T_: e.tensor_scalar(out=rstd.t[:T_, :], in0=ss.t[:T_, 2:3], scalar1=ss.t[:T_, 3:4], scalar2=1.0 / D, op0=ALU.add, op1=ALU.mult), rd=[ss], wr=[rstd])
                rsqrt_ln_exp(rstd.t[:T_, :], rstd.t[:T_, :], rstd, rstd)
                xo_ = xo[0]
                for n_, pp in enumerate((pa, pb)):
                    sl = slice(n_ * 512, (n_ + 1) * 512)
                    P.op("dve", lambda e, pp=pp, sl=sl, T_=T_, xo_=xo_: e.scalar_tensor_tensor(out=xo_.t[:T_, sl], in0=pp.t[:T_, :], scalar=rstd.t[:T_, 0:1], in1=gpost.t[:T_, sl], op0=ALU.mult, op1=ALU.mult), rd=[pp, rstd, gpost], wr=[xo_])
                P.op("pool", lambda e, T_=T_, xo_=xo_, xtile=xtile: e.tensor_tensor(out=xo_.t[:T_, :], in0=xo_.t[:T_, :], in1=xtile.t[:T_, :], op=ALU.add), rd=[xo_, xtile], wr=[xo_])
                if dst_ap is None:
                    P.op("pool", lambda e, T_=T_, xo_=xo_, xtile=xtile: e.tensor_copy(out=xtile.t[:T_, :], in_=xo_.t[:T_, :]), rd=[xo_], wr=[xtile])
                if dst_ap is not None:
                    P.dma("sp", dst_ap, xo_.t[:T_, :], rd=[xo_], wr=[dres], out_final=True)

        S0f = P.sb("S0f", [128, NS, 128], F32); S0b = P.sb("S0b", [128, NS, 128], BF16); S1f = S0f
        vexs = vexp
        xsm = P.sb("xsm", [128, D], F32)
        kcS = P.sb("kcS", [128, NS, 128], BF16); vcS = kcS
        kcT = P.sb("kcT", [128, 2, 128], BF16); kcd = P.sb("kcd", [128, 256], BF16)
        vcp = [P.sb("vcp%d" % i, [128, 2, 2, 128], BF16) for i in range(2)]
        esS = P.sb("esS", [128, NS, 8, 4], BF16)
        esC = P.sb("esC", [128, 8, 64], BF16)
        for t_ in (S0f, vcp[0], vcp[1]):
            P.op("pool", lambda e, t_=t_: e.memset(t_.t[tuple([slice(None)] * len(t_.t.shape))], 0.0), wr=[t_])

        def load_sample_state(l, which, j):
            src_d, dk = (sret_d, 32) if which == "A" else (shg_d, 64)
            for hh in range(2):
                h = 2 * j + hh
                P.dma("sp", S0f.t[hh * 64:hh * 64 + dk, :, hh * 64:(hh + 1) * 64], src_d[l, :, h].rearrange("s k v -> k s v"), wr=[S0f])
            P.op("pool", lambda e: e.tensor_copy(out=S0b.t[:, :, :], in_=S0f.t[:, :, :]), rd=[S0f], wr=[S0b])

        def sample_state_update(l, which, kdt, voff, T_, j):
            cm = tbs["cmS"]
            dst_d, dk = (osret_d, 32) if which == "A" else (oshg_d, 64)
            if True:
                P.op("dve", lambda e, j=j: e.tensor_tensor(out=vexs.t[:T_, :, :], in0=bcast(vtok.t[:T_, voff + j * 128:voff + (j + 1) * 128], 1, NS), in1=bc(cm.t[:T_, :], [T_, NS, 128]), op=ALU.mult), rd=[vtok, cm], wr=[vexs])
                for q in range(4):
                    kb_ = kvp[q % 2]
                    P.op("pe", lambda pe, j=j, q=q, kb_=kb_: pe.matmul(kb_.t[:, :], lhsT=kdt.t[:T_, j, :], rhs=vexs.t[:T_, q * 4:(q + 1) * 4, :], start=True, stop=True), rd=[kdt, vexs], wr=[kb_])
                    for hh in range(2):
                        r_ = slice(hh * 64, (hh + 1) * 64)
                        kv3 = kb_.t[r_, :].rearrange("p (s c) -> p s c", s=4)[:, :, r_]
                        o3 = S1f.t[r_, q * 4:(q + 1) * 4, r_]
                        i3 = S0f.t[r_, q * 4:(q + 1) * 4, r_]
                        if which == "A":
                            P.op("dve", lambda e, o3=o3, i3=i3, kv3=kv3, r_=r_, j=j: e.scalar_tensor_tensor(out=o3, in0=i3, scalar=tbs["decA"].t[r_, 2 + j:3 + j], in1=kv3, op0=ALU.mult, op1=ALU.add), rd=[S0f, kb_, tbs["decA"]], wr=[S1f])
                        else:
                            dve_tt(o3, i3, bc(decD.t[r_, j, q * 4:(q + 1) * 4], [64, 4, 64]), ALU.mult, [S0f, decD], [S1f])
                            dve_tt(o3, o3, kv3, ALU.add, [S1f, kb_], [S1f])
            for hh in range(2):
                h = 2 * j + hh
                P.dma("sp", dst_d[l, :, h].rearrange("s k v -> k s v"), S1f.t[hh * 64:hh * 64 + dk, :, hh * 64:(hh + 1) * 64], rd=[S1f], out_final=True)

        def sample_swa(l, kcur, vb, cs, T_):
            P.dma("pool", kcS.t[:, :, :], sk_d[l].rearrange("s k c -> k s c"), wr=[kcS])
            kres = P.res("osk%d" % l); vres = P.res("osv%d" % l)
            P.dma("sp", osk_d[l, :, 0:124, :], sk_d[l, :, 4:128, :], rd=[], wr=[kres], ch="d_cpk", out_final=True)
            P.dma("sp", osv_d[l, :, 0:124, :], sv_d[l, :, 4:128, :], rd=[], wr=[vres], ch="d_cpv", out_final=True)
            pc = [getpj(), getpj()]
            for s_ in range(NS):
                P.op("act", lambda e, s_=s_: e.activation(out=kcd.t[:, :].rearrange("p (k b d) -> p k b d", k=2, b=2), in_=bcast(kcS.t[:, s_, :].rearrange("p (k d) -> p k d", k=2), 2, 2), func=AF.Copy), rd=[kcS], wr=[kcd])

                def trc(pe):
                    for kv in range(2):
                        ins = pe.transpose(tpp.t[:, kv, :], kcd.t[:, kv * 128:(kv + 1) * 128], ident.t[:, :])
                    return ins
                P.op("pe", trc, rd=[kcd, ident], wr=[tpp])
                P.op("dve", lambda e: e.tensor_copy(out=kcT.t[:, :, :], in_=tpp.t[:, 0:2, :]), rd=[tpp], wr=[kcT])
                def scc(pe, s_=s_):
                    for hh in range(2):
                        r_ = slice(hh * 64, (hh + 1) * 64)
                        for jq in range(4):
                            kv = jq // 2
                            c0 = s_ * 16 + jq * 4
                            ins = pe.matmul(pc[hh].t[:, c0:c0 + 4], lhsT=kcT.t[r_, kv, :], rhs=qrB.t[r_, jq, s_:T_:NS], start=True, stop=True)
                    return ins
                P.op("pe", scc, rd=[kcT, qrB], wr=[pc[0], pc[1]])
            P.dma("pool", vcS.t[:, :, :], sv_d[l].rearrange("s k c -> k s c"), wr=[vcS])
            for hh in range(2):
                P.op("act", lambda e, hh=hh: e.activation(out=esS.t[:, :, hh::2, :], in_=pc[hh].t[:, 0:256].rearrange("p (s j t) -> p s j t", s=NS, j=4), func=AF.Exp, scale=0.125), rd=[pc[hh]], wr=[esS])
            mc4 = tbs["mcacheS"].t[:, 0:T_:NS]
            P.op("dve", lambda e: e.tensor_tensor(out=esS.t[:, :, :, :].rearrange("p s h t -> p (s h) t"), in0=esS.t[:, :, :, :].rearrange("p s h t -> p (s h) t"), in1=bcast(mc4, 1, NS * 8), op=ALU.mult), rd=[esS, tbs["mcacheS"]], wr=[esS])
            pcur = [getpj(), getpj()]

            def scur(pe):
                for hh in range(2):
                    r_ = slice(hh * 64, (hh + 1) * 64)
                    for jq in range(4):
                        kv = jq // 2
                        ins = pe.matmul(pcur[hh].t[:T_, jq * 64:(jq + 1) * 64], lhsT=kcur.t[r_, kv, :T_], rhs=qrB.t[r_, jq, :T_], start=True, stop=True)
                return ins
            P.op("pe", scur, rd=[kcur, qrB], wr=[pcur[0], pcur[1]])
            for hh in range(2):
                P.op("act", lambda e, hh=hh: e.activation(out=esC.t[:T_, hh::2, :], in_=pcur[hh].t[:T_, 0:256].rearrange("p (j t) -> p j t", j=4), func=AF.Exp, scale=0.125), rd=[pcur[hh]], wr=[esC])
            dve_tt(esC.t[:T_, :, :], esC.t[:T_, :, :], bcast(tbs["mcurS"].t[:T_, :T_], 1, 8), ALU.mult, [esC, tbs["mcurS"]], [esC])
            po = [getpj(), getpj()]
            first = {0: True, 1: True}
            for s_ in range(NS):
                vc = vcp[s_ % 2]
                for hh in range(2):
                    P.op("pool", lambda e, s_=s_, hh=hh, vc=vc: e.tensor_copy(out=vc.t[:, :, hh, hh * 64:(hh + 1) * 64], in_=vcS.t[:, s_, :].rearrange("p (k d) -> p k d", k=2)), rd=[vcS], wr=[vc])

                def pvc(pe, s_=s_, vc=vc):
                    for h in range(8):
                        jq, hh, kv = h // 2, h % 2, h // 4
                        i = jq % 2
                        rhs = esS.t[:, s_, h, :]
                        for w_, lt in enumerate((vc.t[:, kv, hh, :], tbs["onespad"].t[:, hh, :])):
                            c0 = (i * 2 + w_) * 64
                            ins = pe.matmul(po[kv].t[:, c0 + s_:c0 + T_:NS], lhsT=lt, rhs=rhs, start=(s_ == 0 and h % 4 == 0 and w_ == 0), stop=False, skip_group_check=True)
                    return ins
                P.op("pe", pvc, rd=[vc, esS, tbs["onespad"]], wr=[po[0], po[1]])

            def pvn(pe):
                for kv in range(2):
                    for i in range(2):
                        jq = kv * 2 + i
                        for hh in range(2):
                            h = 2 * jq + hh
                            for w_, lt in enumerate((vb.t[:T_, kv, hh, :], tbs["onespad"].t[:T_, hh, :])):
                                c0 = (i * 2 + w_) * 64
                                ins = pe.matmul(po[kv].t[:, c0:c0 + T_], lhsT=lt, rhs=esC.t[:T_, h, :T_], start=False, stop=(hh == 1 and i == 1 and w_ == 1), skip_group_check=True)
                return ins
            P.op("pe", pvn, rd=[vb, esC, tbs["onespad"]], wr=[po[0], po[1]])
            for jq in range(4):
                kv, i = jq // 2, jq % 2
                v4 = po[kv].t[:, 0:256].rearrange("p (i w c) -> p i w c", i=2, w=2)
                P.op("dve", lambda e, jq=jq, v4=v4, i=i: e.tensor_scalar(out=rden.t[:, :T_], in0=v4[:, i, 1, 0:T_], scalar1=esink.t[:, l * 4 + jq:l * 4 + jq + 1], scalar2=None, op0=ALU.add), rd=[po[kv], esink], wr=[rden])
                P.op("dve", lambda e: e.reciprocal(out=rden.t[:, :T_], in_=rden.t[:, :T_]), rd=[rden], wr=[rden])
                dve_tt(obt.t[:, :T_], v4[:, i, 0, 0:T_], rden.t[:, :T_], ALU.mult, [po[kv], rden], [obt])
                dve_tt(yT.t[:, 2 + jq, cs], obt.t[:, :T_], res["zb"].t[:, jq, cs], ALU.mult, [obt, res["zb"]], [yT])
            def trkb(pe):
                for kv in range(2):
                    ins = pe.transpose(tpp.t[:T_, kv, :], kcur.t[:, kv, :T_], ident.t[:, :])
                return ins
            P.op("pe", trkb, rd=[kcur, ident], wr=[tpp])
            P.op("dve", lambda e: e.tensor_copy(out=stg.t[:T_, 0:128].rearrange("p (k d) -> p k d", k=2), in_=tpp.t[:T_, 0:2, 0:64]), rd=[tpp], wr=[stg])
            P.op("dve", lambda e: e.tensor_copy(out=stg.t[:T_, 128:256], in_=vtok.t[:T_, 512:640]), rd=[vtok], wr=[stg])
            for t_ in range(4):
                P.dma("sp", osk_d[l, :, 124 + t_, :], stg.t[t_ * NS:(t_ + 1) * NS, 0:128], rd=[stg], wr=[kres], out_final=True)
                P.dma("sp", osv_d[l, :, 124 + t_, :], stg.t[t_ * NS:(t_ + 1) * NS, 128:256], rd=[stg], wr=[vres], out_final=True)

        ES = P.sb("ES", [128, 2, NS, 20], F32)
        q2 = P.sb("q2", [128, 2, NS, 19], F32); q4 = P.sb("q4", [128, 2, NS, 17], F32)
        q8 = P.sb("q8", [128, NS, 13], F32); q16 = P.sb("q16", [128, NS, 5], F32)

        def sample_pool(l, Ec, cs, T_):
            pres = P.res("ospool%d" % l)
            P.op("pool", lambda e: e.memset(ES.t[:, :, :, 0:1], 0.0), wr=[ES])
            for half in range(2):
                P.dma("sp", stg.t[0:120, :], spool_d[l, half * 8:(half + 1) * 8].rearrange("s k c -> (s k) c"), wr=[stg])
                for c in range(2):
                    P.op("pe", lambda pe, c=c: pe.transpose(otp.t[:, c * 128:c * 128 + 120], stg.t[0:120, c * 128:(c + 1) * 128], identf.t[0:120, 0:120]), rd=[stg, identf], wr=[otp])
                P.op("dve", lambda e, half=half: e.tensor_copy(out=ES.t[:, :, half * 8:(half + 1) * 8, 1:16], in_=otp.t[:, 0:256].rearrange("p (c x) -> p c x", c=2)[:, :, 0:120].rearrange("p c (s k) -> p c s k", k=15)), rd=[otp], wr=[ES])
            P.op("dve", lambda e: e.tensor_copy(out=ES.t[:, :, :, 16:20], in_=Ec.t[:, :, 16:16 + T_].rearrange("p c (t s) -> p c s t", s=NS)), rd=[Ec], wr=[ES])
            dve_tt(q2.t[:, :, :, :], ES.t[:, :, :, 1:20], ES.t[:, :, :, 0:19], ALU.add, [ES], [q2])
            dve_tt(q4.t[:, :, :, :], q2.t[:, :, :, 2:19], q2.t[:, :, :, 0:17], ALU.add, [q2], [q4])
            dve_tt(q8.t[:, :, :], q4.t[:, 1, :, 4:17], q4.t[:, 1, :, 0:13], ALU.add, [q4], [q8])
            dve_tt(q16.t[64:128, :, :], q8.t[64:128, :, 8:13], q8.t[64:128, :, 0:5], ALU.add, [q8], [q16])
            wsrc = [(q2.t[0:64, 0, :, 15:19], 0, 0), (q4.t[64:128, 0, :, 13:17], 0, 64), (q8.t[0:64, :, 9:13], 1, 0), (q16.t[64:128, :, 1:5], 1, 64)]
            for src, c, r0 in wsrc:
                dve_tt(plf.t[r0:r0 + 64, c, :T_].rearrange("p (t s) -> p s t", s=NS), src, tbs["pinvR"].t[r0:r0 + 64, c, 0:T_].rearrange("p (t s) -> p s t", s=NS), ALU.mult, [q2, q4, q8, q16, tbs["pinvR"]], [plf])
            dve_tt(plb.t[:, :, :T_], plf.t[:, :, :T_], Ec.t[:, :, 16:16 + T_], ALU.subtract, [plf, Ec], [plb])
            pp = getpj()

            def mmc(pe):
                for c in range(2):
                    ins = pe.matmul(pp.t[:, c * 128:c * 128 + T_], lhsT=wpbd.t[:, l * 2 + c, :], rhs=plb.t[:, c, :T_], start=True, stop=True)
                return ins
            P.op("pe", mmc, rd=[wpbd, plb], wr=[pp])
            for c in range(2):
                P.op("dve", lambda e, c=c: e.scalar_tensor_tensor(out=yT.t[:, 6 + c, cs], in0=pp.t[:, c * 128:c * 128 + T_], scalar=pscT.t[:, l * 2 + c:l * 2 + c + 1], in1=res["zc"].t[:, c, cs], op0=ALU.mult, op1=ALU.mult), rd=[pp, pscT, res["zc"]], wr=[yT])
            P.dma("sp", ospool_d[l, :, 0:11, :], spool_d[l, :, 4:15, :], rd=[], wr=[pres], ch="d_cpp", out_final=True)
            for c in range(2):
                P.op("pe", lambda pe, c=c: pe.transpose(otp.t[:T_, c * 128:(c + 1) * 128], Ec.t[:, c, 16:16 + T_], identf.t[:, :]), rd=[Ec, identf], wr=[otp])
            P.op("dve", lambda e: e.tensor_copy(out=stg.t[:T_, :], in_=otp.t[:T_, 0:256]), rd=[otp], wr=[stg])
            for t_ in range(4):
                P.dma("sp", ospool_d[l, :, 11 + t_, :], stg.t[t_ * NS:(t_ + 1) * NS, :], rd=[stg], wr=[pres], out_final=True)

        lfs = P.sb("lfs", [128, 2], F32); lft = P.sb("lft", [128, 2], F32); expA = P.sb("expA", [128, 2], F32)
        Aall = P.sb("Aall", [128, 4, 2], F32)
        ones1 = P.sb("ones1", [128, 128], F32)
        P.op("pool", lambda e: e.memset(ones1.t[:, :], 1.0), wr=[ones1])
        xin_r = P.res("xin"); xg_r = P.res("xg")
        P.op("pool", lambda e: e.memset(hsg.t[:, 0, :], 0.0), wr=[hsg])
        for r0_ in (512, 1408, 1536):
            P.dma("sp", xin_d[r0_:r0_ + 128, :], hsg.t[:, 0, :], rd=[hsg], wr=[xin_r], ch="d_xin_sp")

        def phase1_tile(l, cs0, ropesrc, tail):
            T_ = 128
            cs = slice(cs0, cs0 + T_)
            hT = cur["hT"]
            par = st["par"]
            st["par"] ^= 1
            pa = getpj(); pb = getpj()

            def tmg(pe):
                for kc in range(8):
                    pe.matmul(pa.t[:T_, :512], lhsT=hT.t[:, kc, cs], rhs=wtm.t[:, kc, 0:512], start=(kc == 0), stop=(kc == 7))
                for kc in range(8):
                    ins = pe.matmul(pb.t[:T_, :128], lhsT=hT.t[:, kc, cs], rhs=wtm.t[:, kc, 512:640], start=(kc == 0), stop=(kc == 7))
                return ins
            P.op("pe", tmg, rd=[hT, wtm], wr=[pa, pb])
            act_copy(vtok.t[:T_, 0:512], pa.t[:T_, :512], [pa], [vtok])
            act_copy(vtok.t[:T_, 512:640], pb.t[:T_, :128], [pb], [vtok])
            vb = vpB[par]
            if tail:
                for hh in range(2):
                    srcb = pb.t[:T_, :128].rearrange("p (k d) -> p k d", k=2)
                    P.op("dve", lambda e, hh=hh, srcb=srcb: e.tensor_copy(out=vb.t[:T_, :, hh, hh * 64:(hh + 1) * 64], in_=srcb), rd=[pb], wr=[vb])
            dve_tt(krA.t[:, :, :T_], res["ka"].t[:, :, cs], res["kas"].t[:, :, cs], ALU.add, [res["ka"], res["kas"]], [krA])
            dve_tt(kdA.t[:, :, :T_], krA.t[:, :, :T_], tbs["gkP"].t[:, :, :T_], ALU.mult, [krA, tbs["gkP"]], [kdA])

            def trk(pe, src=kdA):
                for j in range(2):
                    ins = pe.transpose(tpp.t[:T_, j, :], src.t[:, j, :T_], ident.t[:, :])
                return ins
            P.op("pe", trk, rd=[kdA, ident], wr=[tpp])
            act_copy(kdTok.t[:T_, :, :], tpp.t[:T_, 0:2, :], [tpp], [kdTok])

            def kvA(pe):
                for j in range(2):
                    ins = pe.matmul(kvp[0].t[:, j * 128:(j + 1) * 128], lhsT=kdTok.t[:T_, j, :], rhs=vtok.t[:T_, j * 128:(j + 1) * 128], start=True, stop=True)
                return ins
            P.op("pe", kvA, rd=[kdTok, vtok], wr=[kvp[0]])
            for j in range(2):
                for hh in range(2):
                    r_ = slice(hh * 64, (hh + 1) * 64)
                    P.op("dve", lambda e, j=j, r_=r_: e.scalar_tensor_tensor(out=SA.t[r_, j, r_], in0=SA.t[r_, j, r_], scalar=tbs["decA"].t[r_, j:j + 1], in1=kvp[0].t[r_, j * 128 + r_.start:j * 128 + r_.stop], op0=ALU.mult, op1=ALU.add), rd=[SA, tbs["decA"], kvp[0]], wr=[SA])
            fd = res["fd"]
            act_copy(hsg.t[:, :, :T_], fd.t[:, :, cs], [fd], [hsg], func=AF.Sigmoid)
            for j in range(2):
                P.op("dve", lambda e, j=j: e.tensor_scalar(out=hf.t[:, j, :T_], in0=hsg.t[:, j, :T_], scalar1=omlb.t[:, j, l:l + 1], scalar2=lbc.t[:, j, l:l + 1], op0=ALU.mult, op1=ALU.add), rd=[hsg, omlb, lbc], wr=[hf])
            act_copy(hsg.t[:, :, :T_], hf.t[:, :, :T_], [hf], [hsg], func=AF.Ln)
            P.op("dve", lambda e: e.tensor_reduce(out=lft.t[:, :], in_=hsg.t[:, :, :], axis=mybir.AxisListType.X, op=ALU.add), rd=[hsg], wr=[lft])
            dve_tt(lfs.t[:, :], lfs.t[:, :], lft.t[:, :], ALU.add, [lfs, lft], [lfs])
            for j in range(2):
                P.op("dve", lambda e, j=j: e.tensor_tensor_scan(out=hb.t[:, j, :], data0=ones1.t[:, :], data1=hsg.t[:, j, :], initial=0.0, op0=ALU.mult, op1=ALU.add), rd=[hsg, ones1], wr=[hb])
            dve_tt(heb.t[:, :, :], bc(lft.t[:, :], [128, 2, 128]), hb.t[:, :, :], ALU.subtract, [lft, hb], [heb])
            act_copy(henb.t[:, :, :], heb.t[:, :, :], [heb], [henb], func=AF.Exp)
            act_copy(expA.t[:, :], lft.t[:, :], [lft], [expA], func=AF.Exp)
            P.op("dve", lambda e: e.tensor_scalar(out=hf.t[:, :, :T_], in0=hf.t[:, :, :T_], scalar1=-1.0, scalar2=1.0, op0=ALU.mult, op1=ALU.add), rd=[hf], wr=[hf])
            dve_tt(kdd.t[:, :, :], hf.t[:, :, :], henb.t[:, :, :], ALU.mult, [hf, henb], [kdd])
            P.op("pe", lambda pe: trk(pe, kdd), rd=[kdd, ident], wr=[tpp])
            act_copy(kdTok.t[:T_, :, :], tpp.t[:T_, 0:2, :], [tpp], [kdTok])

            def kvD(pe):
                for j in range(2):
                    ins = pe.matmul(kvp[1].t[:, j * 128:(j + 1) * 128], lhsT=kdTok.t[:T_, j, :], rhs=vtok.t[:T_, 256 + j * 128:256 + (j + 1) * 128], start=True, stop=True)
                return ins
            P.op("pe", kvD, rd=[kdTok, vtok], wr=[kvp[1]])
            for j in range(2):
                for hh in range(2):
                    r_ = slice(hh * 64, (hh + 1) * 64)
                    P.op("dve", lambda e, j=j, r_=r_: e.scalar_tensor_tensor(out=SD.t[r_, j, 0, r_], in0=SD.t[r_, j, 0, r_], scalar=expA.t[r_, j:j + 1], in1=kvp[1].t[r_, j * 128 + r_.start:j * 128 + r_.stop], op0=ALU.mult, op1=ALU.add), rd=SDall + [expA, kvp[1]], wr=SDall)
            if not tail:
                return
            kcur = krB[par]
            dve_tt(kcur.t[:, :, :T_], res["kb"].t[:, :, cs], res["kbs"].t[:, :, cs], ALU.add, [res["kb"], res["kbs"]], [kcur])
            Ec = Eb[par]
            act_copy(Ec.t[:, :, 16:16 + T_], res["uc"].t[:, :, cs], [res["uc"]], [Ec])

        def exchange(l):
            pl = st["par"] ^ 1
            act_copy(expA.t[:, :], lfs.t[:, :], [lfs], [expA], func=AF.Exp)
            for j in range(2):
                P.dma("sp", xin_d[j * 128:(j + 1) * 128, :], SA.t[:, j, :], rd=[SA], wr=[xin_r], ch="d_xin_sp")
                P.dma("sp", xin_d[256 + j * 128:256 + (j + 1) * 128, :], SD.t[:, j, 0, :], rd=SDall, wr=[xin_r], ch="d_xin_sp")
            P.dma("sp", xin_d[512:640, 0:2], expA.t[:, :], rd=[expA], wr=[xin_r], ch="d_xin_sp")
            P.dma("pool", xin_d[640:896, :].rearrange("(k p) c -> p k c", p=128), krB[pl].t[:, :, :], rd=[krB[pl]], wr=[xin_r], ch="d_xin_pool")
            P.dma("pool", xin_d[896:1408, :].rearrange("(a p) c -> p a c", p=128), vpB[pl].t[:, :, :, :].rearrange("p k h c -> p (k h) c"), rd=[vpB[pl]], wr=[xin_r], ch="d_xin_pool")
            for c in range(2):
                P.dma("sp", xin_d[1408 + c * 128:1408 + (c + 1) * 128, 0:16], Eb[pl].t[:, c, 128:144], rd=[Eb[pl]], wr=[xin_r], ch="d_xin_sp")
            P.coll(lambda e: e.collective_compute("AllGather", ALU.bypass, replica_groups=[[0, 1, 2, 3], [4, 5, 6, 7]], ins=[xin_d], outs=[xg_d]), rd=[xin_r], wr=[xg_r])
            selp, sels, cret = tbs["selp"], tbs["sels"], tbs["cret"]

            def rows(q, r0, n):
                return xg_d[q * XR + r0:q * XR + r0 + n, :]

            def accum(q, acc_ap, stage_ap, coef_ap, rdl, acc_t):
                if q == 0:
                    P.op("dve", lambda e: e.tensor_scalar(out=acc_ap, in0=stage_ap, scalar1=coef_ap, scalar2=None, op0=ALU.mult), rd=rdl, wr=[acc_t])
                else:
                    P.op("dve", lambda e: e.scalar_tensor_tensor(out=acc_ap, in0=stage_ap, scalar=coef_ap, in1=acc_ap, op0=ALU.mult, op1=ALU.add), rd=rdl + [acc_t], wr=[acc_t])
            for q in range(4):
                P.dma("sp", hsg.t[:, :, :], rows(q, 640, 256).rearrange("(k p) c -> p k c", p=128), rd=[xg_r], wr=[hsg])
                accum(q, hf.t[:, :, :], hsg.t[:, :, :], selp.t[:, q:q + 1], [hsg, selp], hf)
            P.op("dve", lambda e: e.tensor_copy(out=krB[pl].t[:, :, :], in_=hf.t[:, :, :]), rd=[hf], wr=[krB[pl]])
            for kv in range(2):
                for q in range(4):
                    P.dma("sp", hsg.t[:, :, :], rows(q, 896 + kv * 256, 256).rearrange("(k p) c -> p k c", p=128), rd=[xg_r], wr=[hsg])
                    accum(q, hf.t[:, :, :], hsg.t[:, :, :], selp.t[:, q:q + 1], [hsg, selp], hf)
                P.op("dve", lambda e, kv=kv: e.tensor_copy(out=vpB[pl].t[:, kv, :, :], in_=hf.t[:, :, :]), rd=[hf], wr=[vpB[pl]])
            for q in range(4):
                P.dma("sp", hsg.t[:, :, 0:16], rows(q, 1408, 256)[:, 0:16].rearrange("(c p) k -> p c k", p=128), rd=[xg_r], wr=[hsg])
                accum(q, hf.t[:, :, 0:16], hsg.t[:, :, 0:16], selp.t[:, q:q + 1], [hsg, selp], hf)
            P.op("dve", lambda e: e.tensor_copy(out=Eb[pl].t[:, :, 128:144], in_=hf.t[:, :, 0:16]), rd=[hf], wr=[Eb[pl]])
            for q in range(4):
                P.dma("sp", hsg.t[:, :, :], rows(q, 0, 256).rearrange("(k p) c -> p k c", p=128), rd=[xg_r], wr=[hsg])
                for j in range(2):
                    accum(q, hf.t[:, j, :], hsg.t[:, j, :], cret.t[:, j, q:q + 1], [hsg, cret], hf)
            P.op("dve", lambda e: e.tensor_copy(out=SA.t[:, :, :], in_=hf.t[:, :, :]), rd=[hf], wr=[SA])
            P.op("dve", lambda e: e.tensor_copy(out=SAb.t[:, :, :], in_=hf.t[:, :, :]), rd=[hf], wr=[SAb])
            for q in range(4):
                P.dma("sp", Aall.t[:, q, :], rows(q, 512, 128)[:, 0:2], rd=[xg_r], wr=[Aall])
            P.dma("sp", hb.t[:, :, :], rows(0, 256, 256).rearrange("(k p) c -> p k c", p=128), rd=[xg_r], wr=[hb])
            accum(0, heb.t[:, :, :], hb.t[:, :, :], sels.t[:, 1:2], [hb, sels], heb)
            for r in (1, 2):
                P.dma("sp", hsg.t[:, :, :], rows(r, 256, 256).rearrange("(k p) c -> p k c", p=128), rd=[xg_r], wr=[hsg])
                for j in range(2):
                    P.op("dve", lambda e, j=j, r=r: e.scalar_tensor_tensor(out=hb.t[:, j, :], in0=hb.t[:, j, :], scalar=Aall.t[:, r, j:j + 1], in1=hsg.t[:, j, :], op0=ALU.mult, op1=ALU.add), rd=[hb, Aall, hsg], wr=[hb])
                accum(1, heb.t[:, :, :], hb.t[:, :, :], sels.t[:, r + 1:r + 2], [hb, sels], heb)
            P.op("dve", lambda e: e.tensor_copy(out=SD.t[:, :, 0, :], in_=heb.t[:, :, :]), rd=[heb], wr=SDall)

        yres = [P.res("y%d" % g_) for g_ in range(NG)]
        if with_sample:
            for t_ in range(4):
                P.dma("sp", xsm.t[t_ * NS:(t_ + 1) * NS, :], xs_d[t_ * NS:(t_ + 1) * NS, :], wr=[xsm])
        convert_layer(0)
        convert_wbr(0)
        for l in range(depth):
            load_layer_weights(l)
            if l + 1 < depth:
                convert_layer(l + 1)
                convert_wbr(l + 1)
            src_d = xp_d if l == 0 else yp_d
            if bmode:
                for t_ in (SA, SD, lfs):
                    P.op("pool", lambda e, t_=t_: e.memset(t_.t[tuple([slice(None)] * len(t_.t.shape))], 0.0), wr=(SDall if t_ is SD else [t_]))
                cur["hT"] = hTs[0]
                for g_ in range(NG):
                    for i in range(G):
                        r0 = (g_ * G + i) * 128
                        P.dma("sp", xt[i].t[:, :], src_d[r0:r0 + 128, :], rd=([yres[g_]] if l > 0 else []), wr=[xt[i]])
                        prenorm(l, xt[i], 128, i * 128)
                    for i in range(G):
                        P.dma("sp", rtabG.t[:, :, i * 128:(i + 1) * 128], tb_d["ropeP"][g_ * G + i].rearrange("a p t -> p a t"), wr=[rtabG])
                    fm_blocks(l, range((12 if g_ == NG - 1 else 6) // CPB), NTOK)
                    for i in range(G):
                        ti = g_ * G + i
                        phase1_tile(l, i * 128, tb_d["ropeP"][ti], ti == NT - 1)
                exchange(l)
            cur["hT"] = hTs[0]
            for i in range(G):
                P.dma("sp", xts[0][i].t[:, :], src_d[i * 128:(i + 1) * 128, :], rd=([yres[0]] if l > 0 else []), wr=[xts[0][i]])
                prenorm(l, xts[0][i], 128, i * 128)
            for g_ in range(NG):
                cur["hT"] = hTs[g_ % 2]
                xtc = xts[g_ % 2]
                for i in range(G):
                    P.dma("sp", rtabG.t[:, :, i * 128:(i + 1) * 128], tb_d["ropeP"][g_ * G + i].rearrange("a p t -> p a t"), wr=[rtabG])
                fm_blocks(l, range(36 // CPB), NTOK)
                for i in range(G):
                    ti = g_ * G + i
                    mixers(l, "P", i * 128, 128, ti, tb_d["ropeP"][ti], ti == NT - 1)
                tiles = []
                for i in range(G):
                    r0 = (g_ * G + i) * 128
                    tiles.append((xtc[i], 128, i * 128, yp_d[r0:r0 + 128, :], yres[g_]))
                nxt = None
                if g_ + 1 < NG:
                    nxt = {"g": g_ + 1, "xt": xts[(g_ + 1) % 2], "hT": hTs[(g_ + 1) % 2], "src": src_d,
                           "rd": ([yres[g_ + 1]] if l > 0 else [])}
                if DEBUG_DUMP and l == 0 and g_ == 0:
                    P.dma("sp", dbg_y, yT.t[:, :, :], rd=[yT], out_final=True)
                    P.dma("sp", dbg_h, cur["hT"].t[:, :, :], rd=[cur["hT"]], out_final=True)
                merge_out(l, NTOK, tiles, nxt)
                if DEBUG_DUMP and l == 0 and g_ == 0:
                    P.dma("sp", dbg_m, mT.t[:, :, :], rd=[mT], out_final=True)
            cur["hT"] = hTs[0]
            if with_sample:
                prenorm(l, xsm, TS, 0)
                P.dma("sp", rtabG.t[:, :, :TS], tb_d["ropeS"].rearrange("a p t -> p a t"), wr=[rtabG])
                fm_blocks(l, range(36 // CPB), TS)
                mixers(l, "S", 0, TS, 0, tb_d["ropeS"], False)
                if DEBUG_DUMP and l == 0:
                    P.dma("sp", dbg_ys, yT.t[:, :, :], rd=[yT], out_final=True)
                ysr = P.res("ys")
                if l == depth - 1:
                    merge_out(l, TS, [(xsm, TS, 0, ys_d[:, :], ysr)])
                else:
                    merge_out(l, TS, [(xsm, TS, 0, None, None)])
            if l < depth - 1 and not bmode:
                for t_ in (SA, SAb, SD, SDb, Eb[0], Eb[1], krB[0], krB[1]):
                    P.op("pool", lambda e, t_=t_: e.memset(t_.t[tuple([slice(None)] * len(t_.t.shape))], 0.0), wr=[t_])
                for v_ in vpB:
                    P.op("pool", lambda e, v_=v_: e.memset(v_.t[:, :, :, :], 0.0), wr=[v_])
        P.finish()
    return nc


_FMC, _TMC = _fm_cols()


def kernel(x_prompt, x_sample, state_ret, cache_swa_k, cache_swa_v, state_pool, state_hgrn,
           w_in, w_branch, w_out, g_pre, g_post, attn_sink, w_pool, pool_scale, g_hgrn, lower_bounds):
    f32 = np.float32
    x_prompt = np.asarray(x_prompt, f32); x_sample = np.asarray(x_sample, f32)
    depth = int(np.asarray(w_in).shape[0])
    B, S = x_prompt.shape[0], x_prompt.shape[1]
    RPS = NCORE // B
    NT = S // 128 // RPS
    w_in = np.asarray(w_in, f32)
    wfm = np.zeros((depth, D, NFM * 128), f32)
    valid = _FMC >= 0
    wfm[:, :, valid] = w_in[:, :, _FMC[valid]]
    wtm = np.ascontiguousarray(w_in[:, :, _TMC])
    common = {
        "wfm": wfm, "wtm": wtm, "wbr": np.ascontiguousarray(w_branch, f32), "wout": np.ascontiguousarray(w_out, f32),
        "gpre": np.ascontiguousarray(g_pre, f32), "gpost": np.ascontiguousarray(g_post, f32),
        "sink": np.ascontiguousarray(attn_sink, f32), "wpool": np.ascontiguousarray(w_pool, f32),
        "pscale": np.ascontiguousarray(pool_scale, f32), "ghg": np.ascontiguousarray(g_hgrn, f32),
        "lbnd": np.ascontiguousarray(lower_bounds, f32),
    }
    sk = np.asarray(cache_swa_k, f32).reshape(depth, -1, 128, 128)
    sv = np.asarray(cache_swa_v, f32).reshape(depth, -1, 128, 128)
    in_maps = []
    for c in range(NCORE):
        b, rk = c // RPS, c % RPS
        sl = slice(c * NS, (c + 1) * NS)
        m = dict(common)
        for k, v in host_tables(NT, rk * NT * 128, rk == 0, rank=rk).items():
            m["tb_" + k] = v
        m["xp"] = np.ascontiguousarray(x_prompt[b, rk * NT * 128:(rk + 1) * NT * 128])
        m["xs"] = np.ascontiguousarray(x_sample[sl].transpose(1, 0, 2).reshape(TS, D))
        m["s_ret"] = np.ascontiguousarray(np.asarray(state_ret, f32)[:, sl])
        m["s_k"] = np.ascontiguousarray(sk[:, sl])
        m["s_v"] = np.ascontiguousarray(sv[:, sl])
        m["s_pool"] = np.ascontiguousarray(np.asarray(state_pool, f32)[:, sl])
        m["s_hg"] = np.ascontiguousarray(np.asarray(state_hgrn, f32)[:, sl])
        in_maps.append(m)
    nc = build_program(NT, depth=depth)
    r = run_bass_kernel_spmd(nc, in_maps, core_ids=list(range(NCORE))).results
    y_p = np.stack([np.concatenate([r[b * RPS + k]["yp"] for k in range(RPS)], axis=0) for b in range(B)]).astype(f32)
    lastc = [b * RPS + RPS - 1 for b in range(B)]
    y_s = np.concatenate([r[c]["ys"].reshape(4, NS, D).transpose(1, 0, 2) for c in range(NCORE)], axis=0).astype(f32)
    ret_p = np.stack([r[c]["o_ret"] for c in lastc], axis=1)
    k_p = np.stack([r[c]["o_k"].reshape(depth, 128, 2, 64) for c in lastc], axis=1)
    v_p = np.stack([r[c]["o_v"].reshape(depth, 128, 2, 64) for c in lastc], axis=1)
    pool_p = np.stack([r[c]["o_pool"] for c in lastc], axis=1)
    hg_p = np.stack([r[c]["o_hg"] for c in lastc], axis=1)
    ret_s = np.concatenate([r[c]["os_ret"] for c in range(NCORE)], axis=1)
    k_s = np.concatenate([r[c]["os_k"].reshape(depth, NS, 128, 2, 64) for c in range(NCORE)], axis=1)
    v_s = np.concatenate([r[c]["os_v"].reshape(depth, NS, 128, 2, 64) for c in range(NCORE)], axis=1)
    pool_s = np.concatenate([r[c]["os_pool"] for c in range(NCORE)], axis=1)
    hg_s = np.concatenate([r[c]["os_hg"] for c in range(NCORE)], axis=1)
    outs = (y_p, y_s, ret_p, k_p, v_p, pool_p, hg_p, ret_s, k_s, v_s, pool_s, hg_s)
    return tuple(np.ascontiguousarray(o, dtype=f32) for o in outs)
```

```python
import numpy as np
from contextlib import ExitStack
import ml_dtypes
import concourse.bass as bass
import concourse.mybir as mybir
from concourse.bass_utils import run_bass_kernel_spmd

F32 = mybir.dt.float32
BF16 = mybir.dt.bfloat16
AF = mybir.ActivationFunctionType
ALU = mybir.AluOpType


def bc(ap, shape):
    lst = [list(x) for x in ap.ap]
    for n in shape[len(lst):]:
        lst.append([0, n])
    return bass.AP(ap.tensor, ap.offset, lst)


def bcast(ap, axis, n):
    lst = [list(x) for x in ap.ap]
    lst.insert(axis, [0, n])
    return bass.AP(ap.tensor, ap.offset, lst)


class T:
    __slots__ = ("t", "name", "w", "rd", "psum")

    def __init__(self, t, name, psum=False):
        self.t = t
        self.name = name
        self.w = None
        self.rd = {}
        self.psum = psum


class Prog:
    ENG = ("pe", "act", "dve", "pool", "sp")

    def __init__(self, nc, es):
        self.nc = nc
        self.es = es
        self.sem = {k: es.enter_context(nc.semaphore("s_" + k)) for k in self.ENG}
        self.cnt = {k: 0 for k in self.ENG}
        self.seen = {k: {} for k in self.ENG}
        self.items = {k: [] for k in self.ENG}
        self.chan = {}
        self.finals = {}
        self.uid = 0

    def sb(self, name, shape, dt):
        return T(self.es.enter_context(self.nc.sbuf_tensor("sb_" + name, shape, dt)), name)

    def ps(self, name, shape, dt):
        return T(self.es.enter_context(self.nc.psum_tensor("ps_" + name, shape, dt)), name, psum=True)

    def res(self, name):
        return T(None, name)

    def _deps(self, e, rd, wr):
        deps = {}

        def add(k, c):
            if deps.get(k, 0) < c:
                deps[k] = c

        for r in rd:
            if r.w is not None:
                add(*r.w)
            if r.psum:
                for k, c in r.rd.items():
                    if k != e:
                        add(k, c)
        for r in wr:
            if r.w is not None:
                add(*r.w)
            for k, c in r.rd.items():
                add(k, c)
        waits = []
        for k, c in deps.items():
            if k == e and e == "pe":
                continue
            if k in self.chan:
                c = self.chan[k][1]
            if self.seen[e].get(k, 0) < c:
                self.seen[e][k] = c
                waits.append((k, c))
        return waits

    def _semobj(self, k):
        return self.sem[k] if k in self.sem else self.chan[k][0]

    def op(self, e, fn, rd=(), wr=()):
        waits = self._deps(e, rd, wr)
        self.cnt[e] += 1
        c = self.cnt[e]
        self.items[e].append((waits, fn, self.sem[e], 1))
        for r in rd:
            r.rd[e] = c
        for r in wr:
            r.w = (e, c)
            r.rd = {}

    def dma(self, q, out, in_, rd=(), wr=(), ch=None, slow=False, out_final=False):
        if ch is None:
            ch = "d_" + (wr[0].name if len(wr) and wr[0].t is not None else rd[0].name)
        if ch not in self.chan:
            self.chan[ch] = [self.es.enter_context(self.nc.semaphore(ch)), 0]
        if not isinstance(out, bass.AP):
            out = out[tuple(slice(None) for _ in out.shape)]
        if not isinstance(in_, bass.AP):
            in_ = in_[tuple(slice(None) for _ in in_.shape)]
        waits = self._deps(q, rd, wr)
        self.chan[ch][1] += 16
        c = self.chan[ch][1]
        if slow:
            fn = lambda e: e.dma_start(out=out, in_=in_, allow_slow_non_contiguous=True)
        else:
            fn = lambda e: e.dma_start(out=out, in_=in_)
        self.items[q].append((waits, fn, self.chan[ch][0], 16))
        for r in rd:
            r.rd[ch] = c
        for r in wr:
            r.w = (ch, c)
            r.rd = {}
        if out_final:
            self.finals[ch] = c

    def coll(self, fn, rd=(), wr=(), ch="c_coll"):
        if ch not in self.chan:
            self.chan[ch] = [self.es.enter_context(self.nc.semaphore(ch)), 0]
        waits = self._deps("pool", rd, wr)
        self.chan[ch][1] += 1
        c = self.chan[ch][1]
        self.items["pool"].append((waits, fn, self.chan[ch][0], 1))
        for r in rd:
            r.rd[ch] = c
        for r in wr:
            r.w = (ch, c)
            r.rd = {}

    def make_ident(self, ident_bf, ident_f):
        self.op("pool", lambda e: e.memset(ident_f.t[:, :], 1.0), wr=[ident_f])
        self.op("pool", lambda e: e.affine_select(out=ident_f.t[:, :], in_=ident_f.t[:, :], pattern=[[-1, 128]],
                                                  compare_op=ALU.is_equal, fill=0.0, base=0, channel_multiplier=1), rd=[ident_f], wr=[ident_f])
        self.op("dve", lambda e: e.tensor_copy(out=ident_bf.t[:, :], in_=ident_f.t[:, :]), rd=[ident_f], wr=[ident_bf])

    def finish(self):
        fw = [(ch, c) for ch, c in self.finals.items()]
        engs = {"pe": "tensor", "act": "scalar", "dve": "vector", "pool": "gpsimd", "sp": "sync"}
        with self.nc.Block() as block:
            for k in self.ENG:
                items = self.items[k]
                extra = fw if k == "sp" else []

                def body(eng, items=items, extra=extra):
                    for waits, fn, sem, amt in items:
                        for (sk, sc) in waits:
                            eng.wait_ge(self._semobj(sk), sc)
                        ins = fn(eng)
                        ins.then_inc(sem, amt)
                    for (sk, sc) in extra:
                        eng.wait_ge(self._semobj(sk), sc)
                getattr(block, engs[k])(body)


D = 1024
DEPTH = 4
NCORE = 8
SEQ = 8192
PAST = 8192
NS = 16
TS = NS * 4
G = 2
NFM = 68
CPB = 2
NBLK = NFM // CPB
EPS = 1e-6
GAM = [1.0 - 2.0 ** (-5.0 - h) for h in range(4)]
DEBUG_STOP = 9
DEBUG_SUB = 9
DEBUG_DUMP = False

FM_NAMES = [
    ("ka", 0), ("ka", 1), ("kas", 0), ("kas", 1),
    ("fd", 0), ("fd", 1), ("uc", 0), ("uc", 1),
    ("kb", 0), ("kb", 1), ("kbs", 0), ("kbs", 1),
    ("qa", 0), ("qa", 1), ("qas", 0), ("qas", 1),
    ("qb", 0), ("qb", 1), ("qb", 2), ("qb", 3),
    ("qbs", 0), ("qbs", 1), ("qbs", 2), ("qbs", 3),
    ("za", 0), ("za", 1), ("zc", 0), ("zc", 1),
    ("zb", 0), ("zb", 1), ("zb", 2), ("zb", 3),
    ("qd", 0), ("qd", 1), ("zd", 0), ("zd", 1),
]
RES_SPEC = {"ka": (2, BF16), "kas": (2, BF16), "qa": (2, BF16), "qas": (2, BF16),
            "qb": (4, BF16), "qbs": (4, BF16), "kb": (2, BF16), "kbs": (2, BF16),
            "fd": (2, F32), "uc": (2, F32), "za": (2, BF16), "zb": (4, BF16),
            "zc": (2, BF16), "zd": (2, BF16), "qd": (2, BF16)}
SILU = {"za", "zb", "zc", "zd", "qd"}
ROPE_TAB = {"ka": 0, "qa": 0, "kas": 1, "qas": 1, "kb": 2, "qb": 2, "kbs": 3, "qbs": 3}


def _fm_cols():
    off = {}
    o = 0
    for nm, w in [("q_a", 128), ("k_a", 128), ("v_a", 256), ("z_a", 256), ("q_b", 512), ("k_b", 128),
                  ("v_b", 128), ("z_b", 512), ("u_c", 256), ("z_c", 256), ("q_d", 256), ("f_d", 256),
                  ("i_d", 256), ("z_d", 256), ("gl", 4096)]:
        off[nm] = o
        o += w
    cols = np.full(NFM * 128, -1, np.int64)

    def ret_chunk(base, j, swap):
        c = np.full(128, -1, np.int64)
        for hh in range(2):
            h = 2 * j + hh
            for d in range(32):
                sd = (d + 16) % 32 if swap else d
                c[hh * 64 + d] = base + h * 32 + sd
        return c

    def swa_q_chunk(base, j, swap):
        c = np.zeros(128, np.int64)
        for hh in range(2):
            h = 2 * j + hh
            for d in range(64):
                sd = d
                if swap and d < 16:
                    sd = d + 8 if d < 8 else d - 8
                c[hh * 64 + d] = base + h * 64 + sd
        return c

    def swa_k_chunk(base, kv, swap):
        c = np.zeros(128, np.int64)
        for hh in range(2):
            for d in range(64):
                sd = d
                if swap and d < 16:
                    sd = d + 8 if d < 8 else d - 8
                c[hh * 64 + d] = base + kv * 64 + sd
        return c

    for ci, (nm, j) in enumerate(FM_NAMES):
        if nm == "ka":
            c = ret_chunk(off["k_a"], j, False)
        elif nm == "kas":
            c = ret_chunk(off["k_a"], j, True)
        elif nm == "qa":
            c = ret_chunk(off["q_a"], j, False)
        elif nm == "qas":
            c = ret_chunk(off["q_a"], j, True)
        elif nm == "qb":
            c = swa_q_chunk(off["q_b"], j, False)
        elif nm == "qbs":
            c = swa_q_chunk(off["q_b"], j, True)
        elif nm == "kb":
            c = swa_k_chunk(off["k_b"], j, False)
        elif nm == "kbs":
            c = swa_k_chunk(off["k_b"], j, True)
        else:
            src = {"fd": "f_d", "uc": "u_c", "za": "z_a", "zb": "z_b", "zc": "z_c", "zd": "z_d", "qd": "q_d"}[nm]
            c = off[src] + j * 128 + np.arange(128)
        cols[ci * 128:(ci + 1) * 128] = c
    for m in range(8):
        for i in range(4):
            ci = 36 + m * 4 + i
            cols[ci * 128:(ci + 1) * 128] = off["gl"] + i * 1024 + m * 128 + np.arange(128)
    tm = np.concatenate([off["v_a"] + np.arange(256), off["i_d"] + np.arange(256), off["v_b"] + np.arange(128)])
    return cols, tm


def host_tables(NT, seg_start, first_seg, rank=0):
    tb = {}
    def rope_tabs(pos):
        T_ = len(pos)
        pos = pos.astype(np.float32)
        invA = np.power(np.float32(10000.0), -np.arange(16, dtype=np.float32) / np.float32(16))
        invB = np.power(np.float32(500000.0), -np.arange(8, dtype=np.float32) / np.float32(8))
        angA = (pos[None, :] * invA[:, None]).astype(np.float32)
        angB = (pos[None, :] * invB[:, None]).astype(np.float32)
        cA = np.zeros((128, T_), np.float32); sA = np.zeros((128, T_), np.float32)
        cB = np.ones((128, T_), np.float32); sB = np.zeros((128, T_), np.float32)
        for hh in range(2):
            for d in range(32):
                cA[hh * 64 + d] = np.cos(angA[d % 16])
                sA[hh * 64 + d] = (-1.0 if d < 16 else 1.0) * np.sin(angA[d % 16])
            for d in range(16):
                cB[hh * 64 + d] = np.cos(angB[d % 8])
                sB[hh * 64 + d] = (-1.0 if d < 8 else 1.0) * np.sin(angB[d % 8])
        return np.stack([cA, sA, cB, sB])
    tb["ropeP"] = np.stack([rope_tabs(seg_start + i * 128 + np.arange(128)) for i in range(NT)])
    tS = np.arange(TS) // NS
    tb["ropeS"] = rope_tabs(PAST + tS)
    scale = 32.0 ** -0.5
    def ret_tabs(tpos, cid, C):
        T_ = len(tpos)
        gq = np.zeros((128, 2, T_), np.float32); gk = np.zeros((128, 2, T_), np.float32)
        dm = np.zeros((T_, 4, T_), np.float32)
        for h in range(4):
            j, hh = h // 2, h % 2
            gq[hh * 64:(hh + 1) * 64, j, :] = GAM[h] ** (tpos + 1.0)
            gk[hh * 64:(hh + 1) * 64, j, :] = GAM[h] ** (C - 1.0 - tpos) * scale
            rel = tpos[None, :] - tpos[:, None]
            ok = (cid[None, :] == cid[:, None]) & (rel >= 0)
            dm[:, h, :] = np.where(ok, GAM[h] ** np.maximum(rel, 0) * scale, 0.0)
        return gq, gk, dm
    tP = np.arange(128).astype(np.float64)
    tb["gqP"], tb["gkP"], tb["dmP"] = ret_tabs(tP, np.zeros(128), 128)
    tb["gqS"], tb["gkS"], tb["dmS"] = ret_tabs(tS.astype(np.float64), np.arange(TS) % NS, 4)
    decA = np.zeros((128, 4), np.float32)
    for h in range(4):
        j, hh = h // 2, h % 2
        decA[hh * 64:(hh + 1) * 64, j] = GAM[h] ** 128
        decA[hh * 64:(hh + 1) * 64, 2 + j] = GAM[h] ** 4
    tb["decA"] = decA
    cP = np.arange(128) // 16
    tb["mdP"] = ((cP[:, None] == cP[None, :]) & (np.arange(128)[:, None] <= np.arange(128)[None, :])).astype(np.float32)
    sq = np.arange(TS) % NS
    tb["mdS"] = ((sq[:, None] == sq[None, :]) & (tS[:, None] <= tS[None, :])).astype(np.float32)
    tb["cmP"] = (cP[:, None] == np.arange(8)[None, :]).astype(np.float32)
    tb["cmS"] = (sq[:, None] == np.arange(NS)[None, :]).astype(np.float32)
    rs = np.ones((128, 128), np.float32); rs[:, ::16] = 0.0
    tb["resetP"] = rs
    a = np.arange(128)
    tb["mcurP"] = (a[:, None] <= a[None, :]).astype(np.float32)
    tb["mprevP"] = (a[:, None] >= a[None, :]).astype(np.float32)
    tb["mprev0"] = np.zeros((128, 128), np.float32) if first_seg else tb["mprevP"].copy()
    tb["mcurS"] = tb["mdS"].copy()
    tb["mcacheS"] = (a[:, None] >= tS[None, :]).astype(np.float32)
    wins = [2, 4, 8, 16]
    pr = np.zeros((128, 2, 128), np.float32); p0 = np.zeros((128, 2, 128), np.float32)
    for g_ in range(4):
        c, r0 = g_ // 2, (g_ % 2) * 64
        pr[r0:r0 + 64, c, :] = 1.0 / wins[g_]
        cnt = np.minimum(seg_start + np.arange(128) + 1, wins[g_]).astype(np.float32)
        p0[r0:r0 + 64, c, :] = 1.0 / cnt
    tb["pinvR"] = pr
    tb["pinv0"] = p0
    bo = np.zeros((128, 128), np.float32)
    bo[:64, :64] = 1.0 / 64; bo[64:, 64:] = 1.0 / 64
    tb["bones"] = bo
    op = np.zeros((128, 2, 128), np.float32)
    op[:, 0, :64] = 1.0; op[:, 1, 64:] = 1.0
    tb["onespad"] = op
    selp = np.zeros((128, 4), np.float32); sels = np.zeros((128, 4), np.float32)
    if rank > 0:
        selp[:, rank - 1] = 1.0
        sels[:, rank] = 1.0
    cret = np.zeros((128, 2, 4), np.float32)
    L = NT * 128
    for h in range(4):
        j, hh = h // 2, h % 2
        for q in range(rank):
            cret[hh * 64:(hh + 1) * 64, j, q] = GAM[h] ** (L * (rank - 1 - q))
    tb["selp"] = selp; tb["sels"] = sels; tb["cret"] = cret
    return {k: np.ascontiguousarray(v, dtype=np.float32) for k, v in tb.items()}


def build_program(NT, depth=DEPTH, with_sample=True, bmode=True):
    NG = NT // G
    nc = bass.Bass("TRN2", target_bir_lowering=False)

    def din(name, shape, dt=F32):
        return nc.dram_tensor(name, list(shape), dt, kind="ExternalInput").ap()

    def dout(name, shape):
        return nc.dram_tensor(name, list(shape), F32, kind="ExternalOutput").ap()

    xp_d = din("xp", [NT * 128, D]); xs_d = din("xs", [TS, D])
    wfm_d = din("wfm", [depth, D, NFM * 128]); wtm_d = din("wtm", [depth, D, 640])
    wbr_d = din("wbr", [depth, 1280, D]); wout_d = din("wout", [depth, D, D])
    gpre_d = din("gpre", [depth, D]); gpost_d = din("gpost", [depth, D])
    sink_d = din("sink", [depth, 8]); wpool_d = din("wpool", [depth, 4, 64, 64])
    pscale_d = din("pscale", [depth, 256]); ghg_d = din("ghg", [depth, 256]); lbnd_d = din("lbnd", [depth, 256])
    sret_d = din("s_ret", [depth, NS, 4, 32, 64]); sk_d = din("s_k", [depth, NS, 128, 128])
    sv_d = din("s_v", [depth, NS, 128, 128]); spool_d = din("s_pool", [depth, NS, 15, 256])
    shg_d = din("s_hg", [depth, NS, 4, 64, 64])
    tb_shapes = {k: v.shape for k, v in host_tables(NT, 0, True).items()}
    tb_d = {k: din("tb_" + k, list(s)) for k, s in tb_shapes.items()}
    yp_d = dout("yp", [NT * 128, D]); ys_d = dout("ys", [TS, D])
    if DEBUG_DUMP:
        dbg_y = nc.dram_tensor("dbg_y", [128, 10, G * 128], BF16, kind="ExternalOutput").ap()
        dbg_m = nc.dram_tensor("dbg_m", [128, 8, G * 128], BF16, kind="ExternalOutput").ap()
        dbg_h = nc.dram_tensor("dbg_h", [128, 8, G * 128], BF16, kind="ExternalOutput").ap()
        dbg_ys = nc.dram_tensor("dbg_ys", [128, 10, G * 128], BF16, kind="ExternalOutput").ap()
    oret_d = dout("o_ret", [depth, 4, 32, 64]); ok_d = dout("o_k", [depth, 128, 128]); ov_d = dout("o_v", [depth, 128, 128])
    opool_d = dout("o_pool", [depth, 15, 256]); ohg_d = dout("o_hg", [depth, 4, 64, 64])
    osret_d = dout("os_ret", [depth, NS, 4, 32, 64]); osk_d = dout("os_k", [depth, NS, 128, 128])
    osv_d = dout("os_v", [depth, NS, 128, 128]); ospool_d = dout("os_pool", [depth, NS, 15, 256])
    oshg_d = dout("os_hg", [depth, NS, 4, 64, 64])

    wfm_b = nc.dram_tensor("wfm_b", [depth, NBLK, 128, 8, CPB * 128], BF16, kind="Internal").ap()
    wbr_b = nc.dram_tensor("wbr_b", [depth, 8, 128, 10, 128], BF16, kind="Internal").ap()
    XR = 1664
    xin_d = nc.dram_tensor("xchg_in", [XR, 128], F32, kind="Internal").ap()
    xg_d = nc.dram_tensor("xchg_all", [4 * XR, 128], F32, kind="Internal").ap()
    es = ExitStack()
    with es:
        P = Prog(nc, es)
        NTOK = G * 128
        ident = P.sb("ident", [128, 128], BF16); identf = P.sb("identf", [128, 128], F32)
        P.make_ident(ident, identf)
        tbs = {}
        for k in ("gqP", "gkP", "gqS", "gkS", "decA", "resetP", "pinvR", "pinv0", "selp", "sels", "cret"):
            tbs[k] = P.sb("t_" + k, list(tb_shapes[k]), F32)
            P.dma("sp", tbs[k].t, tb_d[k], wr=[tbs[k]])
        for k in ("dmP", "dmS", "mdP", "mdS", "cmP", "cmS", "mcurP", "mprevP", "mprev0", "mcurS", "mcacheS", "bones", "onespad"):
            tbs[k] = P.sb("t_" + k, list(tb_shapes[k]), BF16)
            P.dma("pool", tbs[k].t, tb_d[k], wr=[tbs[k]])
        gpreT = P.sb("gpreT", [128, depth * 8], F32)
        P.dma("sp", gpreT.t, gpre_d.rearrange("l (kc p) -> p (l kc)", p=128), wr=[gpreT], slow=True)
        pscT = P.sb("pscT", [128, depth * 2], F32)
        P.dma("sp", pscT.t, pscale_d.rearrange("l (c p) -> p (l c)", p=128), wr=[pscT], slow=True)
        ghgT = P.sb("ghgT", [128, depth * 2], F32)
        P.dma("sp", ghgT.t, ghg_d.rearrange("l (c p) -> p (l c)", p=128), wr=[ghgT], slow=True)
        lbT = P.sb("lbT", [128, 2, depth], F32)
        for l_ in range(depth):
            P.dma("sp", lbT.t[:, :, l_], lbnd_d[l_].rearrange("(c p) -> p c", p=128), wr=[lbT], slow=True)
        esink = P.sb("esink", [128, depth * 4], F32)
        for hh in range(2):
            src = bass.AP(sink_d.tensor, sink_d.offset + hh, [[0, 64], [2, depth * 4]])
            P.dma("sp", esink.t[hh * 64:(hh + 1) * 64, :], src, wr=[esink], slow=True)
        P.op("act", lambda e: e.activation(out=esink.t[:, :], in_=esink.t[:, :], func=AF.Exp), rd=[esink], wr=[esink])
        lbm = P.sb("lbm", [128, 2], F32); lbe = P.sb("lbe", [128, 2, depth], F32); lbs = P.sb("lbs", [128, 2], F32)
        lbc = P.sb("lbc", [128, 2, depth], F32); omlb = P.sb("omlb", [128, 2, depth], F32)
        P.op("dve", lambda e: e.tensor_reduce(out=lbm.t[:, :], in_=lbT.t[:, :, :], axis=mybir.AxisListType.X, op=ALU.max), rd=[lbT], wr=[lbm])
        P.op("dve", lambda e: e.tensor_tensor(out=lbe.t[:, :, :], in0=lbT.t[:, :, :], in1=bc(lbm.t[:, :], [128, 2, depth]), op=ALU.subtract), rd=[lbT, lbm], wr=[lbe])
        P.op("act", lambda e: e.activation(out=lbe.t[:, :, :], in_=lbe.t[:, :, :], func=AF.Exp), rd=[lbe], wr=[lbe])
        P.op("dve", lambda e: e.tensor_reduce(out=lbs.t[:, :], in_=lbe.t[:, :, :], axis=mybir.AxisListType.X, op=ALU.add), rd=[lbe], wr=[lbs])
        P.op("dve", lambda e: e.reciprocal(out=lbs.t[:, :], in_=lbs.t[:, :]), rd=[lbs], wr=[lbs])
        P.op("dve", lambda e: e.tensor_tensor(out=lbe.t[:, :, :], in0=lbe.t[:, :, :], in1=bc(lbs.t[:, :], [128, 2, depth]), op=ALU.mult), rd=[lbe, lbs], wr=[lbe])
        P.op("dve", lambda e: e.memset(lbc.t[:, :, :], 0.0), wr=[lbc])
        for l in range(1, depth):
            P.op("dve", lambda e, l=l: e.tensor_tensor(out=lbc.t[:, :, l], in0=lbc.t[:, :, l - 1], in1=lbe.t[:, :, l], op=ALU.add), rd=[lbc, lbe], wr=[lbc])
        P.op("dve", lambda e: e.tensor_scalar(out=omlb.t[:, :, :], in0=lbc.t[:, :, :], scalar1=-1.0, scalar2=1.0, op0=ALU.mult, op1=ALU.add), rd=[lbc], wr=[omlb])
        wpbd = P.sb("wpbd", [128, depth * 2, 128], BF16)
        P.op("pool", lambda e: e.memset(wpbd.t[:, :, :], 0.0), wr=[wpbd])
        for l in range(depth):
            for g_ in range(4):
                c, r0 = g_ // 2, (g_ % 2) * 64
                P.dma("pool", wpbd.t[r0:r0 + 64, l * 2 + c, r0:r0 + 64], wpool_d[l, g_], wr=[wpbd])

        wtm = P.sb("wtm", [128, 8, 640], BF16)
        wbrk = [P.sb("wbrk%d" % i, [128, 10, 128], BF16) for i in range(2)]
        wbst = {"n": 0}
        wbres = [P.res("wbrb%d" % l_) for l_ in range(depth)]
        wout = P.sb("wout", [128, 8, D], BF16)
        gpost = P.sb("gpost", [128, D], F32)
        NWB = 4
        wblk = [P.sb("wblk%d" % i, [128, 8, CPB * 128], BF16) for i in range(NWB)]
        wstate = {"n": 0}
        NB1 = 12 // CPB
        wres = [[P.res("wfmb%d_%d" % (l_, k_)) for k_ in range(2)] for l_ in range(depth)]

        def convert_layer(l):
            for blk in range(NBLK):
                P.dma("pool", wfm_b[l, blk], wfm_d[l, :, blk * CPB * 128:(blk + 1) * CPB * 128].rearrange("(kc p) c -> p kc c", p=128),
                      rd=[], wr=[wres[l][int(blk >= NB1)]], ch="d_cvt%d_%d" % (l, int(blk >= NB1)))

        def convert_wbr(l):
            for m in range(8):
                P.dma("pool", wbr_b[l, m], wbr_d[l, :, m * 128:(m + 1) * 128].rearrange("(kc p) c -> p kc c", p=128), rd=[], wr=[wbres[l]], ch="d_cvb%d" % l)

        def load_block(l, blk):
            wb = wblk[wstate["n"] % NWB]
            wstate["n"] += 1
            P.dma("sp", wb.t[:, :, :], wfm_b[l, blk], rd=[wres[l][int(blk >= NB1)]], wr=[wb])
            return wb

        xt = [P.sb("xt%d" % i, [128, D], F32) for i in range(G)]
        xsn = P.sb("xsn", [128, D], BF16)
        ss = P.sb("ss", [128, 4], F32); rstd = P.sb("rstd", [128, 1], F32)
        hT = P.sb("hT", [128, 8, NTOK], BF16)
        res = {k: P.sb("r_" + k, [128, n, NTOK], dt) for k, (n, dt) in RES_SPEC.items()}
        yT = P.sb("yT", [128, 10, NTOK], BF16)
        mT = P.sb("mT", [128, 8, NTOK], BF16)
        gsb = [P.sb("gsb%d" % i, [128, NTOK], BF16) for i in range(2)]
        macc = P.sb("macc", [128, NTOK], F32); mtmp = P.sb("mtmp", [128, NTOK], F32)
        xo = [P.sb("xo%d" % i, [128, D], F32) for i in range(1)]
        pj = [P.ps("pj%d" % i, [128, 512], F32) for i in range(3)]
        pjs = {"n": 0}

        def getpj():
            p = pj[pjs["n"] % 3]
            pjs["n"] += 1
            return p
        scp = P.ps("scp", [128, 512], F32)
        otp = P.ps("otp", [128, 512], F32)
        kvp = [P.ps("kvp%d" % i, [128, 512], F32) for i in range(2)]
        tpp = P.ps("tpp", [128, 8, 128], BF16)
        rtabG = P.sb("rtabG", [128, 4, G * 128], F32)
        qrA = P.sb("qrA", [128, 2, 128], BF16); qdA = P.sb("qdA", [128, 2, 128], BF16)
        krA = P.sb("krA", [128, 2, 128], BF16); kdA = P.sb("kdA", [128, 2, 128], BF16)
        kdTok = P.sb("kdTok", [128, 2, 128], BF16)
        vtok = P.sb("vtok", [128, 640], BF16)
        vpad = P.sb("vpad", [128, 8, 128], BF16)
        P.op("pool", lambda e: e.memset(vpad.t[:, :, :], 0.0), wr=[vpad])
        scm = P.sb("scm", [128, 4, 128], BF16)
        osq = P.sb("osq", [128, 256], BF16); rsn = P.sb("rsn", [128, 256], F32); ytmp = P.sb("ytmp", [128, 256], F32)
        SA = P.sb("SA", [128, 2, 128], F32); SAb = P.sb("SAb", [128, 2, 128], BF16)
        hsg = P.sb("hsg", [128, 2, 128], F32); hf = P.sb("hf", [128, 2, 128], F32); hb = P.sb("hb", [128, 2, 128], F32)
        heb = P.sb("heb", [128, 2, 128], F32); henb = P.sb("henb", [128, 2, 128], F32)
        qdd = P.sb("qdd", [128, 2, 128], BF16); ktf = P.sb("ktf", [128, 2, 128], F32); ktb = P.sb("ktb", [128, 2, 128], BF16)
        decD = P.sb("decD", [128, 2, 16], F32); kdd = P.sb("kdd", [128, 2, 128], BF16)
        vexp = P.sb("vexp", [128, 16, 128], BF16)
        SD = P.sb("SD", [128, 2, 9, 128], F32); SDb = P.sb("SDb", [128, 2, 8, 128], BF16)
        SDc = [[T(SD.t, "SDc%d%d" % (j_, h_)) for h_ in range(2)] for j_ in range(2)]
        SDall = [SD, SDc[0][0], SDc[0][1], SDc[1][0], SDc[1][1]]
        qrB = P.sb("qrB", [128, 4, 128], BF16)
        krB = [P.sb("krB%d" % i, [128, 2, 128], BF16) for i in range(2)]
        vpB = [P.sb("vpB%d" % i, [128, 2, 2, 128], BF16) for i in range(2)]
        for v_ in vpB:
            P.op("pool", lambda e, v_=v_: e.memset(v_.t[:, :, :, :], 0.0), wr=[v_])
        esbs = [P.sb("esb%d" % i, [128, 4, 128], BF16) for i in range(2)]
        rden = P.sb("rden", [128, 128], F32); obt = P.sb("obt", [128, 128], F32)
        Eb = [P.sb("Eb%d" % i, [128, 2, 144], F32) for i in range(2)]
        s2 = P.sb("s2", [128, 2, 143], F32); s4 = P.sb("s4", [128, 2, 141], F32)
        s8 = P.sb("s8", [128, 137], F32); s16 = P.sb("s16", [128, 129], F32)
        plf = P.sb("plf", [128, 2, 128], F32); plb = P.sb("plb", [128, 2, 128], BF16)
        stg = P.sb("stg", [128, 256], F32)
        for t_ in (SA, SAb, SD, SDb, Eb[0], Eb[1], krB[0], krB[1]):
            nd = len(t_.t.shape)
            P.op("pool", lambda e, t_=t_: e.memset(t_.t[tuple([slice(None)] * len(t_.t.shape))], 0.0), wr=[t_])

        def act_copy(out, in_, rd, wr, func=AF.Copy, **kw):
            P.op("act", lambda e: e.activation(out=out, in_=in_, func=func, **kw), rd=rd, wr=wr)

        def dve_tt(out, in0, in1, op, rd, wr):
            P.op("dve", lambda e: e.tensor_tensor(out=out, in0=in0, in1=in1, op=op), rd=rd, wr=wr)

        def pool_tt(out, in0, in1, op, rd, wr):
            P.op("pool", lambda e: e.tensor_tensor(out=out, in0=in0, in1=in1, op=op), rd=rd, wr=wr)

        def rsqrt_ln_exp(out, in_, rd_t, wr_t):
            P.op("act", lambda e: e.activation(out=out, in_=in_, func=AF.Ln, bias=epsc.t[:in_.shape[0], 0:1]), rd=[rd_t, epsc], wr=[wr_t])
            P.op("act", lambda e: e.activation(out=out, in_=out, func=AF.Exp, scale=-0.5), rd=[wr_t], wr=[wr_t])
        epsc = P.sb("epsc", [128, 1], F32)
        P.op("dve", lambda e: e.memset(epsc.t[:, :], EPS), wr=[epsc])

        def prenorm(l, xtile, T_, col0):
            P.op("act", lambda e: e.activation(out=xsn.t[:T_, :], in_=xtile.t[:T_, :], func=AF.Square, accum_out=ss.t[:T_, 0:1]), rd=[xtile], wr=[xsn, ss])
            P.op("dve", lambda e: e.tensor_scalar(out=rstd.t[:T_, :], in0=ss.t[:T_, 0:1], scalar1=1.0 / D, scalar2=None, op0=ALU.mult), rd=[ss], wr=[rstd])
            rsqrt_ln_exp(rstd.t[:T_, :], rstd.t[:T_, :], rstd, rstd)
            P.op("act", lambda e: e.activation(out=xsn.t[:T_, :], in_=xtile.t[:T_, :], func=AF.Copy, scale=rstd.t[:T_, 0:1]), rd=[xtile, rstd], wr=[xsn])

            def tr(pe):
                for kc in range(8):
                    ins = pe.transpose(tpp.t[:, kc, :T_], xsn.t[:T_, kc * 128:(kc + 1) * 128], ident.t[:T_, :T_])
                return ins
            P.op("pe", tr, rd=[xsn, ident], wr=[tpp])
            dve_tt(hT.t[:, :, col0:col0 + T_], tpp.t[:, :, :T_], bc(gpreT.t[:, l * 8:(l + 1) * 8], [128, 8, T_]), ALU.mult, [tpp, gpreT], [hT])

        def fm_blocks(l, blks, NTK):
            evn = {"n": 0}
            for blk in blks:
                wb = load_block(l, blk)
                for c in range(CPB):
                    ci = blk * CPB + c
                    nm, j = FM_NAMES[ci]
                    ps_ = getpj()

                    def grp(pe, wb=wb, c=c, ps_=ps_):
                        for kc in range(8):
                            ins = pe.matmul(ps_.t[:, :NTK], lhsT=wb.t[:, kc, c * 128:(c + 1) * 128], rhs=hT.t[:, kc, :NTK], start=(kc == 0), stop=(kc == 7))
                        return ins
                    P.op("pe", grp, rd=[wb, hT], wr=[ps_])
                    dst = res[nm]
                    if nm in ROPE_TAB:
                        tab = rtabG.t[:, ROPE_TAB[nm], :NTK]
                        P.op("dve", lambda e, dst=dst, j=j, ps_=ps_, tab=tab: e.tensor_tensor(out=dst.t[:, j, :NTK], in0=ps_.t[:, :NTK], in1=tab, op=ALU.mult), rd=[ps_, rtabG], wr=[dst])
                    elif nm in SILU:
                        act_copy(dst.t[:, j, :NTK], ps_.t[:, :NTK], [ps_], [dst], func=AF.Silu)
                    else:
                        act_copy(dst.t[:, j, :NTK], ps_.t[:, :NTK], [ps_], [dst])

        def load_layer_weights(l):
            P.dma("pool", wtm.t[:, :, :], wtm_d[l].rearrange("(kc p) c -> p kc c", p=128), wr=[wtm])
            P.dma("pool", wout.t[:, :, :], wout_d[l].rearrange("(kc p) c -> p kc c", p=128), wr=[wout])
            src = bass.AP(gpost_d.tensor, gpost_d.offset + l * D, [[0, 128], [1, D]])
            P.dma("sp", gpost.t[:, :], src, wr=[gpost])

        st = {"par": 0}

        def mixers(l, mode, cs0, T_, tile_idx, ropesrc, last):
            cs = slice(cs0, cs0 + T_)
            Pm = mode == "P"
            par = st["par"]
            st["par"] ^= 1
            pa = getpj(); pb = getpj()

            def tmg(pe):
                for kc in range(8):
                    pe.matmul(pa.t[:T_, :512], lhsT=hT.t[:, kc, cs], rhs=wtm.t[:, kc, 0:512], start=(kc == 0), stop=(kc == 7))
                for kc in range(8):
                    ins = pe.matmul(pb.t[:T_, :128], lhsT=hT.t[:, kc, cs], rhs=wtm.t[:, kc, 512:640], start=(kc == 0), stop=(kc == 7))
                return ins
            P.op("pe", tmg, rd=[hT, wtm], wr=[pa, pb])
            if DEBUG_SUB <= -4:
                return
            act_copy(vtok.t[:T_, 0:512], pa.t[:T_, :512], [pa], [vtok])
            act_copy(vtok.t[:T_, 512:640], pb.t[:T_, :128], [pb], [vtok])
            if DEBUG_SUB <= -3:
                return
            for hh in range(2):
                src = pa.t[:T_, :512].rearrange("p (a b d) -> p a b d", a=4, b=2, d=64)[:, :, hh, :]
                dstv = vpad.t[:T_, :, :].rearrange("p (a b) c -> p a b c", b=2)[:, :, hh, hh * 64:(hh + 1) * 64]
                P.op("dve", lambda e, src=src, dstv=dstv: e.tensor_copy(out=dstv, in_=src), rd=[pa], wr=[vpad])
            if DEBUG_SUB <= -2:
                return
            vb = vpB[par]
            for hh in range(2):
                srcb = pb.t[:T_, :128].rearrange("p (k d) -> p k d", k=2)
                P.op("dve", lambda e, hh=hh, srcb=srcb: e.tensor_copy(out=vb.t[:T_, :, hh, hh * 64:(hh + 1) * 64], in_=srcb), rd=[pb], wr=[vb])
            if DEBUG_SUB <= -1:
                return
            if DEBUG_SUB <= 0:
                return
            gq, gk, dm = (tbs["gqP"], tbs["gkP"], tbs["dmP"]) if Pm else (tbs["gqS"], tbs["gkS"], tbs["dmS"])
            for (xa, xsw, outr, gtab, outd) in ((res["qa"], res["qas"], qrA, gq, qdA), (res["ka"], res["kas"], krA, gk, kdA)):
                dve_tt(outr.t[:, :, :T_], xa.t[:, :, cs], xsw.t[:, :, cs], ALU.add, [xa, xsw], [outr])
                dve_tt(outd.t[:, :, :T_], outr.t[:, :, :T_], gtab.t[:, :, :T_], ALU.mult, [outr, gtab], [outd])

            if DEBUG_SUB <= 0.1:
                return

            def trk(pe, src=kdA):
                for j in range(2):
                    ins = pe.transpose(tpp.t[:T_, j, :], src.t[:, j, :T_], ident.t[:, :])
                return ins
            P.op("pe", trk, rd=[kdA, ident], wr=[tpp])
            act_copy(kdTok.t[:T_, :, :], tpp.t[:T_, 0:2, :], [tpp], [kdTok])

            if DEBUG_SUB <= 0.2:
                return

            scqA = getpj()

            def scA(pe):
                for h in range(4):
                    j, hh = h // 2, h % 2
                    bank = scp if hh == 0 else scqA
                    ins = pe.matmul(bank.t[:T_, j * 128:j * 128 + T_], lhsT=krA.t[hh * 64:(hh + 1) * 64, j, :T_], rhs=qrA.t[hh * 64:(hh + 1) * 64, j, :T_], start=True, stop=True)
                return ins
            P.op("pe", scA, rd=[krA, qrA], wr=[scp, scqA])
            for hh, bank in enumerate((scp, scqA)):
                dve_tt(scm.t[:T_, hh::2, :T_], bank.t[:T_, 0:256].rearrange("p (j t) -> p j t", j=2)[:, :, :T_], dm.t[:T_, hh::2, :T_], ALU.mult, [bank, dm], [scm])

            if DEBUG_SUB <= 0.3:
                return
            for j in range(2):
                if not Pm:
                    load_sample_state(l, "A", j)

                def oA(pe, j=j):
                    o_ = otp.t[:, j * 128:j * 128 + T_]
                    for hh in range(2):
                        pe.matmul(o_, lhsT=vpad.t[:T_, 2 * j + hh, :], rhs=scm.t[:T_, 2 * j + hh, :T_], start=(hh == 0), stop=False)
                    if Pm:
                        ins = pe.matmul(o_, lhsT=SAb.t[:, j, :], rhs=qdA.t[:, j, :T_], start=False, stop=True)
                    else:
                        for s_ in range(NS):
                            ins = pe.matmul(otp.t[:, j * 128 + s_:j * 128 + T_:NS], lhsT=S0b.t[:, s_, :], rhs=qdA.t[:, j, s_:T_:NS], start=False, stop=(s_ == NS - 1))
                    return ins
                P.op("pe", oA, rd=[SAb if Pm else S0b, qdA, vpad, scm], wr=[otp])
                if not Pm:
                    sample_state_update(l, "A", kdTok, 0, T_, j)
            if DEBUG_SUB <= 0.4:
                return
            gnorm(l, cs, T_, 0, res["za"], None)
            if DEBUG_SUB <= 0.5:
                return
            if Pm:
                def kvA(pe):
                    for j in range(2):
                        ins = pe.matmul(kvp[0].t[:, j * 128:(j + 1) * 128], lhsT=kdTok.t[:T_, j, :], rhs=vtok.t[:T_, j * 128:(j + 1) * 128], start=True, stop=True)
                    return ins
                P.op("pe", kvA, rd=[kdTok, vtok], wr=[kvp[0]])
                for j in range(2):
                    for hh in range(2):
                        r_ = slice(hh * 64, (hh + 1) * 64)
                        P.op("dve", lambda e, j=j, r_=r_: e.scalar_tensor_tensor(out=SA.t[r_, j, r_], in0=SA.t[r_, j, r_], scalar=tbs["decA"].t[r_, j:j + 1], in1=kvp[0].t[r_, j * 128 + r_.start:j * 128 + r_.stop], op0=ALU.mult, op1=ALU.add), rd=[SA, tbs["decA"], kvp[0]], wr=[SA])
                P.op("dve", lambda e: e.tensor_copy(out=SAb.t[:, :, :], in_=SA.t[:, :, :]), rd=[SA], wr=[SAb])
                if last:
                    for h in range(4):
                        j, hh = h // 2, h % 2
                        P.dma("sp", oret_d[l, h], SA.t[hh * 64:hh * 64 + 32, j, hh * 64:(hh + 1) * 64], rd=[SA], out_final=True)
            if DEBUG_SUB <= 1:
                return
            fd = res["fd"]
            act_copy(hsg.t[:, :, :T_], fd.t[:, :, cs], [fd], [hsg], func=AF.Sigmoid)
            for j in range(2):
                P.op("dve", lambda e, j=j: e.tensor_scalar(out=hf.t[:, j, :T_], in0=hsg.t[:, j, :T_], scalar1=omlb.t[:, j, l:l + 1], scalar2=lbc.t[:, j, l:l + 1], op0=ALU.mult, op1=ALU.add), rd=[hsg, omlb, lbc], wr=[hf])
            act_copy(hsg.t[:, :, :T_], hf.t[:, :, :T_], [hf], [hsg], func=AF.Ln)
            if Pm:
                for j in range(2):
                    P.op("dve", lambda e, j=j: e.tensor_tensor_scan(out=hb.t[:, j, :], data0=tbs["resetP"].t[:, :], data1=hsg.t[:, j, :], initial=0.0, op0=ALU.mult, op1=ALU.add), rd=[hsg, tbs["resetP"]], wr=[hb])
                nch, C = 8, 16
            else:
                P.op("dve", lambda e: e.tensor_copy(out=hb.t[:, :, 0:NS], in_=hsg.t[:, :, 0:NS]), rd=[hsg], wr=[hb])
                for t_ in range(1, 4):
                    dve_tt(hb.t[:, :, t_ * NS:(t_ + 1) * NS], hb.t[:, :, (t_ - 1) * NS:t_ * NS], hsg.t[:, :, t_ * NS:(t_ + 1) * NS], ALU.add, [hb, hsg], [hb])
                nch, C = NS, 4
            act_copy(heb.t[:, :, :T_], hb.t[:, :, :T_], [hb], [heb], func=AF.Exp)
            act_copy(henb.t[:, :, :T_], hb.t[:, :, :T_], [hb], [henb], func=AF.Exp, scale=-1.0)
            dve_tt(qdd.t[:, :, :T_], res["qd"].t[:, :, cs], heb.t[:, :, :T_], ALU.mult, [res["qd"], heb], [qdd])
            P.op("dve", lambda e: e.tensor_scalar(out=hf.t[:, :, :T_], in0=hf.t[:, :, :T_], scalar1=-1.0, scalar2=1.0, op0=ALU.mult, op1=ALU.add), rd=[hf], wr=[hf])
            dve_tt(ktf.t[:, :, :T_], hf.t[:, :, :T_], henb.t[:, :, :T_], ALU.mult, [hf, henb], [ktf])
            act_copy(ktb.t[:, :, :T_], ktf.t[:, :, :T_], [ktf], [ktb])
            if Pm:
                lastv = heb.t[:, :, :].rearrange("p j (c k) -> p j c k", k=16)[:, :, :, 15]
                P.op("dve", lambda e: e.tensor_copy(out=decD.t[:, :, 0:8], in_=lastv), rd=[heb], wr=[decD])
                dve_tt(kdd.t[:, :, :].rearrange("p j (c k) -> p j c k", k=16), ktf.t[:, :, :].rearrange("p j (c k) -> p j c k", k=16), bc(decD.t[:, :, 0:8], [128, 2, 8, 16]), ALU.mult, [ktf, decD], [kdd])
            else:
                P.op("dve", lambda e: e.tensor_copy(out=decD.t[:, :, 0:NS], in_=heb.t[:, :, 3 * NS:4 * NS]), rd=[heb], wr=[decD])
                dve_tt(kdd.t[:, :, :T_].rearrange("p j (k c) -> p j k c", c=NS), ktf.t[:, :, :T_].rearrange("p j (k c) -> p j k c", c=NS), bcast(decD.t[:, :, 0:NS], 2, 4), ALU.mult, [ktf, decD], [kdd])
            P.op("pe", lambda pe: trk(pe, kdd), rd=[kdd, ident], wr=[tpp])
            act_copy(kdTok.t[:T_, :, :], tpp.t[:T_, 0:2, :], [tpp], [kdTok])

            scqD = getpj()

            def scD(pe):
                for h in range(4):
                    j, hh = h // 2, h % 2
                    bank = scp if hh == 0 else scqD
                    ins = pe.matmul(bank.t[:T_, j * 128:j * 128 + T_], lhsT=ktb.t[hh * 64:(hh + 1) * 64, j, :T_], rhs=qdd.t[hh * 64:(hh + 1) * 64, j, :T_], start=True, stop=True)
                return ins
            P.op("pe", scD, rd=[ktb, qdd], wr=[scp, scqD])
            md = tbs["mdP"] if Pm else tbs["mdS"]
            for hh, bank in enumerate((scp, scqD)):
                dve_tt(scm.t[:T_, hh::2, :T_], bank.t[:T_, 0:256].rearrange("p (j t) -> p j t", j=2)[:, :, :T_], bcast(md.t[:T_, :T_], 1, 2), ALU.mult, [bank, md], [scm])
            if Pm:
                cm = tbs["cmP"]
                for j in range(2):
                    P.op("dve", lambda e, j=j: e.tensor_tensor(out=vexp.t[:, 0:8, :], in0=bcast(vtok.t[:, 256 + j * 128:256 + (j + 1) * 128], 1, 8), in1=bc(cm.t[:, :], [128, 8, 128]), op=ALU.mult), rd=[vtok, cm], wr=[vexp])

                    def kvD(pe, j=j):
                        for q in range(2):
                            ins = pe.matmul(kvp[q].t[:, :], lhsT=kdTok.t[:, j, :], rhs=vexp.t[:, q * 4:(q + 1) * 4, :], start=True, stop=True)
                        return ins
                    P.op("pe", kvD, rd=[kdTok, vexp], wr=[kvp[0], kvp[1]])
                    for c in range(8):
                        for hh in range(2):
                            r_ = slice(hh * 64, (hh + 1) * 64)
                            P.op("dve", lambda e, j=j, c=c, r_=r_: e.scalar_tensor_tensor(out=SD.t[r_, j, c + 1, r_], in0=SD.t[r_, j, c, r_], scalar=decD.t[r_, j, c:c + 1], in1=kvp[c // 4].t[r_, (c % 4) * 128 + r_.start:(c % 4) * 128 + r_.stop], op0=ALU.mult, op1=ALU.add), rd=[SDc[j][hh], decD, kvp[c // 4]], wr=[SDc[j][hh]])
                act_copy(SDb.t[:, :, :, :], SD.t[:, :, 0:8, :], SDall, [SDb])

            for j in range(2):
                if not Pm:
                    load_sample_state(l, "D", j)

                def oD(pe, j=j):
                    o_ = otp.t[:, j * 128:j * 128 + T_]
                    for hh in range(2):
                        pe.matmul(o_, lhsT=vpad.t[:T_, 4 + 2 * j + hh, :], rhs=scm.t[:T_, 2 * j + hh, :T_], start=(hh == 0), stop=False)
                    for c in range(nch):
                        if Pm:
                            ins = pe.matmul(otp.t[:, j * 128 + c * 16:j * 128 + (c + 1) * 16], lhsT=SDb.t[:, j, c, :], rhs=qdd.t[:, j, c * 16:(c + 1) * 16], start=False, stop=(c == nch - 1))
                        else:
                            ins = pe.matmul(otp.t[:, j * 128 + c:j * 128 + T_:NS], lhsT=S0b.t[:, c, :], rhs=qdd.t[:, j, c:T_:NS], start=False, stop=(c == nch - 1))
                    return ins
                P.op("pe", oD, rd=[SDb if Pm else S0b, qdd, vpad, scm], wr=[otp])
                if not Pm:
                    sample_state_update(l, "D", kdTok, 256, T_, j)
            gnorm(l, cs, T_, 8, res["zd"], ghgT)
            if Pm:
                P.op("dve", lambda e: e.tensor_copy(out=SD.t[:, :, 0, :], in_=SD.t[:, :, 8, :]), rd=SDall, wr=SDall)
                if last:
                    for h in range(4):
                        j, hh = h // 2, h % 2
                        P.dma("sp", ohg_d[l, h], SD.t[hh * 64:(hh + 1) * 64, j, 0, hh * 64:(hh + 1) * 64], rd=SDall, out_final=True, ch="d_SD")
            if DEBUG_SUB <= 2:
                return
            kcur = krB[par]; kprev = krB[par ^ 1]; vprev = vpB[par ^ 1]
            dve_tt(qrB.t[:, :, :T_], res["qb"].t[:, :, cs], res["qbs"].t[:, :, cs], ALU.add, [res["qb"], res["qbs"]], [qrB])
            dve_tt(kcur.t[:, :, :T_], res["kb"].t[:, :, cs], res["kbs"].t[:, :, cs], ALU.add, [res["kb"], res["kbs"]], [kcur])
            if Pm:
                mprev = tbs["mprev0"] if tile_idx == 0 else tbs["mprevP"]
                mcur = tbs["mcurP"]
                for jq in range(4):
                    kv = jq // 2
                    sp_ = getpj(); sq_ = getpj(); op_ = kvp[jq % 2]
                    esb = esbs[jq % 2]

                    def scB(pe, jq=jq, kv=kv, sp_=sp_, sq_=sq_):
                        for hh, bank in enumerate((sp_, sq_)):
                            r_ = slice(hh * 64, (hh + 1) * 64)
                            for b_, kt_ in enumerate((kprev, kcur)):
                                ins = pe.matmul(bank.t[:, b_ * 128:(b_ + 1) * 128], lhsT=kt_.t[r_, kv, :], rhs=qrB.t[r_, jq, :], start=True, stop=True)
                        return ins
                    P.op("pe", scB, rd=[kprev, kcur, qrB], wr=[sp_, sq_])
                    for hh, bank in enumerate((sp_, sq_)):
                        act_copy(esb.t[:, 2 * hh:2 * hh + 2, :].rearrange("p a t -> p (a t)"), bank.t[:, 0:256], [bank], [esb], func=AF.Exp, scale=0.125)
                    for b_, mk in enumerate((mprev, mcur)):
                        P.op("pool", lambda e, b_=b_, mk=mk, esb=esb: e.tensor_tensor(out=esb.t[:, b_::2, :], in0=esb.t[:, b_::2, :], in1=bcast(mk.t[:, :], 1, 2), op=ALU.mult), rd=[esb, mk], wr=[esb])

                    def pvB(pe, jq=jq, kv=kv, op_=op_, esb=esb):
                        n = 0
                        for hh in range(2):
                            for b_, vv in enumerate((vprev, vb)):
                                pe.matmul(op_.t[:, 0:128], lhsT=vv.t[:, kv, hh, :], rhs=esb.t[:, hh * 2 + b_, :], start=(n == 0), stop=(n == 3))
                                n += 1
                        n = 0
                        for hh in range(2):
                            for b_ in range(2):
                                ins = pe.matmul(op_.t[:, 128:256], lhsT=tbs["onespad"].t[:, hh, :], rhs=esb.t[:, hh * 2 + b_, :], start=(n == 0), stop=(n == 3))
                                n += 1
                        return ins
                    P.op("pe", pvB, rd=[vprev, vb, esb, tbs["onespad"]], wr=[op_])
                    swa_finish(l, jq, op_, cs, T_)
                if last:
                    def trkb(pe):
                        for kv in range(2):
                            ins = pe.transpose(tpp.t[:, kv, :], kcur.t[:, kv, :], ident.t[:, :])
                        return ins
                    P.op("pe", trkb, rd=[kcur, ident], wr=[tpp])
                    P.op("dve", lambda e: e.tensor_copy(out=stg.t[:, 0:128].rearrange("p (k d) -> p k d", k=2), in_=tpp.t[:, 0:2, 0:64]), rd=[tpp], wr=[stg])
                    P.dma("sp", ok_d[l], stg.t[:, 0:128], rd=[stg], out_final=True)
                    P.op("dve", lambda e: e.tensor_copy(out=stg.t[:, 128:256], in_=vtok.t[:, 512:640]), rd=[vtok], wr=[stg])
                    P.dma("sp", ov_d[l], stg.t[:, 128:256], rd=[stg], out_final=True)
            else:
                sample_swa(l, kcur, vb, cs, T_)
            if DEBUG_SUB <= 3:
                return
            uc = res["uc"]
            Ec = Eb[par]; Ep = Eb[par ^ 1]
            P.op("pool", lambda e: e.tensor_copy(out=Ec.t[:, :, 16:16 + T_], in_=uc.t[:, :, cs]), rd=[uc], wr=[Ec])
            if Pm:
                P.op("pool", lambda e: e.tensor_copy(out=Ec.t[:, :, 0:16], in_=Ep.t[:, :, 128:144]), rd=[Ep], wr=[Ec])
                W = 16 + 128
                pooling(l, Ec, W, T_, cs, tbs["pinv0"] if tile_idx == 0 else tbs["pinvR"])
                if last:
                    for c in range(2):
                        P.op("pe", lambda pe, c=c: pe.transpose(otp.t[:, c * 128:(c + 1) * 128], Ec.t[:, c, 16:144], identf.t[:, :]), rd=[Ec, identf], wr=[otp])
                    P.op("dve", lambda e: e.tensor_copy(out=stg.t[:, :], in_=otp.t[:, 0:256]), rd=[otp], wr=[stg])
                    P.dma("sp", opool_d[l], stg.t[113:128, :], rd=[stg], out_final=True)
            else:
                sample_pool(l, Ec, cs, T_)

        def gnorm(l, cs, T_, y0, ztile, gaincol):
            o3 = otp.t[:, 0:256].rearrange("p (j t) -> p j t", j=2)[:, :, :T_]
            P.op("act", lambda e: e.activation(out=osq.t[:, :].rearrange("p (j t) -> p j t", j=2)[:, :, :T_], in_=o3, func=AF.Square), rd=[otp], wr=[osq])
            P.op("pe", lambda pe: pe.matmul(otp.t[:, 256:512], lhsT=tbs["bones"].t[:, :], rhs=osq.t[:, :], start=True, stop=True), rd=[osq, tbs["bones"]], wr=[otp])
            rsqrt_ln_exp(rsn.t[:, :], otp.t[:, 256:512], otp, rsn)
            r3 = rsn.t[:, :].rearrange("p (j t) -> p j t", j=2)[:, :, :T_]
            y3 = ytmp.t[:, :].rearrange("p (j t) -> p j t", j=2)[:, :, :T_]
            dve_tt(y3, o3, r3, ALU.mult, [otp, rsn], [ytmp])
            if gaincol is None:
                dve_tt(yT.t[:, y0:y0 + 2, cs], y3, ztile.t[:, :, cs], ALU.mult, [ytmp, ztile], [yT])
            else:
                for j in range(2):
                    P.op("dve", lambda e, j=j: e.scalar_tensor_tensor(out=yT.t[:, y0 + j, cs], in0=ytmp.t[:, j * 128:j * 128 + T_], scalar=gaincol.t[:, l * 2 + j:l * 2 + j + 1], in1=ztile.t[:, j, cs], op0=ALU.mult, op1=ALU.mult), rd=[ytmp, gaincol, ztile], wr=[yT])

        def swa_finish(l, jq, op_, cs, T_):
            P.op("dve", lambda e: e.tensor_scalar(out=rden.t[:, :T_], in0=op_.t[:, 128:128 + T_], scalar1=esink.t[:, l * 4 + jq:l * 4 + jq + 1], scalar2=None, op0=ALU.add), rd=[op_, esink], wr=[rden])
            P.op("dve", lambda e: e.reciprocal(out=rden.t[:, :T_], in_=rden.t[:, :T_]), rd=[rden], wr=[rden])
            dve_tt(obt.t[:, :T_], op_.t[:, 0:T_], rden.t[:, :T_], ALU.mult, [op_, rden], [obt])
            dve_tt(yT.t[:, 2 + jq, cs], obt.t[:, :T_], res["zb"].t[:, jq, cs], ALU.mult, [obt, res["zb"]], [yT])

        def pooling(l, Ec, W, T_, cs, pinv):
            pool_tt(s2.t[:, :, 0:W - 1], Ec.t[:, :, 1:W], Ec.t[:, :, 0:W - 1], ALU.add, [Ec], [s2])
            pool_tt(s4.t[:, :, 0:W - 3], s2.t[:, :, 2:W - 1], s2.t[:, :, 0:W - 3], ALU.add, [s2], [s4])
            pool_tt(s8.t[:, 0:W - 7], s4.t[:, 1, 4:W - 3], s4.t[:, 1, 0:W - 7], ALU.add, [s4], [s8])
            pool_tt(s16.t[64:128, 0:W - 15], s8.t[64:128, 8:W - 7], s8.t[64:128, 0:W - 15], ALU.add, [s8], [s16])
            wsrc = [(s2.t[0:64, 0, 15:15 + T_], 0, 0), (s4.t[64:128, 0, 13:13 + T_], 0, 64), (s8.t[0:64, 9:9 + T_], 1, 0), (s16.t[64:128, 1:1 + T_], 1, 64)]
            for src, c, r0 in wsrc:
                pool_tt(plf.t[r0:r0 + 64, c, :T_], src, pinv.t[r0:r0 + 64, c, :T_], ALU.mult, [s2, s4, s8, s16, pinv], [plf])
            pool_tt(plb.t[:, :, :T_], plf.t[:, :, :T_], Ec.t[:, :, 16:16 + T_], ALU.subtract, [plf, Ec], [plb])
            pp = getpj()

            def mmc(pe):
                for c in range(2):
                    ins = pe.matmul(pp.t[:, c * 128:c * 128 + T_], lhsT=wpbd.t[:, l * 2 + c, :], rhs=plb.t[:, c, :T_], start=True, stop=True)
                return ins
            P.op("pe", mmc, rd=[wpbd, plb], wr=[pp])
            for c in range(2):
                P.op("dve", lambda e, c=c: e.scalar_tensor_tensor(out=yT.t[:, 6 + c, cs], in0=pp.t[:, c * 128:c * 128 + T_], scalar=pscT.t[:, l * 2 + c:l * 2 + c + 1], in1=res["zc"].t[:, c, cs], op0=ALU.mult, op1=ALU.mult), rd=[pp, pscT, res["zc"]], wr=[yT])

        def merge_out(l, NTK, tiles):
            for m in range(8):
                koff = [0, 2, 6, 8, 10]
                wbk = wbrk[wbst["n"] % 2]
                wbst["n"] += 1
                P.dma("sp", wbk.t[:, :, :], wbr_b[l, m], rd=[wbres[l]], wr=[wbk])
                for i in range(4):
                    if i % CPB == 0:
                        wb = load_block(l, (36 + m * 4 + i) // CPB)
                    pg = getpj()

                    def gg(pe, i=i, pg=pg, wb=wb):
                        for kc in range(8):
                            ins = pe.matmul(pg.t[:, :NTK], lhsT=wb.t[:, kc, (i % CPB) * 128:(i % CPB + 1) * 128], rhs=hT.t[:, kc, :NTK], start=(kc == 0), stop=(kc == 7))
                        return ins
                    P.op("pe", gg, rd=[wb, hT], wr=[pg])
                    gs = gsb[i % 2]
                    act_copy(gs.t[:, :NTK], pg.t[:, :NTK], [pg], [gs], func=AF.Sigmoid)
                    pb_ = getpj()

                    def bb(pe, i=i, pb_=pb_, wbk=wbk):
                        ks = list(range(koff[i], koff[i + 1]))
                        for n, kc in enumerate(ks):
                            ins = pe.matmul(pb_.t[:, :NTK], lhsT=wbk.t[:, kc, :], rhs=yT.t[:, kc, :NTK], start=(n == 0), stop=(n == len(ks) - 1))
                        return ins
                    P.op("pe", bb, rd=[wbk, yT], wr=[pb_])
                    if i == 0:
                        dve_tt(macc.t[:, :NTK], pb_.t[:, :NTK], gs.t[:, :NTK], ALU.mult, [pb_, gs], [macc])
                    else:
                        dve_tt(mtmp.t[:, :NTK], pb_.t[:, :NTK], gs.t[:, :NTK], ALU.mult, [pb_, gs], [mtmp])
                        if i < 3:
                            P.op("pool", lambda e: e.tensor_tensor(out=macc.t[:, :NTK], in0=macc.t[:, :NTK], in1=mtmp.t[:, :NTK], op=ALU.add), rd=[macc, mtmp], wr=[macc])
                        else:
                            P.op("pool", lambda e, m=m: e.tensor_tensor(out=mT.t[:, m, :NTK], in0=macc.t[:, :NTK], in1=mtmp.t[:, :NTK], op=ALU.add), rd=[macc, mtmp], wr=[mT])
            for ti, (xtile, T_, col0, dst_ap, dres) in enumerate(tiles):
                pa = getpj(); pb = getpj()

                def og(pe, pa=pa, pb=pb, col0=col0, T_=T_):
                    for n_, pp in enumerate((pa, pb)):
                        for kc in range(8):
                            ins = pe.matmul(pp.t[:T_, :], lhsT=mT.t[:, kc, col0:col0 + T_], rhs=wout.t[:, kc, n_ * 512:(n_ + 1) * 512], start=(kc == 0), stop=(kc == 7))
                    return ins
                P.op("pe", og, rd=[mT, wout], wr=[pa, pb])
                for n_, pp in enumerate((pa, pb)):
                    P.op("act", lambda e, n_=n_, pp=pp, T_=T_: e.activation(out=xsn.t[:T_, n_ * 512:(n_ + 1) * 512], in_=pp.t[:T_, :], func=AF.Square, accum_out=ss.t[:T_, 2 + n_:3 + n_]), rd=[pp], wr=[xsn, ss])
                P.op("dve", lambda e, T_=T_: e.tensor_scalar(out=rstd.t[:T_, :], in0=ss.t[:T_, 2:3], scalar1=ss.t[:T_, 3:4], scalar2=1.0 / D, op0=ALU.add, op1=ALU.mult), rd=[ss], wr=[rstd])
                rsqrt_ln_exp(rstd.t[:T_, :], rstd.t[:T_, :], rstd, rstd)
                xo_ = xo[0]
                for n_, pp in enumerate((pa, pb)):
                    sl = slice(n_ * 512, (n_ + 1) * 512)
                    P.op("dve", lambda e, pp=pp, sl=sl, T_=T_, xo_=xo_: e.scalar_tensor_tensor(out=xo_.t[:T_, sl], in0=pp.t[:T_, :], scalar=rstd.t[:T_, 0:1], in1=gpost.t[:T_, sl], op0=ALU.mult, op1=ALU.mult), rd=[pp, rstd, gpost], wr=[xo_])
                P.op("pool", lambda e, T_=T_, xo_=xo_, xtile=xtile: e.tensor_tensor(out=xo_.t[:T_, :], in0=xo_.t[:T_, :], in1=xtile.t[:T_, :], op=ALU.add), rd=[xo_, xtile], wr=[xo_])
                if dst_ap is None:
                    P.op("pool", lambda e, T_=T_, xo_=xo_, xtile=xtile: e.tensor_copy(out=xtile.t[:T_, :], in_=xo_.t[:T_, :]), rd=[xo_], wr=[xtile])
                if dst_ap is not None:
                    P.dma("pool", dst_ap, xo_.t[:T_, :], rd=[xo_], wr=[dres], out_final=True, ch="d_xo_pool")

        S0f = P.sb("S0f", [128, NS, 128], F32); S0b = P.sb("S0b", [128, NS, 128], BF16); S1f = S0f
        vexs = vexp
        xsm = P.sb("xsm", [128, D], F32)
        kcS = P.sb("kcS", [128, NS, 128], BF16); vcS = kcS
        kcT = P.sb("kcT", [128, 2, 128], BF16); kcd = P.sb("kcd", [128, 256], BF16)
        vcp = [P.sb("vcp%d" % i, [128, 2, 2, 128], BF16) for i in range(2)]
        esS = P.sb("esS", [128, NS, 8, 4], BF16)
        esC = P.sb("esC", [128, 8, 64], BF16)
        for t_ in (S0f, vcp[0], vcp[1]):
            P.op("pool", lambda e, t_=t_: e.memset(t_.t[tuple([slice(None)] * len(t_.t.shape))], 0.0), wr=[t_])

        def load_sample_state(l, which, j):
            src_d, dk = (sret_d, 32) if which == "A" else (shg_d, 64)
            for hh in range(2):
                h = 2 * j + hh
                P.dma("sp", S0f.t[hh * 64:hh * 64 + dk, :, hh * 64:(hh + 1) * 64], src_d[l, :, h].rearrange("s k v -> k s v"), wr=[S0f])
            P.op("pool", lambda e: e.tensor_copy(out=S0b.t[:, :, :], in_=S0f.t[:, :, :]), rd=[S0f], wr=[S0b])

        def sample_state_update(l, which, kdt, voff, T_, j):
            cm = tbs["cmS"]
            dst_d, dk = (osret_d, 32) if which == "A" else (oshg_d, 64)
            if True:
                P.op("dve", lambda e, j=j: e.tensor_tensor(out=vexs.t[:T_, :, :], in0=bcast(vtok.t[:T_, voff + j * 128:voff + (j + 1) * 128], 1, NS), in1=bc(cm.t[:T_, :], [T_, NS, 128]), op=ALU.mult), rd=[vtok, cm], wr=[vexs])
                for q in range(4):
                    kb_ = kvp[q % 2]
                    P.op("pe", lambda pe, j=j, q=q, kb_=kb_: pe.matmul(kb_.t[:, :], lhsT=kdt.t[:T_, j, :], rhs=vexs.t[:T_, q * 4:(q + 1) * 4, :], start=True, stop=True), rd=[kdt, vexs], wr=[kb_])
                    for hh in range(2):
                        r_ = slice(hh * 64, (hh + 1) * 64)
                        kv3 = kb_.t[r_, :].rearrange("p (s c) -> p s c", s=4)[:, :, r_]
                        o3 = S1f.t[r_, q * 4:(q + 1) * 4, r_]
                        i3 = S0f.t[r_, q * 4:(q + 1) * 4, r_]
                        if which == "A":
                            P.op("dve", lambda e, o3=o3, i3=i3, kv3=kv3, r_=r_, j=j: e.scalar_tensor_tensor(out=o3, in0=i3, scalar=tbs["decA"].t[r_, 2 + j:3 + j], in1=kv3, op0=ALU.mult, op1=ALU.add), rd=[S0f, kb_, tbs["decA"]], wr=[S1f])
                        else:
                            dve_tt(o3, i3, bc(decD.t[r_, j, q * 4:(q + 1) * 4], [64, 4, 64]), ALU.mult, [S0f, decD], [S1f])
                            dve_tt(o3, o3, kv3, ALU.add, [S1f, kb_], [S1f])
            for hh in range(2):
                h = 2 * j + hh
                P.dma("sp", dst_d[l, :, h].rearrange("s k v -> k s v"), S1f.t[hh * 64:hh * 64 + dk, :, hh * 64:(hh + 1) * 64], rd=[S1f], out_final=True)

        def sample_swa(l, kcur, vb, cs, T_):
            P.dma("pool", kcS.t[:, :, :], sk_d[l].rearrange("s k c -> k s c"), wr=[kcS])
            kres = P.res("osk%d" % l); vres = P.res("osv%d" % l)
            P.dma("sp", osk_d[l, :, 0:124, :], sk_d[l, :, 4:128, :], rd=[], wr=[kres], ch="d_cpk", out_final=True)
            P.dma("sp", osv_d[l, :, 0:124, :], sv_d[l, :, 4:128, :], rd=[], wr=[vres], ch="d_cpv", out_final=True)
            pc = [getpj(), getpj()]
            for s_ in range(NS):
                P.op("act", lambda e, s_=s_: e.activation(out=kcd.t[:, :].rearrange("p (k b d) -> p k b d", k=2, b=2), in_=bcast(kcS.t[:, s_, :].rearrange("p (k d) -> p k d", k=2), 2, 2), func=AF.Copy), rd=[kcS], wr=[kcd])

                def trc(pe):
                    for kv in range(2):
                        ins = pe.transpose(tpp.t[:, kv, :], kcd.t[:, kv * 128:(kv + 1) * 128], ident.t[:, :])
                    return ins
                P.op("pe", trc, rd=[kcd, ident], wr=[tpp])
                P.op("dve", lambda e: e.tensor_copy(out=kcT.t[:, :, :], in_=tpp.t[:, 0:2, :]), rd=[tpp], wr=[kcT])
                def scc(pe, s_=s_):
                    for hh in range(2):
                        r_ = slice(hh * 64, (hh + 1) * 64)
                        for jq in range(4):
                            kv = jq // 2
                            c0 = s_ * 16 + jq * 4
                            ins = pe.matmul(pc[hh].t[:, c0:c0 + 4], lhsT=kcT.t[r_, kv, :], rhs=qrB.t[r_, jq, s_:T_:NS], start=True, stop=True)
                    return ins
                P.op("pe", scc, rd=[kcT, qrB], wr=[pc[0], pc[1]])
            P.dma("pool", vcS.t[:, :, :], sv_d[l].rearrange("s k c -> k s c"), wr=[vcS])
            for hh in range(2):
                P.op("act", lambda e, hh=hh: e.activation(out=esS.t[:, :, hh::2, :], in_=pc[hh].t[:, 0:256].rearrange("p (s j t) -> p s j t", s=NS, j=4), func=AF.Exp, scale=0.125), rd=[pc[hh]], wr=[esS])
            mc4 = tbs["mcacheS"].t[:, 0:T_:NS]
            P.op("dve", lambda e: e.tensor_tensor(out=esS.t[:, :, :, :].rearrange("p s h t -> p (s h) t"), in0=esS.t[:, :, :, :].rearrange("p s h t -> p (s h) t"), in1=bcast(mc4, 1, NS * 8), op=ALU.mult), rd=[esS, tbs["mcacheS"]], wr=[esS])
            pcur = [getpj(), getpj()]

            def scur(pe):
                for hh in range(2):
                    r_ = slice(hh * 64, (hh + 1) * 64)
                    for jq in range(4):
                        kv = jq // 2
                        ins = pe.matmul(pcur[hh].t[:T_, jq * 64:(jq + 1) * 64], lhsT=kcur.t[r_, kv, :T_], rhs=qrB.t[r_, jq, :T_], start=True, stop=True)
                return ins
            P.op("pe", scur, rd=[kcur, qrB], wr=[pcur[0], pcur[1]])
            for hh in range(2):
                P.op("act", lambda e, hh=hh: e.activation(out=esC.t[:T_, hh::2, :], in_=pcur[hh].t[:T_, 0:256].rearrange("p (j t) -> p j t", j=4), func=AF.Exp, scale=0.125), rd=[pcur[hh]], wr=[esC])
            dve_tt(esC.t[:T_, :, :], esC.t[:T_, :, :], bcast(tbs["mcurS"].t[:T_, :T_], 1, 8), ALU.mult, [esC, tbs["mcurS"]], [esC])
            po = [getpj(), getpj()]
            first = {0: True, 1: True}
            for s_ in range(NS):
                vc = vcp[s_ % 2]
                for hh in range(2):
                    P.op("pool", lambda e, s_=s_, hh=hh, vc=vc: e.tensor_copy(out=vc.t[:, :, hh, hh * 64:(hh + 1) * 64], in_=vcS.t[:, s_, :].rearrange("p (k d) -> p k d", k=2)), rd=[vcS], wr=[vc])

                def pvc(pe, s_=s_, vc=vc):
                    for h in range(8):
                        jq, hh, kv = h // 2, h % 2, h // 4
                        i = jq % 2
                        rhs = esS.t[:, s_, h, :]
                        for w_, lt in enumerate((vc.t[:, kv, hh, :], tbs["onespad"].t[:, hh, :])):
                            c0 = (i * 2 + w_) * 64
                            ins = pe.matmul(po[kv].t[:, c0 + s_:c0 + T_:NS], lhsT=lt, rhs=rhs, start=(s_ == 0 and h % 4 == 0 and w_ == 0), stop=False, skip_group_check=True)
                    return ins
                P.op("pe", pvc, rd=[vc, esS, tbs["onespad"]], wr=[po[0], po[1]])

            def pvn(pe):
                for kv in range(2):
                    for i in range(2):
                        jq = kv * 2 + i
                        for hh in range(2):
                            h = 2 * jq + hh
                            for w_, lt in enumerate((vb.t[:T_, kv, hh, :], tbs["onespad"].t[:T_, hh, :])):
                                c0 = (i * 2 + w_) * 64
                                ins = pe.matmul(po[kv].t[:, c0:c0 + T_], lhsT=lt, rhs=esC.t[:T_, h, :T_], start=False, stop=(hh == 1 and i == 1 and w_ == 1), skip_group_check=True)
                return ins
            P.op("pe", pvn, rd=[vb, esC, tbs["onespad"]], wr=[po[0], po[1]])
            for jq in range(4):
                kv, i = jq // 2, jq % 2
                v4 = po[kv].t[:, 0:256].rearrange("p (i w c) -> p i w c", i=2, w=2)
                P.op("dve", lambda e, jq=jq, v4=v4, i=i: e.tensor_scalar(out=rden.t[:, :T_], in0=v4[:, i, 1, 0:T_], scalar1=esink.t[:, l * 4 + jq:l * 4 + jq + 1], scalar2=None, op0=ALU.add), rd=[po[kv], esink], wr=[rden])
                P.op("dve", lambda e: e.reciprocal(out=rden.t[:, :T_], in_=rden.t[:, :T_]), rd=[rden], wr=[rden])
                dve_tt(obt.t[:, :T_], v4[:, i, 0, 0:T_], rden.t[:, :T_], ALU.mult, [po[kv], rden], [obt])
                dve_tt(yT.t[:, 2 + jq, cs], obt.t[:, :T_], res["zb"].t[:, jq, cs], ALU.mult, [obt, res["zb"]], [yT])
            def trkb(pe):
                for kv in range(2):
                    ins = pe.transpose(tpp.t[:T_, kv, :], kcur.t[:, kv, :T_], ident.t[:, :])
                return ins
            P.op("pe", trkb, rd=[kcur, ident], wr=[tpp])
            P.op("dve", lambda e: e.tensor_copy(out=stg.t[:T_, 0:128].rearrange("p (k d) -> p k d", k=2), in_=tpp.t[:T_, 0:2, 0:64]), rd=[tpp], wr=[stg])
            P.op("dve", lambda e: e.tensor_copy(out=stg.t[:T_, 128:256], in_=vtok.t[:T_, 512:640]), rd=[vtok], wr=[stg])
            for t_ in range(4):
                P.dma("sp", osk_d[l, :, 124 + t_, :], stg.t[t_ * NS:(t_ + 1) * NS, 0:128], rd=[stg], wr=[kres], out_final=True)
                P.dma("sp", osv_d[l, :, 124 + t_, :], stg.t[t_ * NS:(t_ + 1) * NS, 128:256], rd=[stg], wr=[vres], out_final=True)

        ES = P.sb("ES", [128, 2, NS, 20], F32)
        q2 = P.sb("q2", [128, 2, NS, 19], F32); q4 = P.sb("q4", [128, 2, NS, 17], F32)
        q8 = P.sb("q8", [128, NS, 13], F32); q16 = P.sb("q16", [128, NS, 5], F32)

        def sample_pool(l, Ec, cs, T_):
            pres = P.res("ospool%d" % l)
            P.op("pool", lambda e: e.memset(ES.t[:, :, :, 0:1], 0.0), wr=[ES])
            for half in range(2):
                P.dma("sp", stg.t[0:120, :], spool_d[l, half * 8:(half + 1) * 8].rearrange("s k c -> (s k) c"), wr=[stg])
                for c in range(2):
                    P.op("pe", lambda pe, c=c: pe.transpose(otp.t[:, c * 128:c * 128 + 120], stg.t[0:120, c * 128:(c + 1) * 128], identf.t[0:120, 0:120]), rd=[stg, identf], wr=[otp])
                P.op("dve", lambda e, half=half: e.tensor_copy(out=ES.t[:, :, half * 8:(half + 1) * 8, 1:16], in_=otp.t[:, 0:256].rearrange("p (c x) -> p c x", c=2)[:, :, 0:120].rearrange("p c (s k) -> p c s k", k=15)), rd=[otp], wr=[ES])
            P.op("dve", lambda e: e.tensor_copy(out=ES.t[:, :, :, 16:20], in_=Ec.t[:, :, 16:16 + T_].rearrange("p c (t s) -> p c s t", s=NS)), rd=[Ec], wr=[ES])
            dve_tt(q2.t[:, :, :, :], ES.t[:, :, :, 1:20], ES.t[:, :, :, 0:19], ALU.add, [ES], [q2])
            dve_tt(q4.t[:, :, :, :], q2.t[:, :, :, 2:19], q2.t[:, :, :, 0:17], ALU.add, [q2], [q4])
            dve_tt(q8.t[:, :, :], q4.t[:, 1, :, 4:17], q4.t[:, 1, :, 0:13], ALU.add, [q4], [q8])
            dve_tt(q16.t[64:128, :, :], q8.t[64:128, :, 8:13], q8.t[64:128, :, 0:5], ALU.add, [q8], [q16])
            wsrc = [(q2.t[0:64, 0, :, 15:19], 0, 0), (q4.t[64:128, 0, :, 13:17], 0, 64), (q8.t[0:64, :, 9:13], 1, 0), (q16.t[64:128, :, 1:5], 1, 64)]
            for src, c, r0 in wsrc:
                dve_tt(plf.t[r0:r0 + 64, c, :T_].rearrange("p (t s) -> p s t", s=NS), src, tbs["pinvR"].t[r0:r0 + 64, c, 0:T_].rearrange("p (t s) -> p s t", s=NS), ALU.mult, [q2, q4, q8, q16, tbs["pinvR"]], [plf])
            dve_tt(plb.t[:, :, :T_], plf.t[:, :, :T_], Ec.t[:, :, 16:16 + T_], ALU.subtract, [plf, Ec], [plb])
            pp = getpj()

            def mmc(pe):
                for c in range(2):
                    ins = pe.matmul(pp.t[:, c * 128:c * 128 + T_], lhsT=wpbd.t[:, l * 2 + c, :], rhs=plb.t[:, c, :T_], start=True, stop=True)
                return ins
            P.op("pe", mmc, rd=[wpbd, plb], wr=[pp])
            for c in range(2):
                P.op("dve", lambda e, c=c: e.scalar_tensor_tensor(out=yT.t[:, 6 + c, cs], in0=pp.t[:, c * 128:c * 128 + T_], scalar=pscT.t[:, l * 2 + c:l * 2 + c + 1], in1=res["zc"].t[:, c, cs], op0=ALU.mult, op1=ALU.mult), rd=[pp, pscT, res["zc"]], wr=[yT])
            P.dma("sp", ospool_d[l, :, 0:11, :], spool_d[l, :, 4:15, :], rd=[], wr=[pres], ch="d_cpp", out_final=True)
            for c in range(2):
                P.op("pe", lambda pe, c=c: pe.transpose(otp.t[:T_, c * 128:(c + 1) * 128], Ec.t[:, c, 16:16 + T_], identf.t[:, :]), rd=[Ec, identf], wr=[otp])
            P.op("dve", lambda e: e.tensor_copy(out=stg.t[:T_, :], in_=otp.t[:T_, 0:256]), rd=[otp], wr=[stg])
            for t_ in range(4):
                P.dma("sp", ospool_d[l, :, 11 + t_, :], stg.t[t_ * NS:(t_ + 1) * NS, :], rd=[stg], wr=[pres], out_final=True)

        lfs = P.sb("lfs", [128, 2], F32); lft = P.sb("lft", [128, 2], F32); expA = P.sb("expA", [128, 2], F32)
        Aall = P.sb("Aall", [128, 4, 2], F32)
        ones1 = P.sb("ones1", [128, 128], F32)
        P.op("pool", lambda e: e.memset(ones1.t[:, :], 1.0), wr=[ones1])
        xin_r = P.res("xin"); xg_r = P.res("xg")
        P.op("pool", lambda e: e.memset(hsg.t[:, 0, :], 0.0), wr=[hsg])
        for r0_ in (512, 1408, 1536):
            P.dma("sp", xin_d[r0_:r0_ + 128, :], hsg.t[:, 0, :], rd=[hsg], wr=[xin_r], ch="d_xin_sp")

        def phase1_tile(l, cs0, ropesrc, tail):
            T_ = 128
            cs = slice(cs0, cs0 + T_)
            par = st["par"]
            st["par"] ^= 1
            pa = getpj(); pb = getpj()

            def tmg(pe):
                for kc in range(8):
                    pe.matmul(pa.t[:T_, :512], lhsT=hT.t[:, kc, cs], rhs=wtm.t[:, kc, 0:512], start=(kc == 0), stop=(kc == 7))
                for kc in range(8):
                    ins = pe.matmul(pb.t[:T_, :128], lhsT=hT.t[:, kc, cs], rhs=wtm.t[:, kc, 512:640], start=(kc == 0), stop=(kc == 7))
                return ins
            P.op("pe", tmg, rd=[hT, wtm], wr=[pa, pb])
            act_copy(vtok.t[:T_, 0:512], pa.t[:T_, :512], [pa], [vtok])
            act_copy(vtok.t[:T_, 512:640], pb.t[:T_, :128], [pb], [vtok])
            vb = vpB[par]
            if tail:
                for hh in range(2):
                    srcb = pb.t[:T_, :128].rearrange("p (k d) -> p k d", k=2)
                    P.op("dve", lambda e, hh=hh, srcb=srcb: e.tensor_copy(out=vb.t[:T_, :, hh, hh * 64:(hh + 1) * 64], in_=srcb), rd=[pb], wr=[vb])
            dve_tt(krA.t[:, :, :T_], res["ka"].t[:, :, cs], res["kas"].t[:, :, cs], ALU.add, [res["ka"], res["kas"]], [krA])
            dve_tt(kdA.t[:, :, :T_], krA.t[:, :, :T_], tbs["gkP"].t[:, :, :T_], ALU.mult, [krA, tbs["gkP"]], [kdA])

            def trk(pe, src=kdA):
                for j in range(2):
                    ins = pe.transpose(tpp.t[:T_, j, :], src.t[:, j, :T_], ident.t[:, :])
                return ins
            P.op("pe", trk, rd=[kdA, ident], wr=[tpp])
            act_copy(kdTok.t[:T_, :, :], tpp.t[:T_, 0:2, :], [tpp], [kdTok])

            def kvA(pe):
                for j in range(2):
                    ins = pe.matmul(kvp[0].t[:, j * 128:(j + 1) * 128], lhsT=kdTok.t[:T_, j, :], rhs=vtok.t[:T_, j * 128:(j + 1) * 128], start=True, stop=True)
                return ins
            P.op("pe", kvA, rd=[kdTok, vtok], wr=[kvp[0]])
            for j in range(2):
                for hh in range(2):
                    r_ = slice(hh * 64, (hh + 1) * 64)
                    P.op("dve", lambda e, j=j, r_=r_: e.scalar_tensor_tensor(out=SA.t[r_, j, r_], in0=SA.t[r_, j, r_], scalar=tbs["decA"].t[r_, j:j + 1], in1=kvp[0].t[r_, j * 128 + r_.start:j * 128 + r_.stop], op0=ALU.mult, op1=ALU.add), rd=[SA, tbs["decA"], kvp[0]], wr=[SA])
            fd = res["fd"]
            act_copy(hsg.t[:, :, :T_], fd.t[:, :, cs], [fd], [hsg], func=AF.Sigmoid)
            for j in range(2):
                P.op("dve", lambda e, j=j: e.tensor_scalar(out=hf.t[:, j, :T_], in0=hsg.t[:, j, :T_], scalar1=omlb.t[:, j, l:l + 1], scalar2=lbc.t[:, j, l:l + 1], op0=ALU.mult, op1=ALU.add), rd=[hsg, omlb, lbc], wr=[hf])
            act_copy(hsg.t[:, :, :T_], hf.t[:, :, :T_], [hf], [hsg], func=AF.Ln)
            P.op("dve", lambda e: e.tensor_reduce(out=lft.t[:, :], in_=hsg.t[:, :, :], axis=mybir.AxisListType.X, op=ALU.add), rd=[hsg], wr=[lft])
            dve_tt(lfs.t[:, :], lfs.t[:, :], lft.t[:, :], ALU.add, [lfs, lft], [lfs])
            for j in range(2):
                P.op("dve", lambda e, j=j: e.tensor_tensor_scan(out=hb.t[:, j, :], data0=ones1.t[:, :], data1=hsg.t[:, j, :], initial=0.0, op0=ALU.mult, op1=ALU.add), rd=[hsg, ones1], wr=[hb])
            dve_tt(heb.t[:, :, :], bc(lft.t[:, :], [128, 2, 128]), hb.t[:, :, :], ALU.subtract, [lft, hb], [heb])
            act_copy(henb.t[:, :, :], heb.t[:, :, :], [heb], [henb], func=AF.Exp)
            act_copy(expA.t[:, :], lft.t[:, :], [lft], [expA], func=AF.Exp)
            P.op("dve", lambda e: e.tensor_scalar(out=hf.t[:, :, :T_], in0=hf.t[:, :, :T_], scalar1=-1.0, scalar2=1.0, op0=ALU.mult, op1=ALU.add), rd=[hf], wr=[hf])
            dve_tt(kdd.t[:, :, :], hf.t[:, :, :], henb.t[:, :, :], ALU.mult, [hf, henb], [kdd])
            P.op("pe", lambda pe: trk(pe, kdd), rd=[kdd, ident], wr=[tpp])
            act_copy(kdTok.t[:T_, :, :], tpp.t[:T_, 0:2, :], [tpp], [kdTok])

            def kvD(pe):
                for j in range(2):
                    ins = pe.matmul(kvp[1].t[:, j * 128:(j + 1) * 128], lhsT=kdTok.t[:T_, j, :], rhs=vtok.t[:T_, 256 + j * 128:256 + (j + 1) * 128], start=True, stop=True)
                return ins
            P.op("pe", kvD, rd=[kdTok, vtok], wr=[kvp[1]])
            for j in range(2):
                for hh in range(2):
                    r_ = slice(hh * 64, (hh + 1) * 64)
                    P.op("dve", lambda e, j=j, r_=r_: e.scalar_tensor_tensor(out=SD.t[r_, j, 0, r_], in0=SD.t[r_, j, 0, r_], scalar=expA.t[r_, j:j + 1], in1=kvp[1].t[r_, j * 128 + r_.start:j * 128 + r_.stop], op0=ALU.mult, op1=ALU.add), rd=SDall + [expA, kvp[1]], wr=SDall)
            if not tail:
                return
            kcur = krB[par]
            dve_tt(kcur.t[:, :, :T_], res["kb"].t[:, :, cs], res["kbs"].t[:, :, cs], ALU.add, [res["kb"], res["kbs"]], [kcur])
            Ec = Eb[par]
            act_copy(Ec.t[:, :, 16:16 + T_], res["uc"].t[:, :, cs], [res["uc"]], [Ec])

        def exchange(l):
            pl = st["par"] ^ 1
            act_copy(expA.t[:, :], lfs.t[:, :], [lfs], [expA], func=AF.Exp)
            for j in range(2):
                P.dma("sp", xin_d[j * 128:(j + 1) * 128, :], SA.t[:, j, :], rd=[SA], wr=[xin_r], ch="d_xin_sp")
                P.dma("sp", xin_d[256 + j * 128:256 + (j + 1) * 128, :], SD.t[:, j, 0, :], rd=SDall, wr=[xin_r], ch="d_xin_sp")
            P.dma("sp", xin_d[512:640, 0:2], expA.t[:, :], rd=[expA], wr=[xin_r], ch="d_xin_sp")
            P.dma("pool", xin_d[640:896, :].rearrange("(k p) c -> p k c", p=128), krB[pl].t[:, :, :], rd=[krB[pl]], wr=[xin_r], ch="d_xin_pool")
            P.dma("pool", xin_d[896:1408, :].rearrange("(a p) c -> p a c", p=128), vpB[pl].t[:, :, :, :].rearrange("p k h c -> p (k h) c"), rd=[vpB[pl]], wr=[xin_r], ch="d_xin_pool")
            for c in range(2):
                P.dma("sp", xin_d[1408 + c * 128:1408 + (c + 1) * 128, 0:16], Eb[pl].t[:, c, 128:144], rd=[Eb[pl]], wr=[xin_r], ch="d_xin_sp")
            P.coll(lambda e: e.collective_compute("AllGather", ALU.bypass, replica_groups=[[0, 1, 2, 3], [4, 5, 6, 7]], ins=[xin_d], outs=[xg_d]), rd=[xin_r], wr=[xg_r])
            selp, sels, cret = tbs["selp"], tbs["sels"], tbs["cret"]

            def rows(q, r0, n):
                return xg_d[q * XR + r0:q * XR + r0 + n, :]

            def accum(q, acc_ap, stage_ap, coef_ap, rdl, acc_t):
                if q == 0:
                    P.op("dve", lambda e: e.tensor_scalar(out=acc_ap, in0=stage_ap, scalar1=coef_ap, scalar2=None, op0=ALU.mult), rd=rdl, wr=[acc_t])
                else:
                    P.op("dve", lambda e: e.scalar_tensor_tensor(out=acc_ap, in0=stage_ap, scalar=coef_ap, in1=acc_ap, op0=ALU.mult, op1=ALU.add), rd=rdl + [acc_t], wr=[acc_t])
            for q in range(4):
                P.dma("sp", hsg.t[:, :, :], rows(q, 640, 256).rearrange("(k p) c -> p k c", p=128), rd=[xg_r], wr=[hsg])
                accum(q, hf.t[:, :, :], hsg.t[:, :, :], selp.t[:, q:q + 1], [hsg, selp], hf)
            P.op("dve", lambda e: e.tensor_copy(out=krB[pl].t[:, :, :], in_=hf.t[:, :, :]), rd=[hf], wr=[krB[pl]])
            for kv in range(2):
                for q in range(4):
                    P.dma("sp", hsg.t[:, :, :], rows(q, 896 + kv * 256, 256).rearrange("(k p) c -> p k c", p=128), rd=[xg_r], wr=[hsg])
                    accum(q, hf.t[:, :, :], hsg.t[:, :, :], selp.t[:, q:q + 1], [hsg, selp], hf)
                P.op("dve", lambda e, kv=kv: e.tensor_copy(out=vpB[pl].t[:, kv, :, :], in_=hf.t[:, :, :]), rd=[hf], wr=[vpB[pl]])
            for q in range(4):
                P.dma("sp", hsg.t[:, :, 0:16], rows(q, 1408, 256)[:, 0:16].rearrange("(c p) k -> p c k", p=128), rd=[xg_r], wr=[hsg])
                accum(q, hf.t[:, :, 0:16], hsg.t[:, :, 0:16], selp.t[:, q:q + 1], [hsg, selp], hf)
            P.op("dve", lambda e: e.tensor_copy(out=Eb[pl].t[:, :, 128:144], in_=hf.t[:, :, 0:16]), rd=[hf], wr=[Eb[pl]])
            for q in range(4):
                P.dma("sp", hsg.t[:, :, :], rows(q, 0, 256).rearrange("(k p) c -> p k c", p=128), rd=[xg_r], wr=[hsg])
                for j in range(2):
                    accum(q, hf.t[:, j, :], hsg.t[:, j, :], cret.t[:, j, q:q + 1], [hsg, cret], hf)
            P.op("dve", lambda e: e.tensor_copy(out=SA.t[:, :, :], in_=hf.t[:, :, :]), rd=[hf], wr=[SA])
            P.op("dve", lambda e: e.tensor_copy(out=SAb.t[:, :, :], in_=hf.t[:, :, :]), rd=[hf], wr=[SAb])
            for q in range(4):
                P.dma("sp", Aall.t[:, q, :], rows(q, 512, 128)[:, 0:2], rd=[xg_r], wr=[Aall])
            P.dma("sp", hb.t[:, :, :], rows(0, 256, 256).rearrange("(k p) c -> p k c", p=128), rd=[xg_r], wr=[hb])
            accum(0, heb.t[:, :, :], hb.t[:, :, :], sels.t[:, 1:2], [hb, sels], heb)
            for r in (1, 2):
                P.dma("sp", hsg.t[:, :, :], rows(r, 256, 256).rearrange("(k p) c -> p k c", p=128), rd=[xg_r], wr=[hsg])
                for j in range(2):
                    P.op("dve", lambda e, j=j, r=r: e.scalar_tensor_tensor(out=hb.t[:, j, :], in0=hb.t[:, j, :], scalar=Aall.t[:, r, j:j + 1], in1=hsg.t[:, j, :], op0=ALU.mult, op1=ALU.add), rd=[hb, Aall, hsg], wr=[hb])
                accum(1, heb.t[:, :, :], hb.t[:, :, :], sels.t[:, r + 1:r + 2], [hb, sels], heb)
            P.op("dve", lambda e: e.tensor_copy(out=SD.t[:, :, 0, :], in_=heb.t[:, :, :]), rd=[heb], wr=SDall)

        yres = [P.res("y%d" % g_) for g_ in range(NG)]
        if with_sample:
            for t_ in range(4):
                P.dma("sp", xsm.t[t_ * NS:(t_ + 1) * NS, :], xs_d[t_ * NS:(t_ + 1) * NS, :], wr=[xsm])
        convert_layer(0)
        convert_wbr(0)
        for l in range(depth):
            load_layer_weights(l)
            if l + 1 < depth:
                convert_layer(l + 1)
                convert_wbr(l + 1)
            src_d = xp_d if l == 0 else yp_d
            if bmode:
                for t_ in (SA, SD, lfs):
                    P.op("pool", lambda e, t_=t_: e.memset(t_.t[tuple([slice(None)] * len(t_.t.shape))], 0.0), wr=(SDall if t_ is SD else [t_]))
                for g_ in range(NG):
                    for i in range(G):
                        r0 = (g_ * G + i) * 128
                        P.dma("sp", xt[i].t[:, :], src_d[r0:r0 + 128, :], rd=([yres[g_]] if l > 0 else []), wr=[xt[i]])
                        prenorm(l, xt[i], 128, i * 128)
                    for i in range(G):
                        P.dma("sp", rtabG.t[:, :, i * 128:(i + 1) * 128], tb_d["ropeP"][g_ * G + i].rearrange("a p t -> p a t"), wr=[rtabG])
                    fm_blocks(l, range((12 if g_ == NG - 1 else 6) // CPB), NTOK)
                    for i in range(G):
                        ti = g_ * G + i
                        phase1_tile(l, i * 128, tb_d["ropeP"][ti], ti == NT - 1)
                exchange(l)
            for g_ in range(NG):
                for i in range(G):
                    r0 = (g_ * G + i) * 128
                    P.dma("sp", xt[i].t[:, :], src_d[r0:r0 + 128, :], rd=([yres[g_]] if l > 0 else []), wr=[xt[i]])
                    prenorm(l, xt[i], 128, i * 128)
                for i in range(G):
                    P.dma("sp", rtabG.t[:, :, i * 128:(i + 1) * 128], tb_d["ropeP"][g_ * G + i].rearrange("a p t -> p a t"), wr=[rtabG])
                if DEBUG_STOP >= 1:
                    fm_blocks(l, range(36 // CPB), NTOK)
                if DEBUG_STOP >= 2:
                    for i in range(G):
                        ti = g_ * G + i
                        mixers(l, "P", i * 128, 128, ti, tb_d["ropeP"][ti], ti == NT - 1)
                if DEBUG_STOP >= 3:
                    tiles = []
                    for i in range(G):
                        r0 = (g_ * G + i) * 128
                        tiles.append((xt[i], 128, i * 128, yp_d[r0:r0 + 128, :], yres[g_]))
                    if DEBUG_DUMP and l == 0 and g_ == 0:
                        P.dma("sp", dbg_y, yT.t[:, :, :], rd=[yT], out_final=True)
                        P.dma("sp", dbg_h, hT.t[:, :, :], rd=[hT], out_final=True)
                    merge_out(l, NTOK, tiles)
                    if DEBUG_DUMP and l == 0 and g_ == 0:
                        P.dma("sp", dbg_m, mT.t[:, :, :], rd=[mT], out_final=True)
            if with_sample:
                prenorm(l, xsm, TS, 0)
                P.dma("sp", rtabG.t[:, :, :TS], tb_d["ropeS"].rearrange("a p t -> p a t"), wr=[rtabG])
                fm_blocks(l, range(36 // CPB), TS)
                mixers(l, "S", 0, TS, 0, tb_d["ropeS"], False)
                if DEBUG_DUMP and l == 0:
                    P.dma("sp", dbg_ys, yT.t[:, :, :], rd=[yT], out_final=True)
                ysr = P.res("ys")
                if l == depth - 1:
                    merge_out(l, TS, [(xsm, TS, 0, ys_d[:, :], ysr)])
                else:
                    merge_out(l, TS, [(xsm, TS, 0, None, None)])
            if l < depth - 1 and not bmode:
                for t_ in (SA, SAb, SD, SDb, Eb[0], Eb[1], krB[0], krB[1]):
                    P.op("pool", lambda e, t_=t_: e.memset(t_.t[tuple([slice(None)] * len(t_.t.shape))], 0.0), wr=[t_])
                for v_ in vpB:
                    P.op("pool", lambda e, v_=v_: e.memset(v_.t[:, :, :, :], 0.0), wr=[v_])
        P.finish()
    return nc


_FMC, _TMC = _fm_cols()


def kernel(x_prompt, x_sample, state_ret, cache_swa_k, cache_swa_v, state_pool, state_hgrn,
           w_in, w_branch, w_out, g_pre, g_post, attn_sink, w_pool, pool_scale, g_hgrn, lower_bounds):
    f32 = np.float32
    x_prompt = np.asarray(x_prompt, f32); x_sample = np.asarray(x_sample, f32)
    depth = int(np.asarray(w_in).shape[0])
    B, S = x_prompt.shape[0], x_prompt.shape[1]
    RPS = NCORE // B
    NT = S // 128 // RPS
    w_in = np.asarray(w_in, f32)
    wfm = np.zeros((depth, D, NFM * 128), f32)
    valid = _FMC >= 0
    wfm[:, :, valid] = w_in[:, :, _FMC[valid]]
    wtm = np.ascontiguousarray(w_in[:, :, _TMC])
    common = {
        "wfm": wfm, "wtm": wtm, "wbr": np.ascontiguousarray(w_branch, f32), "wout": np.ascontiguousarray(w_out, f32),
        "gpre": np.ascontiguousarray(g_pre, f32), "gpost": np.ascontiguousarray(g_post, f32),
        "sink": np.ascontiguousarray(attn_sink, f32), "wpool": np.ascontiguousarray(w_pool, f32),
        "pscale": np.ascontiguousarray(pool_scale, f32), "ghg": np.ascontiguousarray(g_hgrn, f32),
        "lbnd": np.ascontiguousarray(lower_bounds, f32),
    }
    sk = np.asarray(cache_swa_k, f32).reshape(depth, -1, 128, 128)
    sv = np.asarray(cache_swa_v, f32).reshape(depth, -1, 128, 128)
    in_maps = []
    for c in range(NCORE):
        b, rk = c // RPS, c % RPS
        sl = slice(c * NS, (c + 1) * NS)
        m = dict(common)
        for k, v in host_tables(NT, rk * NT * 128, rk == 0, rank=rk).items():
            m["tb_" + k] = v
        m["xp"] = np.ascontiguousarray(x_prompt[b, rk * NT * 128:(rk + 1) * NT * 128])
        m["xs"] = np.ascontiguousarray(x_sample[sl].transpose(1, 0, 2).reshape(TS, D))
        m["s_ret"] = np.ascontiguousarray(np.asarray(state_ret, f32)[:, sl])
        m["s_k"] = np.ascontiguousarray(sk[:, sl])
        m["s_v"] = np.ascontiguousarray(sv[:, sl])
        m["s_pool"] = np.ascontiguousarray(np.asarray(state_pool, f32)[:, sl])
        m["s_hg"] = np.ascontiguousarray(np.asarray(state_hgrn, f32)[:, sl])
        in_maps.append(m)
    nc = build_program(NT, depth=depth)
    r = run_bass_kernel_spmd(nc, in_maps, core_ids=list(range(NCORE))).results
    y_p = np.stack([np.concatenate([r[b * RPS + k]["yp"] for k in range(RPS)], axis=0) for b in range(B)]).astype(f32)
    lastc = [b * RPS + RPS - 1 for b in range(B)]
    y_s = np.concatenate([r[c]["ys"].reshape(4, NS, D).transpose(1, 0, 2) for c in range(NCORE)], axis=0).astype(f32)
    ret_p = np.stack([r[c]["o_ret"] for c in lastc], axis=1)
    k_p = np.stack([r[c]["o_k"].reshape(depth, 128, 2, 64) for c in lastc], axis=1)
    v_p = np.stack([r[c]["o_v"].reshape(depth, 128, 2, 64) for c in lastc], axis=1)
    pool_p = np.stack([r[c]["o_pool"] for c in lastc], axis=1)
    hg_p = np.stack([r[c]["o_hg"] for c in lastc], axis=1)
    ret_s = np.concatenate([r[c]["os_ret"] for c in range(NCORE)], axis=1)
    k_s = np.concatenate([r[c]["os_k"].reshape(depth, NS, 128, 2, 64) for c in range(NCORE)], axis=1)
    v_s = np.concatenate([r[c]["os_v"].reshape(depth, NS, 128, 2, 64) for c in range(NCORE)], axis=1)
    pool_s = np.concatenate([r[c]["os_pool"] for c in range(NCORE)], axis=1)
    hg_s = np.concatenate([r[c]["os_hg"] for c in range(NCORE)], axis=1)
    outs = (y_p, y_s, ret_p, k_p, v_p, pool_p, hg_p, ret_s, k_s, v_s, pool_s, hg_s)
    return tuple(np.ascontiguousarray(o, dtype=f32) for o in outs)
```
